# Optimizing a Trainium2 kernel written in Bass

```python
import jax, jax.numpy as jnp
from jax import lax
import numpy as np

D_MODEL = 1024
BATCH = 4
SEQ = 4096
DEPTH = 1

HEAD_DIM = 64
N_HEADS_SWA = D_MODEL // (2 * HEAD_DIM)
N_KV_SWA = max(1, N_HEADS_SWA // 4)
N_HEADS_FOX = D_MODEL // (2 * HEAD_DIM)
WINDOW = 128
BLOCK = 128
D_FF = ((8 * D_MODEL // 3 + 127) // 128) * 128
CONV_WIDTH = 3
EPS = 1e-6
NEG = -1e30

Q_A = N_HEADS_SWA * HEAD_DIM
KV_A = N_KV_SWA * HEAD_DIM
Q_B = N_HEADS_FOX * HEAD_DIM
F_B = N_HEADS_FOX
D_IN = Q_A + 2 * KV_A + 3 * Q_B + F_B
D_MIX = Q_A + Q_B

kernel_name = "hybrid_swa_sink_alibi_fox_convffn_adaln"


def rms_norm(x, g):
    xf = x.astype(jnp.float32)
    y = xf * lax.rsqrt(jnp.mean(xf * xf, axis=-1, keepdims=True) + EPS)
    return (y * g.astype(jnp.float32)).astype(x.dtype)


def modulate(h, shift, scale):
    return h * (1 + scale[:, None, :]) + shift[:, None, :]


def alibi_slopes(n_heads):
    return jnp.exp2(-8.0 * (jnp.arange(n_heads, dtype=jnp.float32) + 1) / n_heads)


def swa_attention(q, k, v, sinks):
    B, S, H, D = q.shape
    Hkv = k.shape[2]
    G = H // Hkv
    nb = S // BLOCK
    qb = q.reshape(B, nb, BLOCK, Hkv, G, D)

    def band(t):
        prev = jnp.pad(t[:, :S - BLOCK], ((0, 0), (BLOCK, 0), (0, 0), (0, 0)))
        return jnp.concatenate([prev.reshape(B, nb, BLOCK, Hkv, D),
                                t.reshape(B, nb, BLOCK, Hkv, D)], axis=2)

    kb, vb = band(k), band(v)
    scores = jnp.einsum('bnqhgd,bnkhd->bnhgqk', qb, kb).astype(jnp.float32) * (HEAD_DIM ** -0.5)
    n_idx = jnp.arange(nb)[:, None, None]
    q_idx = jnp.arange(BLOCK)[None, :, None]
    k_idx = jnp.arange(2 * BLOCK)[None, None, :]
    dist = (BLOCK + q_idx) - k_idx
    key_abs = n_idx * BLOCK - BLOCK + k_idx
    valid = (dist >= 0) & (dist < WINDOW) & (key_abs >= 0)
    slopes = alibi_slopes(H).reshape(Hkv, G)[:, :, None, None]
    scores = scores - slopes * dist[0].astype(jnp.float32)
    scores = jnp.where(valid[None, :, None, None], scores, NEG)
    sink = sinks.astype(jnp.float32).reshape(Hkv, G)[:, :, None, None]
    m = jnp.maximum(scores.max(axis=-1, keepdims=True), sink)
    p = jnp.exp(scores - m)
    probs = p / (p.sum(axis=-1, keepdims=True) + jnp.exp(sink - m))
    out = jnp.einsum('bnhgqk,bnkhd->bnqhgd', probs.astype(v.dtype), vb)
    return out.reshape(B, S, H, D)


def forgetting_attention(q, k, v, f_logit):
    B, S, H, D = q.shape
    nb = S // BLOCK
    F = jnp.cumsum(jax.nn.log_sigmoid(f_logit.astype(jnp.float32)), axis=1)
    Fk = F.transpose(0, 2, 1)
    qb = q.reshape(B, nb, BLOCK, H, D).transpose(1, 0, 2, 3, 4)
    Fq = Fk.reshape(B, H, nb, BLOCK).transpose(2, 0, 1, 3)
    k_pos = jnp.arange(S)

    def one_block(args):
        qi, Fi, n = args
        s = jnp.einsum('bqhd,bkhd->bhqk', qi, k).astype(jnp.float32) * (HEAD_DIM ** -0.5)
        s = s + (Fi[..., None] - Fk[:, :, None, :])
        q_pos = n * BLOCK + jnp.arange(BLOCK)
        s = jnp.where(k_pos[None, :] <= q_pos[:, None], s, NEG)
        p = jax.nn.softmax(s, axis=-1)
        return jnp.einsum('bhqk,bkhd->bqhd', p.astype(v.dtype), v)

    out = lax.map(one_block, (qb, Fq, jnp.arange(nb)))
    return out.transpose(1, 0, 2, 3, 4).reshape(B, S, H, D)


def causal_dwconv(u, w, b):
    C = u.shape[-1]
    y = lax.conv_general_dilated(u, w[:, None, :], window_strides=(1,),
                                 padding=[(CONV_WIDTH - 1, 0)],
                                 dimension_numbers=('NWC', 'WIO', 'NWC'),
                                 feature_group_count=C)
    return y + b


def hybrid_mixer(h, w_in, b_f, sinks, w_out):
    B, S, _ = h.shape
    z = h @ w_in
    o = 0
    qa = z[..., o:o + Q_A].reshape(B, S, N_HEADS_SWA, HEAD_DIM); o += Q_A
    ka = z[..., o:o + KV_A].reshape(B, S, N_KV_SWA, HEAD_DIM); o += KV_A
    va = z[..., o:o + KV_A].reshape(B, S, N_KV_SWA, HEAD_DIM); o += KV_A
    qb = z[..., o:o + Q_B].reshape(B, S, N_HEADS_FOX, HEAD_DIM); o += Q_B
    kb = z[..., o:o + Q_B].reshape(B, S, N_HEADS_FOX, HEAD_DIM); o += Q_B
    vb = z[..., o:o + Q_B].reshape(B, S, N_HEADS_FOX, HEAD_DIM); o += Q_B
    fb = z[..., o:o + F_B] + b_f
    ya = swa_attention(qa, ka, va, sinks).reshape(B, S, Q_A)
    yb = forgetting_attention(qb, kb, vb, fb).reshape(B, S, Q_B)
    return jnp.concatenate([ya, yb], axis=-1) @ w_out


def conv_glu_mlp(h, w_up, conv_w, conv_b, w_down):
    u = causal_dwconv(h @ w_up, conv_w, conv_b)
    a, g = jnp.split(u, 2, axis=-1)
    return (jax.nn.silu(g) * a) @ w_down


def setup_inputs(seed: int = 0) -> dict:
    key = jax.random.key(seed)
    ks = jax.random.split(key, 16)
    f32 = jnp.float32
    nrm = lambda k, shp, s: jax.random.normal(k, shp, f32) * s
    return {
        "x": nrm(ks[0], (BATCH, SEQ, D_MODEL), 1.0),
        "c": nrm(ks[1], (BATCH, D_MODEL), 1.0),
        "w_ada": nrm(ks[2], (D_MODEL, 6 * D_MODEL), 0.5 * D_MODEL ** -0.5),
        "b_ada": nrm(ks[3], (6 * D_MODEL,), 0.02),
        "g_attn": 1.0 + nrm(ks[4], (D_MODEL,), 0.05),
        "w_in": nrm(ks[5], (D_MODEL, D_IN), D_MODEL ** -0.5),
        "b_f": 2.0 + nrm(ks[6], (N_HEADS_FOX,), 0.5),
        "sinks": nrm(ks[7], (N_HEADS_SWA,), 0.5),
        "w_out": nrm(ks[8], (D_MIX, D_MODEL), D_MIX ** -0.5),
        "g_mlp": 1.0 + nrm(ks[9], (D_MODEL,), 0.05),
        "w_up": nrm(ks[10], (D_MODEL, 2 * D_FF), D_MODEL ** -0.5),
        "conv_w": nrm(ks[11], (CONV_WIDTH, 2 * D_FF), CONV_WIDTH ** -0.5),
        "conv_b": nrm(ks[12], (2 * D_FF,), 0.02),
        "w_down": nrm(ks[13], (D_FF, D_MODEL), D_FF ** -0.5),
        "g_final": 1.0 + nrm(ks[14], (D_MODEL,), 0.05),
    }


def reference(x, c, w_ada, b_ada, g_attn, w_in, b_f, sinks, w_out,
              g_mlp, w_up, conv_w, conv_b, w_down, g_final):
    mod = jax.nn.silu(c) @ w_ada + b_ada
    sh1, sc1, ga1, sh2, sc2, ga2 = jnp.split(mod, 6, axis=-1)
    for _ in range(DEPTH):
        h = modulate(rms_norm(x, g_attn), sh1, sc1)
        x = x + ga1[:, None, :] * hybrid_mixer(h, w_in, b_f, sinks, w_out)
        h2 = modulate(rms_norm(x, g_mlp), sh2, sc2)
        x = x + ga2[:, None, :] * conv_glu_mlp(h2, w_up, conv_w, conv_b, w_down)
    return rms_norm(x, g_final)
```

```python
import os
import numpy as np
import ml_dtypes
from contextlib import ExitStack
import concourse.bass as bass
import concourse.mybir as mybir
from concourse.bass_utils import run_bass_kernel_spmd

F32 = mybir.dt.float32
BF16 = mybir.dt.bfloat16
ALU = mybir.AluOpType
AF = mybir.ActivationFunctionType

D = 1024
S = 4096
NOWN = 2048
DFF = 2816
EPS = 1e-6
NEGM = -30000.0
NQ = 2176
Q0 = 1920


class Eng:
    def __init__(self, nc, name, h, es):
        self.name = name
        self.h = h
        self.sem = es.enter_context(nc.semaphore("s_" + name))
        self.cnt = 0
        self.seen = {}


class Buf:
    def __init__(self, name, excl=False):
        self.name = name
        self.excl = excl
        self.w = {}
        self.r = {}
        self.dsem = None
        self.dcnt = 0


def _upd(d, ev):
    s, v = ev
    if d.get(id(s), (None, 0))[1] < v:
        d[id(s)] = ev


class K:
    def __init__(self, nc, es):
        self.nc = nc
        self.es = es
        self.pe = Eng(nc, "pe", nc.tensor, es)
        self.act = Eng(nc, "act", nc.scalar, es)
        self.dve = Eng(nc, "dve", nc.vector, es)
        self.pool = Eng(nc, "pool", nc.gpsimd, es)
        self.sp = Eng(nc, "sp", nc.sync, es)
        self.engs = [self.pe, self.act, self.dve, self.pool, self.sp]
        self.dsems = []
        self.pend = []
        self.nbuf = 0

    def buf(self, name=None, excl=False):
        self.nbuf += 1
        return Buf(name or "b%d" % self.nbuf, excl)

    def _need(self, eng, reads, writes, pwrites=(), xreads=()):
        ev = {}
        for b in xreads:
            for e in b.w.values():
                _upd(ev, e)
        for b in reads:
            for e in b.w.values():
                _upd(ev, e)
            if b.excl:
                for e in b.r.values():
                    if e[0] is not eng.sem:
                        _upd(ev, e)
        for b in writes:
            for e in b.w.values():
                _upd(ev, e)
            for e in b.r.values():
                _upd(ev, e)
        for b in pwrites:
            for e in b.r.values():
                _upd(ev, e)
        for s, v in ev.values():
            if eng is self.pe and s is self.pe.sem:
                continue
            if eng.seen.get(id(s), 0) >= v:
                continue
            eng.h.wait_ge(s, v)
            eng.seen[id(s)] = v

    def _reg(self, ev, reads, writes, pwrites=()):
        for b in reads:
            _upd(b.r, ev)
        for b in writes:
            b.w = {}
            _upd(b.w, ev)
            b.r = {}
        for b in pwrites:
            _upd(b.w, ev)

    def op(self, eng, fn, reads=(), writes=(), pwrites=(), xreads=()):
        self._need(eng, reads, writes, pwrites, xreads)
        ins = fn()
        eng.cnt += 1
        ins.then_inc(eng.sem, 1)
        self._reg((eng.sem, eng.cnt), list(reads) + list(xreads), writes, pwrites)
        return ins

    def mm(self, out, lhsT, rhs, reads, writes, start=True, stop=True, inc=True, transpose=False):
        pe = self.pe
        self._need(pe, reads, writes)
        if transpose:
            ins = self.nc.tensor.transpose(out, lhsT, rhs)
        else:
            ins = self.nc.tensor.matmul(out, lhsT, rhs, start=start, stop=stop)
        self.pend.append((list(reads), list(writes)))
        if inc:
            pe.cnt += 1
            ins.then_inc(pe.sem, 1)
            for r_, w_ in self.pend:
                self._reg((pe.sem, pe.cnt), r_, w_)
            self.pend = []
        return ins

    def dma(self, q, out, in_, prim, reads=(), writes=(), pwrites=()):
        assert not (q is self.pe)
        if prim.dsem is None:
            prim.dsem = self.es.enter_context(self.nc.semaphore("d_%s" % prim.name))
            self.dsems.append(prim)
        self._need(q, reads, writes, pwrites)
        ins = q.h.dma_start(out=out, in_=in_)
        prim.dcnt += 16
        ins.then_inc(prim.dsem, 16)
        self._reg((prim.dsem, prim.dcnt), reads, writes, pwrites)
        return ins

    def barrier(self):
        assert not self.pend
        evs = [(e.sem, e.cnt) for e in self.engs if e.cnt > 0]
        evs += [(b.dsem, b.dcnt) for b in self.dsems if b.dcnt > 0]
        for e in self.engs:
            for s, v in evs:
                if s is e.sem:
                    continue
                if e.seen.get(id(s), 0) >= v:
                    continue
                e.h.wait_ge(s, v)
                e.seen[id(s)] = v


class StopBuild(Exception):
    pass


def build_nc():
    STOP = int(os.environ.get('MKSTOP', '99'))
    nc = bass.Bass("TRN2", target_bir_lowering=False)

    def din(name, shape, dt=F32):
        return nc.dram_tensor(name, list(shape), dt, kind="ExternalInput").ap()

    xa = din("xa", [S, D])
    ct = din("ct", [128, 8])
    w_ada = din("w_ada", [D, 6 * D])
    b_ada = din("b_ada", [1, 6 * D])
    gat = din("gat", [128, 8])
    gmt = din("gmt", [128, 8])
    gfb = din("gfb", [128, D])
    w_in = din("w_in", [D, 2312])
    bfc = din("bfc", [8, 1])
    sinkb = din("sinkb", [128, 8])
    w_out = din("w_out", [D, D])
    w_up = din("w_up", [D, 2 * DFF])
    cwt = din("cwt", [128, 44, 3])
    cbt = din("cbt", [128, 44])
    w_down = din("w_down", [DFF, D])
    pmd = din("pm", [128, 1])
    hfd = din("hf", [128, 1])
    cmaskd = din("cmask", [128, 128], BF16)
    swabd = din("swab", [128, 8, 2, 128], BF16)
    identd = din("ident", [128, 128], BF16)
    yout = nc.dram_tensor("y", [NOWN, D], F32, kind="ExternalOutput").ap()
    DBG = bool(os.environ.get("MKDBG"))
    if DBG:
        dbg_mod = nc.dram_tensor("dbg_mod", [1, 6 * D], F32, kind="ExternalOutput").ap()
        dbg_h = nc.dram_tensor("dbg_h", [128, 8, S], BF16, kind="ExternalOutput").ap()
        dbg_g = nc.dram_tensor("dbg_g", [8, 3, S], BF16, kind="ExternalOutput").ap()
        dbg_at = nc.dram_tensor("dbg_at", [128, 8, NQ], BF16, kind="ExternalOutput").ap()
        dbg_ka = nc.dram_tensor("dbg_ka", [2, 70, S], BF16, kind="ExternalOutput").ap()
        dbg_qa = nc.dram_tensor("dbg_qa", [2, 70, NQ], BF16, kind="ExternalOutput").ap()
        dbg_v = nc.dram_tensor("dbg_v", [128, 32, 2, 128], BF16, kind="ExternalOutput").ap()
        dbg_x1 = nc.dram_tensor("dbg_x1", [128, 5, D], F32, kind="ExternalOutput").ap()
        dbg_h2 = nc.dram_tensor("dbg_h2", [128, 8, 640], BF16, kind="ExternalOutput").ap()
        dbg_ht = nc.dram_tensor("dbg_ht", [128, 22, 512], BF16, kind="ExternalOutput").ap()

    w_in_r = w_in.rearrange("(k p) c -> p k c", p=128)
    w_up_r = w_up.rearrange("(k p) c -> p k c", p=128)
    wup_bf = nc.dram_tensor("wup_bf", [44, 128, 8, 128], BF16, kind="Internal").ap()

    try:
      with ExitStack() as es:
        k = K(nc, es)
        PE, ACT, DVE, POOL, SP = k.pe, k.act, k.dve, k.pool, k.sp

        def sb(name, shape, dt=F32, st=es):
            return st.enter_context(nc.sbuf_tensor("sb_" + name, list(shape), dt))

        ps_t = [es.enter_context(nc.psum_tensor("ps%d" % i, [128, 512], F32)) for i in range(8)]
        ps_b = [k.buf("ps%d" % i, excl=True) for i in range(8)]
        psi = [0]
        psti = [0]
        pst_pool = [list(range(8))]

        def nps():
            i = psi[0] % 4
            psi[0] += 1
            return ps_t[i], ps_b[i]

        pai = [0]

        def npa():
            i = 4 + pai[0] % 2
            pai[0] += 1
            return ps_t[i], ps_b[i]

        def npst():
            i = pst_pool[0][psti[0] % len(pst_pool[0])]
            psti[0] += 1
            return ps_t[i][:].bitcast(BF16), ps_b[i]

        cb = k.buf("consts")
        ident = sb("ident", [128, 128], BF16)
        cmask = sb("cmask", [128, 128], BF16)
        pm = sb("pm", [128, 1])
        hf = sb("hf", [128, 1])
        ct_s = sb("ct_s", [128, 8])
        gat_s = sb("gat_s", [128, 8])
        gmt_s = sb("gmt_s", [128, 8])
        gfb_s = sb("gfb_s", [128, D])
        bfc_s = sb("bfc_s", [8, 1])
        sink_s = sb("sink_s", [128, 8])
        cw_s = sb("cw_s", [128, 44, 3])
        cbb_s = sb("cbb_s", [128, 44])
        for dst, src in ((ident, identd), (cmask, cmaskd), (pm, pmd), (hf, hfd),
                         (ct_s, ct), (gat_s, gat), (gmt_s, gmt), (gfb_s, gfb), (bfc_s, bfc),
                         (sink_s, sinkb), (cw_s, cwt), (cbb_s, cbt)):
            k.dma(SP, dst[:], src[:], cb, writes=[cb])

        cc = k.buf("cc")
        ones32 = sb("ones32", [128, 128])
        mhalf = sb("mhalf", [128, 1])
        k.op(DVE, lambda: nc.vector.memset(ones32[:], 1.0), writes=[cc])
        k.op(DVE, lambda: nc.vector.memset(mhalf[:], -0.5), writes=[cc])
        modT = sb("modT", [128, 32])
        A1 = sb("A1", [128, 8])
        A2 = sb("A2", [128, 8])
        GA1b = sb("GA1b", [128, D])
        GA2b = sb("GA2b", [128, D])
        nbf = sb("nbf", [8, 1])
        es_s = sb("es_s", [128, 8])
        modb = k.buf("mod")
        AT = sb("AT", [128, 8, NQ], BF16)
        ATb = k.buf("AT")
        stgb = k.buf("stg")

        def rms_stats(xt, xb, junk, junkb, ssq, rstd, stb):
            k.op(ACT, lambda: nc.scalar.activation(junk[:], xt, AF.Square, accum_out=ssq[:]),
                 reads=[xb], writes=[junkb, stb])
            k.op(POOL, lambda: nc.gpsimd.tensor_scalar(rstd[:], ssq[:], 1.0 / D, EPS, ALU.mult, ALU.add),
                 reads=[stb], writes=[stb])
            k.op(POOL, lambda: nc.gpsimd.tensor_tensor(rstd[:], rstd[:], mhalf[:], ALU.pow),
                 reads=[stb, cc], writes=[stb])

        with ExitStack() as s1:
            swab = sb("swab", [128, 8, 2, 128], BF16, s1)
            esb = sb("esb", [128, 8, 128], F32, s1)
            k.dma(SP, swab[:], swabd[:], cb, writes=[cb])
            hNT = sb("hNT", [128, 8, S], BF16, s1)
            hN_b = [k.buf("hN%d" % i) for i in range(8)]
            s0 = ExitStack()
            if True:
                modrow = sb("modrow", [1, 6 * D], F32, s0)
                bada_s = sb("bada_s", [1, 6 * D], F32, s0)
                scb = k.buf("sc")
                sc = sb("sc", [128, 8], F32, s0)
                k.dma(SP, bada_s[:], b_ada[:], cb, writes=[cb])
                k.op(ACT, lambda: nc.scalar.activation(sc[:], ct_s[:], AF.Silu), reads=[cb], writes=[scb])
                wa_t = [sb("wa%d" % i, [128, 1536], F32, s0) for i in range(2)]
                wa_b = [k.buf("wa%d" % i) for i in range(2)]
                mrb = k.buf("modrow")
                adac = [0]

                def ada_step(c0, kk):
                    wt, wb = wa_t[adac[0] % 2], wa_b[adac[0] % 2]
                    adac[0] += 1
                    k.dma(SP, wt[:, 0:1536], w_ada[kk * 128:(kk + 1) * 128, c0:c0 + 1536], wb, writes=[wb])
                    for j in range(3):
                        k.mm(ps_t[j][0:1, :], sc[:, kk:kk + 1], wt[:, j * 512:(j + 1) * 512], [scb, wb], [ps_b[j]],
                             start=(kk == 0), stop=(kk == 7), inc=(j == 2))
                    if kk == 7:
                        for j in range(3):
                            cs = slice(c0 + j * 512, c0 + (j + 1) * 512)
                            k.op(DVE, lambda: nc.vector.tensor_tensor(modrow[0:1, cs], ps_t[j][0:1, :], bada_s[0:1, cs], ALU.add),
                                 reads=[ps_b[j], cb], pwrites=[mrb])

                for pz in range(2):
                    for kk in range(8):
                        ada_step(pz * 1536, kk)
                pt, pb = nps()
                for idx in range(16):
                    col0 = [0, 1024][idx // 8] + (idx % 8) * 128
                    k.mm(pt[:, idx:idx + 1], modrow[0:1, col0:col0 + 128], ones32[0:1, 0:1], [mrb, cc], [pb],
                         start=True, stop=True, inc=(idx == 15))
                k.op(DVE, lambda: nc.vector.tensor_copy(modT[:, 0:16], pt[:, 0:16]), reads=[pb], writes=[modb])
                k.op(DVE, lambda: nc.vector.tensor_scalar(A1[:], modT[:, 8:16], 1.0, None, ALU.add), reads=[modb], writes=[modb])
                k.op(DVE, lambda: nc.vector.tensor_tensor(A1[:], A1[:], gat_s[:], ALU.mult), reads=[modb, cb], writes=[modb])
                for hh in range(2):
                    pt, pb = nps()
                    k.mm(pt[:, :], ones32[0:1, 0:128], modrow[0:1, 2048 + hh * 512:2048 + (hh + 1) * 512], [mrb, cc], [pb])
                    k.op(DVE, lambda: nc.vector.tensor_copy(GA1b[:, hh * 512:(hh + 1) * 512], pt[:, :]), reads=[pb], writes=[modb])
                k.op(DVE, lambda: nc.vector.tensor_scalar(nbf[:], bfc_s[:], -1.0, None, ALU.mult), reads=[cb], writes=[modb])
                k.op(ACT, lambda: nc.scalar.activation(es_s[:], sink_s[:], AF.Exp), reads=[cb], writes=[modb])
                for h in range(8):
                    k.op(DVE, lambda: nc.vector.tensor_scalar(esb[:, h, :], ones32[:, :], es_s[:, h:h + 1], None, ALU.mult),
                         reads=[modb, cc], writes=[modb])
            SH1 = modT[:, 0:8]
            SH2 = modT[:, 16:24]

            pst_pool[0] = [3, 4, 5, 6]
            psti[0] = 0
            with ExitStack() as s1a:
                NX = 3
                x_t = [sb("x%d" % i, [128, D], F32, s1a) for i in range(NX)]
                x_b = [k.buf("x%d" % i) for i in range(NX)]
                xn_t = [sb("xn%d" % i, [128, D], BF16, s1a) for i in range(2)]
                xn_b = [k.buf("xn%d" % i) for i in range(2)]
                junk = sb("junk", [128, D], BF16, s1a)
                junkb = k.buf("junk")
                st_t = [(sb("ssq%d" % i, [128, 1], F32, s1a), sb("rstd%d" % i, [128, 1], F32, s1a)) for i in range(NX)]
                st_b = [k.buf("st%d" % i) for i in range(NX)]
                def p1_a(t):
                    xt, xb = x_t[t % NX], x_b[t % NX]
                    ssq, rstd = st_t[t % NX]
                    stb = st_b[t % NX]
                    xn, xnb = xn_t[t % 2], xn_b[t % 2]
                    k.dma(SP, xt[:], xa[t * 128:(t + 1) * 128, :], xb, writes=[xb])
                    rms_stats(xt[:], xb, junk, junkb, ssq, rstd, stb)
                    k.op(DVE, lambda: nc.vector.tensor_scalar(xn[:], xt[:], rstd[:, 0:1], None, ALU.mult),
                         reads=[xb, stb], writes=[xnb])
                    ptA, pbA = npst()
                    ptB, pbB = npst()
                    for c in range(8):
                        pt, pb = (ptA, pbA) if c % 2 == 0 else (ptB, pbB)
                        k.mm(pt[:, (c // 2) * 128:(c // 2 + 1) * 128], xn[:, c * 128:(c + 1) * 128], ident[:], [xnb, cb], [pb],
                             inc=(c >= 6), transpose=True)
                    return ptA, pbA, ptB, pbB

                def p1_b(t, ptA, pbA, ptB, pbB):
                    hb = hN_b[t // 4]
                    for c in range(8):
                        o = hNT[:, c, t * 128:(t + 1) * 128]
                        if c % 2 == 0:
                            i_ = ptA[:, (c // 2) * 128:(c // 2 + 1) * 128]
                            k.op(ACT, lambda: nc.scalar.activation(o, i_, AF.Identity, bias=SH1[:, c:c + 1], scale=A1[:, c:c + 1]),
                                 reads=[pbA, modb], pwrites=[hb])
                        else:
                            i_ = ptB[:, (c // 2) * 128:(c // 2 + 1) * 128]
                            k.op(DVE, lambda: nc.vector.tensor_scalar(o, i_, A1[:, c:c + 1], SH1[:, c:c + 1], ALU.mult, ALU.add),
                                 reads=[pbB, modb], pwrites=[hb])

                pend1 = {0: p1_a(0)}
                for t in range(32):
                    if t + 1 < 32:
                        pend1[t + 1] = p1_a(t + 1)
                    if t % 2 == 0:
                        ada_step(3072 + (t // 16) * 1536, (t // 2) % 8)
                    p1_b(t, *pend1.pop(t))
                pt, pb = nps()
                for idx in range(16):
                    col0 = [3072, 4096][idx // 8] + (idx % 8) * 128
                    k.mm(pt[:, idx:idx + 1], modrow[0:1, col0:col0 + 128], ones32[0:1, 0:1], [mrb, cc], [pb],
                         start=True, stop=True, inc=(idx == 15))
                k.op(DVE, lambda: nc.vector.tensor_copy(modT[:, 16:32], pt[:, 0:16]), reads=[pb], writes=[modb])
                k.op(DVE, lambda: nc.vector.tensor_scalar(A2[:], modT[:, 24:32], 1.0, None, ALU.add), reads=[modb], writes=[modb])
                k.op(DVE, lambda: nc.vector.tensor_tensor(A2[:], A2[:], gmt_s[:], ALU.mult), reads=[modb, cb], writes=[modb])
                for hh in range(2):
                    pt, pb = nps()
                    k.mm(pt[:, :], ones32[0:1, 0:128], modrow[0:1, 5120 + hh * 512:5120 + (hh + 1) * 512], [mrb, cc], [pb])
                    k.op(DVE, lambda: nc.vector.tensor_copy(GA2b[:, hh * 512:(hh + 1) * 512], pt[:, :]), reads=[pb], writes=[modb])
                if DBG:
                    k.dma(SP, dbg_mod[:], modrow[:], k.buf('dbg0'), reads=[mrb])
                if DBG:
                    k.dma(SP, dbg_h[:], hNT[:], k.buf('dbg1'), reads=hN_b)
                k.barrier()
            s0.close()

            if STOP <= 1:
                k.barrier()
                raise StopBuild()
            pst_pool[0] = [6, 7]
            psti[0] = 0
            Gs = sb("Gs", [8, 3, S], BF16, s1)
            Gsb = k.buf("Gs")
            with ExitStack() as s2a:
                wf = sb("wf", [128, 8, 8], BF16, s2a)
                wfb = k.buf("wf")
                k.dma(POOL, wf[:], w_in_r[:, :, 2304:2312], wfb, writes=[wfb])
                E1 = sb("E1", [8, S], F32, s2a)
                E2 = sb("E2", [8, S], F32, s2a)
                on8 = sb("on8", [8, S], F32, s2a)
                e1b, e2b, onb = k.buf("E1"), k.buf("E2"), k.buf("on8")
                k.op(POOL, lambda: nc.gpsimd.memset(on8[:], 1.0), writes=[onb])
                for n in range(8):
                    pt, pb = nps()
                    for kk in range(8):
                        k.mm(pt[0:8, :], wf[:, kk, :], hNT[:, kk, n * 512:(n + 1) * 512], [wfb, hN_b[n]], [pb],
                             start=(kk == 0), stop=(kk == 7), inc=(kk == 7))
                    k.op(ACT, lambda: nc.scalar.activation(E1[:, n * 512:(n + 1) * 512], pt[0:8, :], AF.Exp,
                                                           bias=nbf[:, 0:1], scale=-1.0),
                         reads=[pb, modb], writes=[e1b])
                k.op(ACT, lambda: nc.scalar.activation(E2[:], E1[:], AF.Ln, bias=1.0, scale=1.0), reads=[e1b], writes=[e2b])
                k.op(DVE, lambda: nc.vector.tensor_tensor_scan(E1[:], on8[:], E2[:], 0.0, ALU.mult, ALU.add),
                     reads=[onb, e2b], writes=[e1b])
                k.op(DVE, lambda: nc.vector.tensor_copy(Gs[:, 0, :], E1[:]), reads=[e1b], writes=[Gsb])
                k.op(DVE, lambda: nc.vector.tensor_tensor(E2[:], E1[:], Gs[:, 0, :], ALU.subtract), reads=[e1b, Gsb], writes=[e2b])
                k.op(DVE, lambda: nc.vector.tensor_copy(Gs[:, 1, :], E2[:]), reads=[e2b], writes=[Gsb])
                k.op(DVE, lambda: nc.vector.tensor_tensor(E1[:], E2[:], Gs[:, 1, :], ALU.subtract), reads=[e2b, Gsb], writes=[e1b])
                k.op(DVE, lambda: nc.vector.tensor_copy(Gs[:, 2, :], E1[:]), reads=[e1b], writes=[Gsb])
                if DBG:
                    k.dma(SP, dbg_g[:], Gs[:], k.buf('dbg2'), reads=[Gsb])
                k.barrier()

            PT_t = [sb("PT%d" % i, [128, 512], BF16, s1) for i in range(4)]
            PT_b = [k.buf("PT%d" % i) for i in range(4)]
            pti = [0]

            def npt():
                i = pti[0] % 4
                pti[0] += 1
                return PT_t[i], PT_b[i]
            den_t = [sb("den%d" % i, [64, 512], F32, s1) for i in range(2)]
            den_b = [k.buf("den%d" % i) for i in range(2)]
            deni = [0]
            wq_t = [sb("wq%d" % i, [128, 8, 128], BF16, s1) for i in range(2)]
            wk_t = [sb("wk%d" % i, [128, 8, 128], BF16, s1) for i in range(2)]
            wv_t = [sb("wv%d" % i, [128, 8, 128], BF16, s1) for i in range(2)]
            wq_b = [k.buf("wq%d" % i) for i in range(2)]
            wk_b = [k.buf("wk%d" % i) for i in range(2)]
            wv_b = [k.buf("wv%d" % i) for i in range(2)]

            QCH = [(0, 128)] + [(128 + 512 * i, 512) for i in range(4)]

            with ExitStack() as s2b:
                KA = [sb("KA%d" % i, [70, S], BF16, s2b) for i in range(2)]
                QA = [sb("QA%d" % i, [70, NQ], BF16, s2b) for i in range(2)]
                V = sb("V", [128, 32, 2, 128], BF16, s2b)
                KAb = [k.buf("KA%d" % i) for i in range(2)]
                QAb = [k.buf("QA%d" % i) for i in range(2)]
                Vb = k.buf("V")
                for i in range(2):
                    k.op(POOL, lambda: nc.gpsimd.memset(KA[i][64:70, :], -1.0), writes=[KAb[i]])
                    k.op(POOL, lambda: nc.gpsimd.memset(QA[i][64:70, :], 1.0), writes=[QAb[i]])
                k.op(POOL, lambda: nc.gpsimd.memset(V[:, :, :, 64:128], 1.0), writes=[Vb])

                def load_fox_w(g):
                    s = g % 2
                    k.dma(POOL, wq_t[s][:], w_in_r[:, :, 768 + 128 * g:768 + 128 * (g + 1)], wq_b[s], writes=[wq_b[s]])
                    k.dma(POOL, wk_t[s][:], w_in_r[:, :, 1280 + 128 * g:1280 + 128 * (g + 1)], wk_b[s], writes=[wk_b[s]])
                    k.dma(POOL, wv_t[s][:], w_in_r[:, :, 1792 + 128 * g:1792 + 128 * (g + 1)], wv_b[s], writes=[wv_b[s]])

                load_fox_w(0)
                for g in range(4):
                    s = g % 2
                    if g + 1 < 4:
                        load_fox_w(g + 1)
                    for sl_ in range(11 * g, 11 * (g + 1)):
                        k.dma(POOL, wup_bf[sl_], w_up_r[:, :, sl_ * 128:(sl_ + 1) * 128], stgb, pwrites=[stgb])
                    for i in range(2):
                        h = 2 * g + i
                        k.dma(SP, KA[i][64:67, :], Gs[h:h + 1, :, :], KAb[i], reads=[Gsb], writes=[KAb[i]])
                        k.dma(SP, QA[i][67:70, :], Gs[h:h + 1, :, Q0:S], QAb[i], reads=[Gsb], writes=[QAb[i]])
                    for n in range(8):
                        pt, pb = nps()
                        for kk in range(8):
                            k.mm(pt[:, :], wk_t[s][:, kk, :], hNT[:, kk, n * 512:(n + 1) * 512], [wk_b[s], hN_b[n]], [pb],
                                 start=(kk == 0), stop=(kk == 7), inc=(kk == 7))
                        k.op(ACT, lambda: nc.scalar.copy(KA[0][0:64, n * 512:(n + 1) * 512], pt[0:64, :]), xreads=[pb], writes=[KAb[0]])
                        k.op(DVE, lambda: nc.vector.tensor_copy(KA[1][0:64, n * 512:(n + 1) * 512], pt[64:128, :]), xreads=[pb], writes=[KAb[1]])
                    for (c0, n_) in QCH:
                        pt, pb = nps()
                        t0 = Q0 + c0
                        for kk in range(8):
                            k.mm(pt[:, 0:n_], wq_t[s][:, kk, :], hNT[:, kk, t0:t0 + n_], [wq_b[s], hN_b[t0 // 512]], [pb],
                                 start=(kk == 0), stop=(kk == 7), inc=(kk == 7))
                        k.op(ACT, lambda: nc.scalar.mul(QA[0][0:64, c0:c0 + n_], pt[0:64, 0:n_], 0.125), xreads=[pb], writes=[QAb[0]])
                        k.op(DVE, lambda: nc.vector.tensor_scalar(QA[1][0:64, c0:c0 + n_], pt[64:128, 0:n_], 0.125, None, ALU.mult),
                             xreads=[pb], writes=[QAb[1]])
                    for kb4 in range(8):
                        pt, pb = nps()
                        for j in range(4):
                            kb = kb4 * 4 + j
                            for kk in range(8):
                                k.mm(pt[:, j * 128:(j + 1) * 128], hNT[:, kk, kb * 128:(kb + 1) * 128], wv_t[s][:, kk, :],
                                     [wv_b[s], hN_b[kb // 4]], [pb], start=(kk == 0), stop=(kk == 7), inc=(kk == 7 and j == 3))
                        eng = ACT if kb4 % 2 == 0 else DVE
                        src = pt[:, :].rearrange("p (j h d) -> p j h d", j=4, h=2)
                        dst = V[:, kb4 * 4:(kb4 + 1) * 4, :, 0:64]
                        if eng is ACT:
                            k.op(ACT, lambda: nc.scalar.copy(dst, src), reads=[pb], writes=[Vb])
                        else:
                            k.op(DVE, lambda: nc.vector.tensor_copy(dst, src), reads=[pb], writes=[Vb])
                    items = []
                    for qi, (c0, n_) in enumerate(QCH):
                        d0 = 15 if qi == 0 else 16 + 4 * (qi - 1)
                        nkb = d0 + n_ // 128
                        for i in range(2):
                            for kb in range(nkb):
                                items.append((qi, c0, n_, d0, nkb, i, kb))
                    stq = {}
                    accs = {}

                    def emit_qk(idx):
                        qi, c0, n_, d0, nkb, i, kb = items[idx]
                        j = kb - d0
                        st, sb_ = nps()
                        lo = 0 if j < 0 else 128 * j
                        kcol = KA[i][0:70, kb * 128:(kb + 1) * 128]
                        if j >= 0:
                            k.mm(st[:, lo:lo + 128], kcol, QA[i][0:70, c0 + lo:c0 + lo + 128], [KAb[i], QAb[i]], [sb_],
                                 start=True, stop=False, inc=False)
                            k.mm(st[:, lo:lo + 128], ident[:], cmask[:], [cb], [sb_], start=False, stop=True,
                                 inc=(lo + 128 >= n_))
                            if lo + 128 < n_:
                                k.mm(st[:, lo + 128:n_], kcol, QA[i][0:70, c0 + lo + 128:c0 + n_], [KAb[i], QAb[i]], [sb_])
                        else:
                            k.mm(st[:, 0:n_], kcol, QA[i][0:70, c0:c0 + n_], [KAb[i], QAb[i]], [sb_])
                        stq[idx] = (st, sb_, lo)

                    def emit_rest(idx):
                        qi, c0, n_, d0, nkb, i, kb = items[idx]
                        h = 2 * g + i
                        st, sb_, lo = stq.pop(idx)
                        if kb == 0:
                            accs[(qi, i)] = npa()
                        at, ab = accs[(qi, i)]
                        pT, pTb = npt()
                        if qi >= 1 and kb < 16:
                            k.op(ACT, lambda: nc.scalar.activation(pT[:, lo:n_], st[:, lo:n_], AF.Exp, bias=pm[:, 0:1], scale=1.0),
                                 reads=[sb_, cb], writes=[pTb])
                        else:
                            k.op(ACT, lambda: nc.scalar.activation(pT[:, lo:n_], st[:, lo:n_], AF.Exp), reads=[sb_], writes=[pTb])
                        k.mm(at[:, lo:n_], V[:, kb, i, :], pT[:, lo:n_], [Vb, pTb], [ab],
                             start=(kb == 0), stop=(kb == nkb - 1), inc=(kb == nkb - 1))
                        if kb == nkb - 1:
                            dn, dnb = den_t[deni[0] % 2], den_b[deni[0] % 2]
                            deni[0] += 1
                            k.op(DVE, lambda: nc.vector.tensor_copy(dn[:, 0:n_], at[64:128, 0:n_]), reads=[ab], writes=[dnb])
                            k.op(DVE, lambda: nc.vector.reciprocal(dn[:, 0:n_], dn[:, 0:n_]), reads=[dnb], writes=[dnb])
                            o = AT[(h % 2) * 64:(h % 2) * 64 + 64, 4 + h // 2, c0:c0 + n_]
                            k.op(DVE, lambda: nc.vector.tensor_tensor(o, at[0:64, 0:n_], dn[:, 0:n_], ALU.mult),
                                 reads=[ab, dnb], pwrites=[ATb])

                    LOOK = 2
                    for idx in range(min(LOOK, len(items))):
                        emit_qk(idx)
                    for idx in range(len(items)):
                        if idx + LOOK < len(items):
                            emit_qk(idx + LOOK)
                        emit_rest(idx)
                if DBG:
                    for i in range(2):
                        k.dma(SP, dbg_ka[i], KA[i][:], k.buf('dbgk%d' % i), reads=[KAb[i]])
                        k.dma(SP, dbg_qa[i], QA[i][:], k.buf('dbgq%d' % i), reads=[QAb[i]])
                    k.dma(SP, dbg_v[:], V[:], k.buf('dbgv'), reads=[Vb])
                k.barrier()

            if STOP <= 2:
                k.barrier()
                raise StopBuild()
            with ExitStack() as s2c:
                KS = sb("KS", [64, 2304], BF16, s2c)
                QS = sb("QS", [64, 4, NQ], BF16, s2c)
                VS = sb("VS", [128, 18, 128], BF16, s2c)
                KSb, QSb, VSb = k.buf("KS"), k.buf("QS"), k.buf("VS")
                k.op(POOL, lambda: nc.gpsimd.memset(VS[:, :, 64:128], 1.0), writes=[VSb])
                dn4_t = [sb("dn4_%d" % i, [128, 4], F32, s2c) for i in range(2)]
                dn4_b = [k.buf("dn4_%d" % i) for i in range(2)]
                atk_t = [sb("atk%d" % i, [128, 256], BF16, s2c) for i in range(2)]
                atk_b = [k.buf("atk%d" % i) for i in range(2)]
                swc = [0]
                T0 = 1792
                for g in range(2):
                    k.dma(POOL, wq_t[0][:], w_in_r[:, :, 256 * g:256 * g + 128], wq_b[0], writes=[wq_b[0]])
                    k.dma(POOL, wq_t[1][:], w_in_r[:, :, 256 * g + 128:256 * g + 256], wq_b[1], writes=[wq_b[1]])
                    k.dma(POOL, wk_t[0][:, :, 0:64], w_in_r[:, :, 512 + 64 * g:512 + 64 * (g + 1)], wk_b[0], writes=[wk_b[0]])
                    k.dma(POOL, wv_t[0][:, :, 0:64], w_in_r[:, :, 640 + 64 * g:640 + 64 * (g + 1)], wv_b[0], writes=[wv_b[0]])
                    for (c0, n_) in [(0, 512), (512, 512), (1024, 512), (1536, 512), (2048, 256)]:
                        pt, pb = nps()
                        t0 = T0 + c0
                        for kk in range(8):
                            k.mm(pt[0:64, 0:n_], wk_t[0][:, kk, 0:64], hNT[:, kk, t0:t0 + n_], [wk_b[0], hN_b[t0 // 512]], [pb],
                                 start=(kk == 0), stop=(kk == 7), inc=(kk == 7))
                        k.op(ACT, lambda: nc.scalar.copy(KS[:, c0:c0 + n_], pt[0:64, 0:n_]), reads=[pb], writes=[KSb])
                    for s in range(2):
                        for (c0, n_) in QCH:
                            pt, pb = nps()
                            t0 = Q0 + c0
                            for kk in range(8):
                                k.mm(pt[:, 0:n_], wq_t[s][:, kk, :], hNT[:, kk, t0:t0 + n_], [wq_b[s], hN_b[t0 // 512]], [pb],
                                     start=(kk == 0), stop=(kk == 7), inc=(kk == 7))
                            k.op(ACT, lambda: nc.scalar.mul(QS[:, 2 * s, c0:c0 + n_], pt[0:64, 0:n_], 0.125), xreads=[pb], writes=[QSb])
                            k.op(DVE, lambda: nc.vector.tensor_scalar(QS[:, 2 * s + 1, c0:c0 + n_], pt[64:128, 0:n_], 0.125, None, ALU.mult),
                                 xreads=[pb], writes=[QSb])
                    for b0 in range(0, 18, 4):
                        nb = min(4, 18 - b0)
                        pt, pb = nps()
                        for j in range(nb):
                            kb = 14 + b0 + j
                            for kk in range(8):
                                k.mm(pt[:, j * 64:(j + 1) * 64], hNT[:, kk, kb * 128:(kb + 1) * 128], wv_t[0][:, kk, 0:64],
                                     [wv_b[0], hN_b[kb // 4]], [pb], start=(kk == 0), stop=(kk == 7), inc=(kk == 7 and j == nb - 1))
                        k.op(DVE, lambda: nc.vector.tensor_copy(VS[:, b0:b0 + nb, 0:64],
                                                                pt[:, 0:nb * 64].rearrange("p (j d) -> p j d", j=nb)),
                             reads=[pb], writes=[VSb])
                    sq = {}

                    def swa_qk(bi):
                        q0 = bi * 128
                        pts = []
                        for which in range(2):
                            st, sb_ = nps()
                            kcol = KS[:, (bi + which) * 128:(bi + which + 1) * 128]
                            k.mm(st[:, :].rearrange("p (a b) -> p a b", a=4), ident[:], swab[:, 4 * g:4 * g + 4, which, :], [cb], [sb_],
                                 start=True, stop=False, inc=False)
                            for m in range(4):
                                k.mm(st[:, m * 128:(m + 1) * 128], kcol, QS[:, m, q0:q0 + 128], [KSb, QSb], [sb_],
                                     start=False, stop=(m == 3), inc=(m == 3))
                            pts.append((st, sb_))
                        sq[bi] = pts

                    def swa_rest1(bi):
                        pts = []
                        for which, (st, sb_) in enumerate(sq.pop(bi)):
                            pT, pTb = npt()
                            if which == 0 and bi == 1:
                                k.op(ACT, lambda: nc.scalar.activation(pT[:, :], st[:, :], AF.Exp, bias=pm[:, 0:1], scale=1.0),
                                     reads=[sb_, cb], writes=[pTb])
                            else:
                                k.op(ACT, lambda: nc.scalar.activation(pT[:, :], st[:, :], AF.Exp), reads=[sb_], writes=[pTb])
                            pts.append((pT, pTb))
                        at, ab = npa()
                        for m in range(4):
                            for which in range(2):
                                pT, pTb = pts[which]
                                k.mm(at[:, m * 65:(m + 1) * 65], pT[:, m * 128:(m + 1) * 128], VS[:, bi + which, 0:65], [VSb, pTb], [ab],
                                     start=(which == 0), stop=(which == 1), inc=(m == 3 and which == 1))
                        c = swc[0]
                        swc[0] += 1
                        dn4, dn4b = dn4_t[c % 2], dn4_b[c % 2]
                        atk, atkb = atk_t[c % 2], atk_b[c % 2]
                        av = at[:, 0:260].rearrange("p (m c) -> p m c", c=65)
                        k.op(DVE, lambda: nc.vector.tensor_tensor(dn4[:, :], av[:, :, 64], es_s[:, 4 * g:4 * g + 4], ALU.add),
                             reads=[ab, modb], writes=[dn4b])
                        k.op(DVE, lambda: nc.vector.reciprocal(dn4[:, :], dn4[:, :]), reads=[dn4b], writes=[dn4b])
                        for m in range(4):
                            k.op(DVE, lambda: nc.vector.tensor_scalar(atk[:, m * 64:(m + 1) * 64], av[:, m, 0:64], dn4[:, m:m + 1], None, ALU.mult),
                                 reads=[ab, dn4b], writes=[atkb] if m == 0 else [], pwrites=[] if m == 0 else [atkb])
                        return atk, atkb

                    def swa_rest2(bi, atk, atkb):
                        q0 = bi * 128
                        pt, pb = npst()
                        for j in range(2):
                            k.mm(pt[:, j * 128:(j + 1) * 128], atk[:, j * 128:(j + 1) * 128], ident[:], [atkb, cb], [pb],
                                 inc=(j == 1), transpose=True)
                        k.op(ACT, lambda: nc.scalar.copy(AT[:, 2 * g:2 * g + 2, q0:q0 + 128], pt[:, 0:256].rearrange("p (j q) -> p j q", j=2)),
                             reads=[pb], pwrites=[ATb])

                    swa_qk(0)
                    prev = None
                    for bi in range(17):
                        if bi + 1 < 17:
                            swa_qk(bi + 1)
                        cur = swa_rest1(bi)
                        if prev is not None:
                            swa_rest2(bi - 1, *prev)
                        prev = cur
                    swa_rest2(16, *prev)
                if DBG:
                    k.dma(SP, dbg_at[:], AT[:], k.buf('dbg3'), reads=[ATb])
                k.barrier()
            k.barrier()

        if STOP <= 3:
            k.barrier()
            raise StopBuild()
        with ExitStack() as s3:
            wo = sb("wo", [128, 8, D], BF16, s3)
            wob = k.buf("wo")
            for kk in range(8):
                k.dma(POOL, wo[:, kk, :], w_out[kk * 128:(kk + 1) * 128, :], wob, pwrites=[wob])
            for kk in range(8):
                k.op(DVE, lambda: nc.vector.tensor_tensor(wo[:, kk, :], wo[:, kk, :], GA1b[:, :], ALU.mult),
                     reads=[modb], writes=[wob] if kk == 0 else [], pwrites=[] if kk == 0 else [wob])
            wd = sb("wd", [128, 22, D], BF16, s3)
            wdb = k.buf("wd")
            NT_MAX = 5
            x1 = sb("xres1", [128, NT_MAX, D], F32, s3)
            x1b = [k.buf("x1_%d" % i) for i in range(NT_MAX)]
            h2T = sb("h2T", [128, 8, NT_MAX * 128], BF16, s3)
            h2b = k.buf("h2T")
            hT = sb("hT", [128, 22, 512], BF16, s3)
            hTb = k.buf("hT")
            carry = sb("carry", [128, 44, 2], F32, s3)
            carb = k.buf("carry")
            U_t = [sb("U%d" % i, [128, 2 + NT_MAX * 128], F32, s3) for i in range(2)]
            U_b = [k.buf("U%d" % i) for i in range(2)]
            ya_t = [sb("ya%d" % i, [128, 512], F32, s3) for i in range(2)]
            ya_b = [k.buf("ya%d" % i) for i in range(2)]
            yg_t = [sb("yg%d" % i, [128, 512], F32, s3) for i in range(2)]
            yg_b = [k.buf("yg%d" % i) for i in range(2)]
            sg_t = [sb("sg%d" % i, [128, 512], F32, s3) for i in range(1)] * 2
            sg_b = [k.buf("sg%d" % i) for i in range(1)] * 2
            NWU = 4
            wu_t = [sb("wu%d" % i, [128, 8, 128], BF16, s3) for i in range(NWU)]
            wu_b = [k.buf("wu%d" % i) for i in range(NWU)]
            xr_t = [sb("xr%d" % i, [128, D], F32, s3) for i in range(2)]
            xr_b = [k.buf("xr%d" % i) for i in range(2)]
            tmp_t = [sb("tmp%d" % i, [128, D], F32, s3) for i in range(2)]
            tmp_b = [k.buf("tmp%d" % i) for i in range(2)]
            xn2_t = [sb("xn2_%d" % i, [128, D], BF16, s3) for i in range(2)]
            xn2_b = [k.buf("xn2_%d" % i) for i in range(2)]
            junk2 = sb("junk2", [128, D], BF16, s3)
            junk2b = k.buf("junk2")
            st2_t = [(sb("ssq2_%d" % i, [128, 1], F32, s3), sb("rstd2_%d" % i, [128, 1], F32, s3)) for i in range(4)]
            st2_b = [k.buf("st2_%d" % i) for i in range(4)]
            cnt = {"t": 0, "u": 0, "o": 0, "s": 0}

            TGS = [list(range(-1, 4)), list(range(4, 8)), list(range(8, 12)), list(range(12, 16))]
            slabs = [(ti, m, part) for ti in range(len(TGS)) for m in range(22) for part in range(2)]

            def issue_wu(n):
                if n >= len(slabs):
                    return
                _, m, part = slabs[n]
                k.dma(SP, wu_t[n % NWU][:], wup_bf[part * 22 + m], wu_b[n % NWU], reads=[stgb], writes=[wu_b[n % NWU]])

            PRE = 3
            for n in range(PRE):
                issue_wu(n)
            for m in range(22):
                k.dma(POOL, wd[:, m, :], w_down[m * 128:(m + 1) * 128, :], wdb, pwrites=[wdb])
            sl = 0
            for ti, tiles in enumerate(TGS):
                ntl = len(tiles)
                ntok = ntl * 128
                nown = ntok - (128 if ti == 0 else 0)
                def op_a(li, ot):
                    acol = (ot + 1) * 128
                    tloc = 15 + ot + 1
                    c = cnt["t"]
                    cnt["t"] += 1
                    xr, xrb = xr_t[c % 2], xr_b[c % 2]
                    k.dma(SP, xr[:], xa[tloc * 128:(tloc + 1) * 128, :], xrb, writes=[xrb])
                    pA, pAb = nps()
                    pB, pBb = nps()
                    for hh, (pp, ppb) in enumerate(((pA, pAb), (pB, pBb))):
                        for kk in range(8):
                            k.mm(pp[:, :], AT[:, kk, acol:acol + 128], wo[:, kk, hh * 512:(hh + 1) * 512], [ATb, wob], [ppb],
                                 start=(kk == 0), stop=(kk == 7), inc=(kk == 7))
                        k.op(DVE, lambda: nc.vector.tensor_tensor(x1[:, li, hh * 512:(hh + 1) * 512], pp[:, :], xr[:, hh * 512:(hh + 1) * 512], ALU.add),
                             reads=[ppb, xrb], writes=[x1b[li]] if hh == 0 else [], pwrites=[] if hh == 0 else [x1b[li]])
                    si = cnt["s"] % 4
                    cnt["s"] += 1
                    ssq, rstd = st2_t[si]
                    stb = st2_b[si]
                    xn, xnb = xn2_t[c % 2], xn2_b[c % 2]
                    rms_stats(x1[:, li, :], x1b[li], junk2, junk2b, ssq, rstd, stb)
                    k.op(DVE, lambda: nc.vector.tensor_scalar(xn[:], x1[:, li, :], rstd[:, 0:1], None, ALU.mult),
                         reads=[x1b[li], stb], writes=[xnb])
                    return xn, xnb

                def op_b(li, xn, xnb):
                    ptA, pbA = npst()
                    ptB, pbB = npst()
                    for cc_ in range(8):
                        pt, pb = (ptA, pbA) if cc_ % 2 == 0 else (ptB, pbB)
                        k.mm(pt[:, (cc_ // 2) * 128:(cc_ // 2 + 1) * 128], xn[:, cc_ * 128:(cc_ + 1) * 128], ident[:], [xnb, cb], [pb],
                             inc=(cc_ >= 6), transpose=True)
                    for cc_ in range(8):
                        o = h2T[:, cc_, li * 128:(li + 1) * 128]
                        if cc_ % 2 == 0:
                            i_ = ptA[:, (cc_ // 2) * 128:(cc_ // 2 + 1) * 128]
                            k.op(ACT, lambda: nc.scalar.activation(o, i_, AF.Identity, bias=SH2[:, cc_:cc_ + 1], scale=A2[:, cc_:cc_ + 1]),
                                 reads=[pbA, modb], pwrites=[h2b])
                        else:
                            i_ = ptB[:, (cc_ // 2) * 128:(cc_ // 2 + 1) * 128]
                            k.op(DVE, lambda: nc.vector.tensor_scalar(o, i_, A2[:, cc_:cc_ + 1], SH2[:, cc_:cc_ + 1], ALU.mult, ALU.add),
                                 reads=[pbB, modb], pwrites=[h2b])

                pendo = {0: op_a(0, tiles[0])}
                for li, ot in enumerate(tiles):
                    if li + 1 < ntl:
                        pendo[li + 1] = op_a(li + 1, tiles[li + 1])
                    op_b(li, *pendo.pop(li))
                off = 128 if ti == 0 else 0
                for m in range(22):
                    for part in range(2):
                        mp = part * 22 + m
                        wu, wub = wu_t[sl % NWU], wu_b[sl % NWU]
                        issue_wu(sl + PRE)
                        sl += 1
                        U, Ub = U_t[cnt["u"] % 2], U_b[cnt["u"] % 2]
                        cnt["u"] += 1
                        if part == 0:
                            yv, yvb = ya_t[m % 2], ya_b[m % 2]
                        else:
                            yv, yvb = yg_t[m % 2], yg_b[m % 2]
                        first = True
                        for c0 in range(0, ntok, 512):
                            n_ = min(512, ntok - c0)
                            pt, pb = nps()
                            for kk in range(8):
                                k.mm(pt[:, 0:n_], wu[:, kk, :], h2T[:, kk, c0:c0 + n_], [wub, h2b], [pb],
                                     start=(kk == 0), stop=(kk == 7), inc=(kk == 7))
                            k.op(ACT, lambda: nc.scalar.copy(U[:, 2 + c0:2 + c0 + n_], pt[:, 0:n_]), reads=[pb],
                                 writes=[Ub] if first else [], pwrites=[] if first else [Ub])
                            lo = max(c0, off)
                            hi = c0 + n_
                            if hi > lo:
                                k.op(ACT, lambda: nc.scalar.activation(yv[:, lo - off:hi - off], pt[:, lo - c0:hi - c0], AF.Identity,
                                                                       bias=cbb_s[:, mp:mp + 1], scale=cw_s[:, mp, 2:3]),
                                     reads=[pb, cb], writes=[yvb] if first else [], pwrites=[] if first else [yvb])
                                first = False
                        if ti == 0:
                            k.op(ACT, lambda: nc.scalar.activation(U[:, 128:130], U[:, 128:130], AF.Copy, scale=hf[:, 0:1]),
                                 reads=[cb], writes=[Ub])
                        else:
                            k.op(ACT, lambda: nc.scalar.copy(U[:, 0:2], carry[:, mp, :]), reads=[carb], writes=[Ub])
                        k.op(ACT, lambda: nc.scalar.copy(carry[:, mp, :], U[:, ntok:ntok + 2]), reads=[Ub], writes=[carb])
                        b0 = 2 + off
                        k.op(DVE, lambda: nc.vector.scalar_tensor_tensor(yv[:, 0:nown], U[:, b0 - 1:b0 - 1 + nown], cw_s[:, mp, 1:2], yv[:, 0:nown], ALU.mult, ALU.add),
                             reads=[Ub, cb], writes=[yvb])
                        k.op(DVE, lambda: nc.vector.scalar_tensor_tensor(yv[:, 0:nown], U[:, b0 - 2:b0 - 2 + nown], cw_s[:, mp, 0:1], yv[:, 0:nown], ALU.mult, ALU.add),
                             reads=[Ub, cb], writes=[yvb])
                        if part == 1:
                            sg, sgb = sg_t[m % 2], sg_b[m % 2]
                            k.op(ACT, lambda: nc.scalar.activation(sg[:, 0:nown], yv[:, 0:nown], AF.Silu), reads=[yvb], writes=[sgb])
                            k.op(DVE, lambda: nc.vector.tensor_tensor(hT[:, m, 0:nown], sg[:, 0:nown], ya_t[m % 2][:, 0:nown], ALU.mult),
                                 reads=[sgb, ya_b[m % 2]], pwrites=[hTb])
                if DBG and ti == 0:
                    k.dma(SP, dbg_x1[:], x1[:], k.buf('dbg4'), reads=x1b)
                    k.dma(SP, dbg_h2[:], h2T[:], k.buf('dbg5'), reads=[h2b])
                    k.dma(SP, dbg_ht[:], hT[:], k.buf('dbg6'), reads=[hTb])
                if ti == 0:
                    for m in range(22):
                        k.op(DVE, lambda: nc.vector.tensor_tensor(wd[:, m, :], wd[:, m, :], GA2b[:, :], ALU.mult),
                             reads=[modb], writes=[wdb] if m == 0 else [], pwrites=[] if m == 0 else [wdb])
                for li, ot in enumerate(tiles):
                    if ot < 0:
                        continue
                    hc = (li - (1 if ti == 0 else 0)) * 128
                    c = cnt["o"]
                    cnt["o"] += 1
                    tm, tmb = tmp_t[c % 2], tmp_b[c % 2]
                    pA, pAb = nps()
                    pB, pBb = nps()
                    for hh, (pp, ppb) in enumerate(((pA, pAb), (pB, pBb))):
                        for m in range(22):
                            k.mm(pp[:, :], hT[:, m, hc:hc + 128], wd[:, m, hh * 512:(hh + 1) * 512], [hTb, wdb], [ppb],
                                 start=(m == 0), stop=(m == 21), inc=(m == 21))
                        k.op(DVE, lambda: nc.vector.tensor_tensor(tm[:, hh * 512:(hh + 1) * 512], pp[:, :], x1[:, li, hh * 512:(hh + 1) * 512], ALU.add),
                             reads=[ppb, x1b[li]], writes=[tmb] if hh == 0 else [], pwrites=[] if hh == 0 else [tmb])
                    si = cnt["s"] % 4
                    cnt["s"] += 1
                    ssq, rstd = st2_t[si]
                    stb = st2_b[si]
                    rms_stats(tm[:], tmb, junk2, junk2b, ssq, rstd, stb)
                    k.op(DVE, lambda: nc.vector.scalar_tensor_tensor(tm[:], tm[:], rstd[:, 0:1], gfb_s[:], ALU.mult, ALU.mult),
                         reads=[stb, cb], writes=[tmb])
                    k.dma(SP, yout[ot * 128:(ot + 1) * 128, :], tm[:], tmb, reads=[tmb])
            k.barrier()
    except StopBuild:
        pass
    return nc


_NC = None


def _bf(a):
    return np.ascontiguousarray(a.astype(ml_dtypes.bfloat16))


def kernel(x, c, w_ada, b_ada, g_attn, w_in, b_f, sinks, w_out, g_mlp, w_up, conv_w, conv_b, w_down, g_final):
    global _NC
    f = lambda a: np.ascontiguousarray(np.asarray(a, dtype=np.float32))
    x, c, w_ada, b_ada, g_attn, w_in, b_f, sinks, w_out, g_mlp, w_up, conv_w, conv_b, w_down, g_final = map(
        f, (x, c, w_ada, b_ada, g_attn, w_in, b_f, sinks, w_out, g_mlp, w_up, conv_w, conv_b, w_down, g_final))
    if _NC is None:
        _NC = build_nc()
    nc = _NC
    kk = np.arange(128)[:, None]
    qq = np.arange(128)[None, :]
    cmask = np.where(kk <= qq, 0.0, NEGM).astype(np.float32)
    swab = np.zeros((128, 8, 2, 128), np.float32)
    for h in range(8):
        slope = 2.0 ** (-(h + 1))
        swab[:, h, 1, :] = np.where(kk <= qq, -slope * (qq - kk), NEGM)
        swab[:, h, 0, :] = np.where(kk > qq, -slope * (128 + qq - kk), NEGM)
    ident = np.eye(128, dtype=np.float32)
    tmaj = lambda v: f(v.reshape(8, 128).T)
    common = {
        "w_ada": w_ada, "b_ada": f(b_ada.reshape(1, -1)), "gat": tmaj(g_attn), "gmt": tmaj(g_mlp),
        "gfb": f(np.broadcast_to(g_final[None, :], (128, D))), "w_in": w_in, "bfc": f(b_f.reshape(8, 1)),
        "sinkb": f(np.broadcast_to(sinks[None, :], (128, 8))), "w_out": w_out, "w_up": w_up,
        "cwt": f(conv_w.T.reshape(44, 128, 3).transpose(1, 0, 2)), "cbt": f(conv_b.reshape(44, 128).T),
        "w_down": w_down, "cmask": _bf(cmask), "swab": _bf(swab), "ident": _bf(ident),
    }
    in_maps = []
    for core in range(8):
        b, p = core // 2, core % 2
        xa = x[b] if p == 1 else np.concatenate([x[b, :NOWN], x[b, :NOWN]], axis=0)
        m = dict(common)
        m["xa"] = f(xa)
        m["ct"] = tmaj(c[b])
        m["pm"] = np.full((128, 1), 0.0 if p == 1 else NEGM, np.float32)
        m["hf"] = np.full((128, 1), float(p), np.float32)
        in_maps.append(m)
    res = run_bass_kernel_spmd(nc, in_maps, core_ids=list(range(8)))
    if os.environ.get("MKDBG"):
        global _DBG
        _DBG = res.results
    out = np.empty((4, S, D), np.float32)
    for core in range(8):
        b, p = core // 2, core % 2
        out[b, p * NOWN:(p + 1) * NOWN] = res.results[core]["y"]
    return out
```

```python
import os
import numpy as np
import ml_dtypes
from contextlib import ExitStack
import concourse.bass as bass
import concourse.mybir as mybir
from concourse.bass_utils import run_bass_kernel_spmd

F32 = mybir.dt.float32
BF16 = mybir.dt.bfloat16
ALU = mybir.AluOpType
AF = mybir.ActivationFunctionType

D = 1024
S = 4096
NOWN = 2048
DFF = 2816
EPS = 1e-6
NEGM = -30000.0
NQ = 2176
Q0 = 1920


class Eng:
    def __init__(self, nc, name, h, es):
        self.name = name
        self.h = h
        self.sem = es.enter_context(nc.semaphore("s_" + name))
        self.cnt = 0
        self.seen = {}


class Buf:
    def __init__(self, name, excl=False):
        self.name = name
        self.excl = excl
        self.w = {}
        self.r = {}
        self.dsem = None
        self.dcnt = 0


def _upd(d, ev):
    s, v = ev
    if d.get(id(s), (None, 0))[1] < v:
        d[id(s)] = ev


class K:
    def __init__(self, nc, es):
        self.nc = nc
        self.es = es
        self.pe = Eng(nc, "pe", nc.tensor, es)
        self.act = Eng(nc, "act", nc.scalar, es)
        self.dve = Eng(nc, "dve", nc.vector, es)
        self.pool = Eng(nc, "pool", nc.gpsimd, es)
        self.sp = Eng(nc, "sp", nc.sync, es)
        self.engs = [self.pe, self.act, self.dve, self.pool, self.sp]
        self.dsems = []
        self.pend = []
        self.nbuf = 0

    def buf(self, name=None, excl=False):
        self.nbuf += 1
        return Buf(name or "b%d" % self.nbuf, excl)

    def _need(self, eng, reads, writes, pwrites=(), xreads=()):
        ev = {}
        for b in xreads:
            for e in b.w.values():
                _upd(ev, e)
        for b in reads:
            for e in b.w.values():
                _upd(ev, e)
            if b.excl:
                for e in b.r.values():
                    if e[0] is not eng.sem:
                        _upd(ev, e)
        for b in writes:
            for e in b.w.values():
                _upd(ev, e)
            for e in b.r.values():
                _upd(ev, e)
        for b in pwrites:
            for e in b.r.values():
                _upd(ev, e)
        for s, v in ev.values():
            if eng is self.pe and s is self.pe.sem:
                continue
            if eng.seen.get(id(s), 0) >= v:
                continue
            eng.h.wait_ge(s, v)
            eng.seen[id(s)] = v

    def _reg(self, ev, reads, writes, pwrites=()):
        for b in reads:
            _upd(b.r, ev)
        for b in writes:
            b.w = {}
            _upd(b.w, ev)
            b.r = {}
        for b in pwrites:
            _upd(b.w, ev)

    def op(self, eng, fn, reads=(), writes=(), pwrites=(), xreads=()):
        self._need(eng, reads, writes, pwrites, xreads)
        ins = fn()
        eng.cnt += 1
        ins.then_inc(eng.sem, 1)
        self._reg((eng.sem, eng.cnt), list(reads) + list(xreads), writes, pwrites)
        return ins

    def mm(self, out, lhsT, rhs, reads, writes, start=True, stop=True, inc=True, transpose=False):
        pe = self.pe
        self._need(pe, reads, writes)
        if transpose:
            ins = self.nc.tensor.transpose(out, lhsT, rhs)
        else:
            ins = self.nc.tensor.matmul(out, lhsT, rhs, start=start, stop=stop)
        self.pend.append((list(reads), list(writes)))
        if inc:
            pe.cnt += 1
            ins.then_inc(pe.sem, 1)
            for r_, w_ in self.pend:
                self._reg((pe.sem, pe.cnt), r_, w_)
            self.pend = []
        return ins

    def dma(self, q, out, in_, prim, reads=(), writes=(), pwrites=()):
        assert not (q is self.pe)
        if prim.dsem is None:
            prim.dsem = self.es.enter_context(self.nc.semaphore("d_%s" % prim.name))
            self.dsems.append(prim)
        self._need(q, reads, writes, pwrites)
        ins = q.h.dma_start(out=out, in_=in_)
        prim.dcnt += 16
        ins.then_inc(prim.dsem, 16)
        self._reg((prim.dsem, prim.dcnt), reads, writes, pwrites)
        return ins

    def barrier(self):
        assert not self.pend
        evs = [(e.sem, e.cnt) for e in self.engs if e.cnt > 0]
        evs += [(b.dsem, b.dcnt) for b in self.dsems if b.dcnt > 0]
        for e in self.engs:
            for s, v in evs:
                if s is e.sem:
                    continue
                if e.seen.get(id(s), 0) >= v:
                    continue
                e.h.wait_ge(s, v)
                e.seen[id(s)] = v


class StopBuild(Exception):
    pass


def build_nc():
    STOP = int(os.environ.get('MKSTOP', '99'))
    nc = bass.Bass("TRN2", target_bir_lowering=False)

    def din(name, shape, dt=F32):
        return nc.dram_tensor(name, list(shape), dt, kind="ExternalInput").ap()

    xa = din("xa", [S, D])
    ct = din("ct", [128, 8])
    w_ada = din("w_ada", [D, 6 * D])
    b_ada = din("b_ada", [1, 6 * D])
    gat = din("gat", [128, 8])
    gmt = din("gmt", [128, 8])
    gfb = din("gfb", [128, D])
    w_in = din("w_in", [D, 2312])
    bfc = din("bfc", [8, 1])
    sinkb = din("sinkb", [128, 8])
    w_out = din("w_out", [D, D])
    w_up = din("w_up", [D, 2 * DFF])
    cwt = din("cwt", [128, 44, 3])
    cbt = din("cbt", [128, 44])
    w_down = din("w_down", [DFF, D])
    pmd = din("pm", [128, 1])
    hfd = din("hf", [128, 1])
    cmaskd = din("cmask", [128, 128], BF16)
    swabd = din("swab", [128, 8, 2, 128], BF16)
    identd = din("ident", [128, 128], BF16)
    yout = nc.dram_tensor("y", [NOWN, D], F32, kind="ExternalOutput").ap()
    DBG = bool(os.environ.get("MKDBG"))
    if DBG:
        dbg_mod = nc.dram_tensor("dbg_mod", [1, 6 * D], F32, kind="ExternalOutput").ap()
        dbg_h = nc.dram_tensor("dbg_h", [128, 8, S], BF16, kind="ExternalOutput").ap()
        dbg_g = nc.dram_tensor("dbg_g", [8, 3, S], BF16, kind="ExternalOutput").ap()
        dbg_at = nc.dram_tensor("dbg_at", [128, 8, NQ], BF16, kind="ExternalOutput").ap()
        dbg_ka = nc.dram_tensor("dbg_ka", [2, 70, S], BF16, kind="ExternalOutput").ap()
        dbg_qa = nc.dram_tensor("dbg_qa", [2, 70, NQ], BF16, kind="ExternalOutput").ap()
        dbg_v = nc.dram_tensor("dbg_v", [128, 32, 2, 128], BF16, kind="ExternalOutput").ap()
        dbg_x1 = nc.dram_tensor("dbg_x1", [128, 5, D], F32, kind="ExternalOutput").ap()
        dbg_h2 = nc.dram_tensor("dbg_h2", [128, 8, 640], BF16, kind="ExternalOutput").ap()
        dbg_ht = nc.dram_tensor("dbg_ht", [128, 22, 512], BF16, kind="ExternalOutput").ap()

    w_in_r = w_in.rearrange("(k p) c -> p k c", p=128)
    w_up_r = w_up.rearrange("(k p) c -> p k c", p=128)
    wup_bf = nc.dram_tensor("wup_bf", [44, 128, 8, 128], BF16, kind="Internal").ap()

    try:
      with ExitStack() as es:
        k = K(nc, es)
        PE, ACT, DVE, POOL, SP = k.pe, k.act, k.dve, k.pool, k.sp

        def sb(name, shape, dt=F32, st=es):
            return st.enter_context(nc.sbuf_tensor("sb_" + name, list(shape), dt))

        ps_t = [es.enter_context(nc.psum_tensor("ps%d" % i, [128, 512], F32)) for i in range(8)]
        ps_b = [k.buf("ps%d" % i, excl=True) for i in range(8)]
        psi = [0]
        psti = [0]
        pst_pool = [list(range(8))]

        def nps():
            i = psi[0] % 4
            psi[0] += 1
            return ps_t[i], ps_b[i]

        pai = [0]

        def npa():
            i = 4 + pai[0] % 2
            pai[0] += 1
            return ps_t[i], ps_b[i]

        def npst():
            i = pst_pool[0][psti[0] % len(pst_pool[0])]
            psti[0] += 1
            return ps_t[i][:].bitcast(BF16), ps_b[i]

        cb = k.buf("consts")
        ident = sb("ident", [128, 128], BF16)
        cmask = sb("cmask", [128, 128], BF16)
        pm = sb("pm", [128, 1])
        hf = sb("hf", [128, 1])
        ct_s = sb("ct_s", [128, 8])
        gat_s = sb("gat_s", [128, 8])
        gmt_s = sb("gmt_s", [128, 8])
        gfb_s = sb("gfb_s", [128, D])
        bfc_s = sb("bfc_s", [8, 1])
        sink_s = sb("sink_s", [128, 8])
        cw_s = sb("cw_s", [128, 44, 3])
        cbb_s = sb("cbb_s", [128, 44])
        for dst, src in ((ident, identd), (cmask, cmaskd), (pm, pmd), (hf, hfd),
                         (ct_s, ct), (gat_s, gat), (gmt_s, gmt), (gfb_s, gfb), (bfc_s, bfc),
                         (sink_s, sinkb), (cw_s, cwt), (cbb_s, cbt)):
            k.dma(SP, dst[:], src[:], cb, writes=[cb])

        cc = k.buf("cc")
        ones32 = sb("ones32", [128, 128])
        mhalf = sb("mhalf", [128, 1])
        k.op(DVE, lambda: nc.vector.memset(ones32[:], 1.0), writes=[cc])
        k.op(DVE, lambda: nc.vector.memset(mhalf[:], -0.5), writes=[cc])
        modT = sb("modT", [128, 32])
        A1 = sb("A1", [128, 8])
        A2 = sb("A2", [128, 8])
        GA1b = sb("GA1b", [128, D])
        GA2b = sb("GA2b", [128, D])
        nbf = sb("nbf", [8, 1])
        es_s = sb("es_s", [128, 8])
        modb = k.buf("mod")
        AT = sb("AT", [128, 8, NQ], BF16)
        ATb = k.buf("AT")
        stgb = k.buf("stg")

        def rms_stats(xt, xb, junk, junkb, ssq, rstd, stb):
            k.op(ACT, lambda: nc.scalar.activation(junk[:], xt, AF.Square, accum_out=ssq[:]),
                 reads=[xb], writes=[junkb, stb])
            k.op(POOL, lambda: nc.gpsimd.tensor_scalar(rstd[:], ssq[:], 1.0 / D, EPS, ALU.mult, ALU.add),
                 reads=[stb], writes=[stb])
            k.op(POOL, lambda: nc.gpsimd.tensor_tensor(rstd[:], rstd[:], mhalf[:], ALU.pow),
                 reads=[stb, cc], writes=[stb])

        with ExitStack() as s1:
            swab = sb("swab", [128, 8, 2, 128], BF16, s1)
            esb = sb("esb", [128, 8, 128], F32, s1)
            k.dma(SP, swab[:], swabd[:], cb, writes=[cb])
            hNT = sb("hNT", [128, 8, S], BF16, s1)
            hN_b = [k.buf("hN%d" % i) for i in range(8)]
            s0 = ExitStack()
            if True:
                modrow = sb("modrow", [1, 6 * D], F32, s0)
                bada_s = sb("bada_s", [1, 6 * D], F32, s0)
                scb = k.buf("sc")
                sc = sb("sc", [128, 8], F32, s0)
                k.dma(SP, bada_s[:], b_ada[:], cb, writes=[cb])
                k.op(ACT, lambda: nc.scalar.activation(sc[:], ct_s[:], AF.Silu), reads=[cb], writes=[scb])
                wa_t = [sb("wa%d" % i, [128, 1536], F32, s0) for i in range(2)]
                wa_b = [k.buf("wa%d" % i) for i in range(2)]
                mrb = k.buf("modrow")
                adac = [0]

                def ada_step(c0, kk):
                    wt, wb = wa_t[adac[0] % 2], wa_b[adac[0] % 2]
                    adac[0] += 1
                    k.dma(SP, wt[:, 0:1536], w_ada[kk * 128:(kk + 1) * 128, c0:c0 + 1536], wb, writes=[wb])
                    for j in range(3):
                        k.mm(ps_t[j][0:1, :], sc[:, kk:kk + 1], wt[:, j * 512:(j + 1) * 512], [scb, wb], [ps_b[j]],
                             start=(kk == 0), stop=(kk == 7), inc=(j == 2))
                    if kk == 7:
                        for j in range(3):
                            cs = slice(c0 + j * 512, c0 + (j + 1) * 512)
                            k.op(DVE, lambda: nc.vector.tensor_tensor(modrow[0:1, cs], ps_t[j][0:1, :], bada_s[0:1, cs], ALU.add),
                                 reads=[ps_b[j], cb], pwrites=[mrb])

                for pz in range(2):
                    for kk in range(8):
                        ada_step(pz * 1536, kk)
                pt, pb = nps()
                for idx in range(16):
                    col0 = [0, 1024][idx // 8] + (idx % 8) * 128
                    k.mm(pt[:, idx:idx + 1], modrow[0:1, col0:col0 + 128], ones32[0:1, 0:1], [mrb, cc], [pb],
                         start=True, stop=True, inc=(idx == 15))
                k.op(DVE, lambda: nc.vector.tensor_copy(modT[:, 0:16], pt[:, 0:16]), reads=[pb], writes=[modb])
                k.op(DVE, lambda: nc.vector.tensor_scalar(A1[:], modT[:, 8:16], 1.0, None, ALU.add), reads=[modb], writes=[modb])
                k.op(DVE, lambda: nc.vector.tensor_tensor(A1[:], A1[:], gat_s[:], ALU.mult), reads=[modb, cb], writes=[modb])
                for hh in range(2):
                    pt, pb = nps()
                    k.mm(pt[:, :], ones32[0:1, 0:128], modrow[0:1, 2048 + hh * 512:2048 + (hh + 1) * 512], [mrb, cc], [pb])
                    k.op(DVE, lambda: nc.vector.tensor_copy(GA1b[:, hh * 512:(hh + 1) * 512], pt[:, :]), reads=[pb], writes=[modb])
                k.op(DVE, lambda: nc.vector.tensor_scalar(nbf[:], bfc_s[:], -1.0, None, ALU.mult), reads=[cb], writes=[modb])
                k.op(ACT, lambda: nc.scalar.activation(es_s[:], sink_s[:], AF.Exp), reads=[cb], writes=[modb])
                for h in range(8):
                    k.op(DVE, lambda: nc.vector.tensor_scalar(esb[:, h, :], ones32[:, :], es_s[:, h:h + 1], None, ALU.mult),
                         reads=[modb, cc], writes=[modb])
            SH1 = modT[:, 0:8]
            SH2 = modT[:, 16:24]

            pst_pool[0] = [3, 4, 5, 6]
            psti[0] = 0
            with ExitStack() as s1a:
                NX = 3
                x_t = [sb("x%d" % i, [128, D], F32, s1a) for i in range(NX)]
                x_b = [k.buf("x%d" % i) for i in range(NX)]
                xn_t = [sb("xn%d" % i, [128, D], BF16, s1a) for i in range(2)]
                xn_b = [k.buf("xn%d" % i) for i in range(2)]
                junk = sb("junk", [128, D], BF16, s1a)
                junkb = k.buf("junk")
                st_t = [(sb("ssq%d" % i, [128, 1], F32, s1a), sb("rstd%d" % i, [128, 1], F32, s1a)) for i in range(NX)]
                st_b = [k.buf("st%d" % i) for i in range(NX)]
                def p1_a(t):
                    xt, xb = x_t[t % NX], x_b[t % NX]
                    ssq, rstd = st_t[t % NX]
                    stb = st_b[t % NX]
                    xn, xnb = xn_t[t % 2], xn_b[t % 2]
                    k.dma(SP, xt[:], xa[t * 128:(t + 1) * 128, :], xb, writes=[xb])
                    rms_stats(xt[:], xb, junk, junkb, ssq, rstd, stb)
                    k.op(DVE, lambda: nc.vector.tensor_scalar(xn[:], xt[:], rstd[:, 0:1], None, ALU.mult),
                         reads=[xb, stb], writes=[xnb])
                    ptA, pbA = npst()
                    ptB, pbB = npst()
                    for c in range(8):
                        pt, pb = (ptA, pbA) if c % 2 == 0 else (ptB, pbB)
                        k.mm(pt[:, (c // 2) * 128:(c // 2 + 1) * 128], xn[:, c * 128:(c + 1) * 128], ident[:], [xnb, cb], [pb],
                             inc=(c >= 6), transpose=True)
                    return ptA, pbA, ptB, pbB

                def p1_b(t, ptA, pbA, ptB, pbB):
                    hb = hN_b[t // 4]
                    for c in range(8):
                        o = hNT[:, c, t * 128:(t + 1) * 128]
                        if c % 2 == 0:
                            i_ = ptA[:, (c // 2) * 128:(c // 2 + 1) * 128]
                            k.op(ACT, lambda: nc.scalar.activation(o, i_, AF.Identity, bias=SH1[:, c:c + 1], scale=A1[:, c:c + 1]),
                                 reads=[pbA, modb], pwrites=[hb])
                        else:
                            i_ = ptB[:, (c // 2) * 128:(c // 2 + 1) * 128]
                            k.op(DVE, lambda: nc.vector.tensor_scalar(o, i_, A1[:, c:c + 1], SH1[:, c:c + 1], ALU.mult, ALU.add),
                                 reads=[pbB, modb], pwrites=[hb])

                pend1 = {0: p1_a(0)}
                for t in range(32):
                    if t + 1 < 32:
                        pend1[t + 1] = p1_a(t + 1)
                    if t % 2 == 0:
                        ada_step(3072 + (t // 16) * 1536, (t // 2) % 8)
                    p1_b(t, *pend1.pop(t))
                pt, pb = nps()
                for idx in range(16):
                    col0 = [3072, 4096][idx // 8] + (idx % 8) * 128
                    k.mm(pt[:, idx:idx + 1], modrow[0:1, col0:col0 + 128], ones32[0:1, 0:1], [mrb, cc], [pb],
                         start=True, stop=True, inc=(idx == 15))
                k.op(DVE, lambda: nc.vector.tensor_copy(modT[:, 16:32], pt[:, 0:16]), reads=[pb], writes=[modb])
                k.op(DVE, lambda: nc.vector.tensor_scalar(A2[:], modT[:, 24:32], 1.0, None, ALU.add), reads=[modb], writes=[modb])
                k.op(DVE, lambda: nc.vector.tensor_tensor(A2[:], A2[:], gmt_s[:], ALU.mult), reads=[modb, cb], writes=[modb])
                for hh in range(2):
                    pt, pb = nps()
                    k.mm(pt[:, :], ones32[0:1, 0:128], modrow[0:1, 5120 + hh * 512:5120 + (hh + 1) * 512], [mrb, cc], [pb])
                    k.op(DVE, lambda: nc.vector.tensor_copy(GA2b[:, hh * 512:(hh + 1) * 512], pt[:, :]), reads=[pb], writes=[modb])
                if DBG:
                    k.dma(SP, dbg_mod[:], modrow[:], k.buf('dbg0'), reads=[mrb])
                if DBG:
                    k.dma(SP, dbg_h[:], hNT[:], k.buf('dbg1'), reads=hN_b)
                k.barrier()
            s0.close()

            if STOP <= 1:
                k.barrier()
                raise StopBuild()
            pst_pool[0] = [6, 7]
            psti[0] = 0
            Gs = sb("Gs", [8, 3, S], BF16, s1)
            Gsb = k.buf("Gs")
            with ExitStack() as s2a:
                wf = sb("wf", [128, 8, 8], BF16, s2a)
                wfb = k.buf("wf")
                k.dma(POOL, wf[:], w_in_r[:, :, 2304:2312], wfb, writes=[wfb])
                E1 = sb("E1", [8, S], F32, s2a)
                E2 = sb("E2", [8, S], F32, s2a)
                on8 = sb("on8", [8, S], F32, s2a)
                e1b, e2b, onb = k.buf("E1"), k.buf("E2"), k.buf("on8")
                k.op(POOL, lambda: nc.gpsimd.memset(on8[:], 1.0), writes=[onb])
                for n in range(8):
                    pt, pb = nps()
                    for kk in range(8):
                        k.mm(pt[0:8, :], wf[:, kk, :], hNT[:, kk, n * 512:(n + 1) * 512], [wfb, hN_b[n]], [pb],
                             start=(kk == 0), stop=(kk == 7), inc=(kk == 7))
                    k.op(ACT, lambda: nc.scalar.activation(E1[:, n * 512:(n + 1) * 512], pt[0:8, :], AF.Exp,
                                                           bias=nbf[:, 0:1], scale=-1.0),
                         reads=[pb, modb], writes=[e1b])
                k.op(ACT, lambda: nc.scalar.activation(E2[:], E1[:], AF.Ln, bias=1.0, scale=1.0), reads=[e1b], writes=[e2b])
                k.op(DVE, lambda: nc.vector.tensor_tensor_scan(E1[:], on8[:], E2[:], 0.0, ALU.mult, ALU.add),
                     reads=[onb, e2b], writes=[e1b])
                k.op(DVE, lambda: nc.vector.tensor_copy(Gs[:, 0, :], E1[:]), reads=[e1b], writes=[Gsb])
                k.op(DVE, lambda: nc.vector.tensor_tensor(E2[:], E1[:], Gs[:, 0, :], ALU.subtract), reads=[e1b, Gsb], writes=[e2b])
                k.op(DVE, lambda: nc.vector.tensor_copy(Gs[:, 1, :], E2[:]), reads=[e2b], writes=[Gsb])
                k.op(DVE, lambda: nc.vector.tensor_tensor(E1[:], E2[:], Gs[:, 1, :], ALU.subtract), reads=[e2b, Gsb], writes=[e1b])
                k.op(DVE, lambda: nc.vector.tensor_copy(Gs[:, 2, :], E1[:]), reads=[e1b], writes=[Gsb])
                if DBG:
                    k.dma(SP, dbg_g[:], Gs[:], k.buf('dbg2'), reads=[Gsb])
                k.barrier()

            PT_t = [sb("PT%d" % i, [128, 512], BF16, s1) for i in range(4)]
            PT_b = [k.buf("PT%d" % i) for i in range(4)]
            pti = [0]

            def npt():
                i = pti[0] % 4
                pti[0] += 1
                return PT_t[i], PT_b[i]
            den_t = [sb("den%d" % i, [64, 512], F32, s1) for i in range(2)]
            den_b = [k.buf("den%d" % i) for i in range(2)]
            deni = [0]
            wq_t = [sb("wq%d" % i, [128, 8, 128], BF16, s1) for i in range(2)]
            wk_t = [sb("wk%d" % i, [128, 8, 128], BF16, s1) for i in range(2)]
            wv_t = [sb("wv%d" % i, [128, 8, 128], BF16, s1) for i in range(2)]
            wq_b = [k.buf("wq%d" % i) for i in range(2)]
            wk_b = [k.buf("wk%d" % i) for i in range(2)]
            wv_b = [k.buf("wv%d" % i) for i in range(2)]

            QCH = [(0, 128)] + [(128 + 512 * i, 512) for i in range(4)]

            with ExitStack() as s2b:
                KA = [sb("KA%d" % i, [70, S], BF16, s2b) for i in range(2)]
                QA = [sb("QA%d" % i, [70, NQ], BF16, s2b) for i in range(2)]
                V = sb("V", [128, 32, 2, 128], BF16, s2b)
                KAb = [k.buf("KA%d" % i) for i in range(2)]
                QAb = [k.buf("QA%d" % i) for i in range(2)]
                Vb = k.buf("V")
                for i in range(2):
                    k.op(POOL, lambda: nc.gpsimd.memset(KA[i][64:70, :], -1.0), writes=[KAb[i]])
                    k.op(POOL, lambda: nc.gpsimd.memset(QA[i][64:70, :], 1.0), writes=[QAb[i]])
                k.op(POOL, lambda: nc.gpsimd.memset(V[:, :, :, 64:128], 1.0), writes=[Vb])

                def load_fox_w(g):
                    s = g % 2
                    k.dma(POOL, wq_t[s][:], w_in_r[:, :, 768 + 128 * g:768 + 128 * (g + 1)], wq_b[s], writes=[wq_b[s]])
                    k.dma(POOL, wk_t[s][:], w_in_r[:, :, 1280 + 128 * g:1280 + 128 * (g + 1)], wk_b[s], writes=[wk_b[s]])
                    k.dma(POOL, wv_t[s][:], w_in_r[:, :, 1792 + 128 * g:1792 + 128 * (g + 1)], wv_b[s], writes=[wv_b[s]])

                load_fox_w(0)
                for g in range(4):
                    s = g % 2
                    if g + 1 < 4:
                        load_fox_w(g + 1)
                    for sl_ in range(11 * g, 11 * (g + 1)):
                        k.dma(POOL, wup_bf[sl_], w_up_r[:, :, sl_ * 128:(sl_ + 1) * 128], stgb, pwrites=[stgb])
                    for i in range(2):
                        h = 2 * g + i
                        k.dma(SP, KA[i][64:67, :], Gs[h:h + 1, :, :], KAb[i], reads=[Gsb], writes=[KAb[i]])
                        k.dma(SP, QA[i][67:70, :], Gs[h:h + 1, :, Q0:S], QAb[i], reads=[Gsb], writes=[QAb[i]])
                    for n in range(8):
                        pt, pb = nps()
                        for kk in range(8):
                            k.mm(pt[:, :], wk_t[s][:, kk, :], hNT[:, kk, n * 512:(n + 1) * 512], [wk_b[s], hN_b[n]], [pb],
                                 start=(kk == 0), stop=(kk == 7), inc=(kk == 7))
                        k.op(ACT, lambda: nc.scalar.copy(KA[0][0:64, n * 512:(n + 1) * 512], pt[0:64, :]), xreads=[pb], writes=[KAb[0]])
                        k.op(DVE, lambda: nc.vector.tensor_copy(KA[1][0:64, n * 512:(n + 1) * 512], pt[64:128, :]), xreads=[pb], writes=[KAb[1]])
                    for (c0, n_) in QCH:
                        pt, pb = nps()
                        t0 = Q0 + c0
                        for kk in range(8):
                            k.mm(pt[:, 0:n_], wq_t[s][:, kk, :], hNT[:, kk, t0:t0 + n_], [wq_b[s], hN_b[t0 // 512]], [pb],
                                 start=(kk == 0), stop=(kk == 7), inc=(kk == 7))
                        k.op(ACT, lambda: nc.scalar.mul(QA[0][0:64, c0:c0 + n_], pt[0:64, 0:n_], 0.125), xreads=[pb], writes=[QAb[0]])
                        k.op(DVE, lambda: nc.vector.tensor_scalar(QA[1][0:64, c0:c0 + n_], pt[64:128, 0:n_], 0.125, None, ALU.mult),
                             xreads=[pb], writes=[QAb[1]])
                    for kb4 in range(8):
                        pt, pb = nps()
                        for j in range(4):
                            kb = kb4 * 4 + j
                            for kk in range(8):
                                k.mm(pt[:, j * 128:(j + 1) * 128], hNT[:, kk, kb * 128:(kb + 1) * 128], wv_t[s][:, kk, :],
                                     [wv_b[s], hN_b[kb // 4]], [pb], start=(kk == 0), stop=(kk == 7), inc=(kk == 7 and j == 3))
                        eng = ACT if kb4 % 2 == 0 else DVE
                        src = pt[:, :].rearrange("p (j h d) -> p j h d", j=4, h=2)
                        dst = V[:, kb4 * 4:(kb4 + 1) * 4, :, 0:64]
                        if eng is ACT:
                            k.op(ACT, lambda: nc.scalar.copy(dst, src), reads=[pb], writes=[Vb])
                        else:
                            k.op(DVE, lambda: nc.vector.tensor_copy(dst, src), reads=[pb], writes=[Vb])
                    items = []
                    for qi, (c0, n_) in enumerate(QCH):
                        d0 = 15 if qi == 0 else 16 + 4 * (qi - 1)
                        nkb = d0 + n_ // 128
                        for i in range(2):
                            for kb in range(nkb):
                                items.append((qi, c0, n_, d0, nkb, i, kb))
                    stq = {}
                    accs = {}

                    def emit_qk(idx):
                        qi, c0, n_, d0, nkb, i, kb = items[idx]
                        j = kb - d0
                        st, sb_ = nps()
                        lo = 0 if j < 0 else 128 * j
                        kcol = KA[i][0:70, kb * 128:(kb + 1) * 128]
                        if j >= 0:
                            k.mm(st[:, lo:lo + 128], kcol, QA[i][0:70, c0 + lo:c0 + lo + 128], [KAb[i], QAb[i]], [sb_],
                                 start=True, stop=False, inc=False)
                            k.mm(st[:, lo:lo + 128], ident[:], cmask[:], [cb], [sb_], start=False, stop=True,
                                 inc=(lo + 128 >= n_))
                            if lo + 128 < n_:
                                k.mm(st[:, lo + 128:n_], kcol, QA[i][0:70, c0 + lo + 128:c0 + n_], [KAb[i], QAb[i]], [sb_])
                        else:
                            k.mm(st[:, 0:n_], kcol, QA[i][0:70, c0:c0 + n_], [KAb[i], QAb[i]], [sb_])
                        stq[idx] = (st, sb_, lo)

                    def emit_rest(idx):
                        qi, c0, n_, d0, nkb, i, kb = items[idx]
                        h = 2 * g + i
                        st, sb_, lo = stq.pop(idx)
                        if kb == 0:
                            accs[(qi, i)] = npa()
                        at, ab = accs[(qi, i)]
                        pT, pTb = npt()
                        if qi >= 1 and kb < 16:
                            k.op(ACT, lambda: nc.scalar.activation(pT[:, lo:n_], st[:, lo:n_], AF.Exp, bias=pm[:, 0:1], scale=1.0),
                                 reads=[sb_, cb], writes=[pTb])
                        else:
                            k.op(ACT, lambda: nc.scalar.activation(pT[:, lo:n_], st[:, lo:n_], AF.Exp), reads=[sb_], writes=[pTb])
                        k.mm(at[:, lo:n_], V[:, kb, i, :], pT[:, lo:n_], [Vb, pTb], [ab],
                             start=(kb == 0), stop=(kb == nkb - 1), inc=(kb == nkb - 1))
                        if kb == nkb - 1:
                            dn, dnb = den_t[deni[0] % 2], den_b[deni[0] % 2]
                            deni[0] += 1
                            k.op(DVE, lambda: nc.vector.tensor_copy(dn[:, 0:n_], at[64:128, 0:n_]), reads=[ab], writes=[dnb])
                            k.op(DVE, lambda: nc.vector.reciprocal(dn[:, 0:n_], dn[:, 0:n_]), reads=[dnb], writes=[dnb])
                            o = AT[(h % 2) * 64:(h % 2) * 64 + 64, 4 + h // 2, c0:c0 + n_]
                            k.op(DVE, lambda: nc.vector.tensor_tensor(o, at[0:64, 0:n_], dn[:, 0:n_], ALU.mult),
                                 reads=[ab, dnb], pwrites=[ATb])

                    LOOK = 2
                    for idx in range(min(LOOK, len(items))):
                        emit_qk(idx)
                    for idx in range(len(items)):
                        if idx + LOOK < len(items):
                            emit_qk(idx + LOOK)
                        emit_rest(idx)
                if DBG:
                    for i in range(2):
                        k.dma(SP, dbg_ka[i], KA[i][:], k.buf('dbgk%d' % i), reads=[KAb[i]])
                        k.dma(SP, dbg_qa[i], QA[i][:], k.buf('dbgq%d' % i), reads=[QAb[i]])
                    k.dma(SP, dbg_v[:], V[:], k.buf('dbgv'), reads=[Vb])
                k.barrier()

            if STOP <= 2:
                k.barrier()
                raise StopBuild()
            with ExitStack() as s2c:
                KS = sb("KS", [64, 2304], BF16, s2c)
                QS = sb("QS", [64, 4, NQ], BF16, s2c)
                VS = sb("VS", [128, 18, 128], BF16, s2c)
                KSb, QSb, VSb = k.buf("KS"), k.buf("QS"), k.buf("VS")
                k.op(POOL, lambda: nc.gpsimd.memset(VS[:, :, 64:128], 1.0), writes=[VSb])
                dn4_t = [sb("dn4_%d" % i, [128, 4], F32, s2c) for i in range(2)]
                dn4_b = [k.buf("dn4_%d" % i) for i in range(2)]
                atk_t = [sb("atk%d" % i, [128, 256], BF16, s2c) for i in range(2)]
                atk_b = [k.buf("atk%d" % i) for i in range(2)]
                swc = [0]
                T0 = 1792
                for g in range(2):
                    k.dma(POOL, wq_t[0][:], w_in_r[:, :, 256 * g:256 * g + 128], wq_b[0], writes=[wq_b[0]])
                    k.dma(POOL, wq_t[1][:], w_in_r[:, :, 256 * g + 128:256 * g + 256], wq_b[1], writes=[wq_b[1]])
                    k.dma(POOL, wk_t[0][:, :, 0:64], w_in_r[:, :, 512 + 64 * g:512 + 64 * (g + 1)], wk_b[0], writes=[wk_b[0]])
                    k.dma(POOL, wv_t[0][:, :, 0:64], w_in_r[:, :, 640 + 64 * g:640 + 64 * (g + 1)], wv_b[0], writes=[wv_b[0]])
                    for (c0, n_) in [(0, 512), (512, 512), (1024, 512), (1536, 512), (2048, 256)]:
                        pt, pb = nps()
                        t0 = T0 + c0
                        for kk in range(8):
                            k.mm(pt[0:64, 0:n_], wk_t[0][:, kk, 0:64], hNT[:, kk, t0:t0 + n_], [wk_b[0], hN_b[t0 // 512]], [pb],
                                 start=(kk == 0), stop=(kk == 7), inc=(kk == 7))
                        k.op(ACT, lambda: nc.scalar.copy(KS[:, c0:c0 + n_], pt[0:64, 0:n_]), reads=[pb], writes=[KSb])
                    for s in range(2):
                        for (c0, n_) in QCH:
                            pt, pb = nps()
                            t0 = Q0 + c0
                            for kk in range(8):
                                k.mm(pt[:, 0:n_], wq_t[s][:, kk, :], hNT[:, kk, t0:t0 + n_], [wq_b[s], hN_b[t0 // 512]], [pb],
                                     start=(kk == 0), stop=(kk == 7), inc=(kk == 7))
                            k.op(ACT, lambda: nc.scalar.mul(QS[:, 2 * s, c0:c0 + n_], pt[0:64, 0:n_], 0.125), xreads=[pb], writes=[QSb])
                            k.op(DVE, lambda: nc.vector.tensor_scalar(QS[:, 2 * s + 1, c0:c0 + n_], pt[64:128, 0:n_], 0.125, None, ALU.mult),
                                 xreads=[pb], writes=[QSb])
                    for b0 in range(0, 18, 4):
                        nb = min(4, 18 - b0)
                        pt, pb = nps()
                        for j in range(nb):
                            kb = 14 + b0 + j
                            for kk in range(8):
                                k.mm(pt[:, j * 64:(j + 1) * 64], hNT[:, kk, kb * 128:(kb + 1) * 128], wv_t[0][:, kk, 0:64],
                                     [wv_b[0], hN_b[kb // 4]], [pb], start=(kk == 0), stop=(kk == 7), inc=(kk == 7 and j == nb - 1))
                        k.op(DVE, lambda: nc.vector.tensor_copy(VS[:, b0:b0 + nb, 0:64],
                                                                pt[:, 0:nb * 64].rearrange("p (j d) -> p j d", j=nb)),
                             reads=[pb], writes=[VSb])
                    sq = {}

                    def swa_qk(bi):
                        q0 = bi * 128
                        pts = []
                        for which in range(2):
                            st, sb_ = nps()
                            kcol = KS[:, (bi + which) * 128:(bi + which + 1) * 128]
                            k.mm(st[:, :].rearrange("p (a b) -> p a b", a=4), ident[:], swab[:, 4 * g:4 * g + 4, which, :], [cb], [sb_],
                                 start=True, stop=False, inc=False)
                            for m in range(4):
                                k.mm(st[:, m * 128:(m + 1) * 128], kcol, QS[:, m, q0:q0 + 128], [KSb, QSb], [sb_],
                                     start=False, stop=(m == 3), inc=(m == 3))
                            pts.append((st, sb_))
                        sq[bi] = pts

                    def swa_rest1(bi):
                        pts = []
                        for which, (st, sb_) in enumerate(sq.pop(bi)):
                            pT, pTb = npt()
                            if which == 0 and bi == 1:
                                k.op(ACT, lambda: nc.scalar.activation(pT[:, :], st[:, :], AF.Exp, bias=pm[:, 0:1], scale=1.0),
                                     reads=[sb_, cb], writes=[pTb])
                            else:
                                k.op(ACT, lambda: nc.scalar.activation(pT[:, :], st[:, :], AF.Exp), reads=[sb_], writes=[pTb])
                            pts.append((pT, pTb))
                        at, ab = npa()
                        for m in range(4):
                            for which in range(2):
                                pT, pTb = pts[which]
                                k.mm(at[:, m * 65:(m + 1) * 65], pT[:, m * 128:(m + 1) * 128], VS[:, bi + which, 0:65], [VSb, pTb], [ab],
                                     start=(which == 0), stop=(which == 1), inc=(m == 3 and which == 1))
                        c = swc[0]
                        swc[0] += 1
                        dn4, dn4b = dn4_t[c % 2], dn4_b[c % 2]
                        atk, atkb = atk_t[c % 2], atk_b[c % 2]
                        av = at[:, 0:260].rearrange("p (m c) -> p m c", c=65)
                        k.op(DVE, lambda: nc.vector.tensor_tensor(dn4[:, :], av[:, :, 64], es_s[:, 4 * g:4 * g + 4], ALU.add),
                             reads=[ab, modb], writes=[dn4b])
                        k.op(DVE, lambda: nc.vector.reciprocal(dn4[:, :], dn4[:, :]), reads=[dn4b], writes=[dn4b])
                        for m in range(4):
                            k.op(DVE, lambda: nc.vector.tensor_scalar(atk[:, m * 64:(m + 1) * 64], av[:, m, 0:64], dn4[:, m:m + 1], None, ALU.mult),
                                 reads=[ab, dn4b], writes=[atkb] if m == 0 else [], pwrites=[] if m == 0 else [atkb])
                        return atk, atkb

                    def swa_rest2(bi, atk, atkb):
                        q0 = bi * 128
                        pt, pb = npst()
                        for j in range(2):
                            k.mm(pt[:, j * 128:(j + 1) * 128], atk[:, j * 128:(j + 1) * 128], ident[:], [atkb, cb], [pb],
                                 inc=(j == 1), transpose=True)
                        k.op(ACT, lambda: nc.scalar.copy(AT[:, 2 * g:2 * g + 2, q0:q0 + 128], pt[:, 0:256].rearrange("p (j q) -> p j q", j=2)),
                             reads=[pb], pwrites=[ATb])

                    swa_qk(0)
                    prev = None
                    for bi in range(17):
                        if bi + 1 < 17:
                            swa_qk(bi + 1)
                        cur = swa_rest1(bi)
                        if prev is not None:
                            swa_rest2(bi - 1, *prev)
                        prev = cur
                    swa_rest2(16, *prev)
                if DBG:
                    k.dma(SP, dbg_at[:], AT[:], k.buf('dbg3'), reads=[ATb])
                k.barrier()
            k.barrier()

        if STOP <= 3:
            k.barrier()
            raise StopBuild()
        with ExitStack() as s3:
            wo = sb("wo", [128, 8, D], BF16, s3)
            wob = k.buf("wo")
            for kk in range(8):
                k.dma(POOL, wo[:, kk, :], w_out[kk * 128:(kk + 1) * 128, :], wob, pwrites=[wob])
            for kk in range(8):
                k.op(DVE, lambda: nc.vector.tensor_tensor(wo[:, kk, :], wo[:, kk, :], GA1b[:, :], ALU.mult),
                     reads=[modb], writes=[wob] if kk == 0 else [], pwrites=[] if kk == 0 else [wob])
            wd = sb("wd", [128, 22, D], BF16, s3)
            wdb = k.buf("wd")
            NT_MAX = 5
            x1 = sb("xres1", [128, NT_MAX, D], F32, s3)
            x1b = [k.buf("x1_%d" % i) for i in range(NT_MAX)]
            h2T = sb("h2T", [128, 8, NT_MAX * 128], BF16, s3)
            h2b = k.buf("h2T")
            hT = sb("hT", [128, 22, 512], BF16, s3)
            hTb = k.buf("hT")
            carry = sb("carry", [128, 44, 2], F32, s3)
            carb = k.buf("carry")
            U_t = [sb("U%d" % i, [128, 2 + NT_MAX * 128], F32, s3) for i in range(2)]
            U_b = [k.buf("U%d" % i) for i in range(2)]
            ya_t = [sb("ya%d" % i, [128, 512], F32, s3) for i in range(2)]
            ya_b = [k.buf("ya%d" % i) for i in range(2)]
            yg_t = [sb("yg%d" % i, [128, 512], F32, s3) for i in range(2)]
            yg_b = [k.buf("yg%d" % i) for i in range(2)]
            sg_t = [sb("sg%d" % i, [128, 512], F32, s3) for i in range(1)] * 2
            sg_b = [k.buf("sg%d" % i) for i in range(1)] * 2
            NWU = 4
            wu_t = [sb("wu%d" % i, [128, 8, 128], BF16, s3) for i in range(NWU)]
            wu_b = [k.buf("wu%d" % i) for i in range(NWU)]
            xr_t = [sb("xr%d" % i, [128, D], F32, s3) for i in range(2)]
            xr_b = [k.buf("xr%d" % i) for i in range(2)]
            tmp_t = [sb("tmp%d" % i, [128, D], F32, s3) for i in range(2)]
            tmp_b = [k.buf("tmp%d" % i) for i in range(2)]
            xn2_t = [sb("xn2_%d" % i, [128, D], BF16, s3) for i in range(2)]
            xn2_b = [k.buf("xn2_%d" % i) for i in range(2)]
            junk2 = sb("junk2", [128, D], BF16, s3)
            junk2b = k.buf("junk2")
            st2_t = [(sb("ssq2_%d" % i, [128, 1], F32, s3), sb("rstd2_%d" % i, [128, 1], F32, s3)) for i in range(4)]
            st2_b = [k.buf("st2_%d" % i) for i in range(4)]
            cnt = {"t": 0, "u": 0, "o": 0, "s": 0}

            TGS = [list(range(-1, 4)), list(range(4, 8)), list(range(8, 12)), list(range(12, 16))]
            slabs = [(ti, m, part) for ti in range(len(TGS)) for m in range(22) for part in range(2)]

            def issue_wu(n):
                if n >= len(slabs):
                    return
                _, m, part = slabs[n]
                k.dma(SP, wu_t[n % NWU][:], wup_bf[part * 22 + m], wu_b[n % NWU], reads=[stgb], writes=[wu_b[n % NWU]])

            PRE = 3
            for n in range(PRE):
                issue_wu(n)
            for m in range(22):
                k.dma(POOL, wd[:, m, :], w_down[m * 128:(m + 1) * 128, :], wdb, pwrites=[wdb])
            sl = 0
            for ti, tiles in enumerate(TGS):
                ntl = len(tiles)
                ntok = ntl * 128
                nown = ntok - (128 if ti == 0 else 0)
                def op_a(li, ot):
                    acol = (ot + 1) * 128
                    tloc = 15 + ot + 1
                    c = cnt["t"]
                    cnt["t"] += 1
                    xr, xrb = xr_t[c % 2], xr_b[c % 2]
                    k.dma(SP, xr[:], xa[tloc * 128:(tloc + 1) * 128, :], xrb, writes=[xrb])
                    pA, pAb = nps()
                    pB, pBb = nps()
                    for hh, (pp, ppb) in enumerate(((pA, pAb), (pB, pBb))):
                        for kk in range(8):
                            k.mm(pp[:, :], AT[:, kk, acol:acol + 128], wo[:, kk, hh * 512:(hh + 1) * 512], [ATb, wob], [ppb],
                                 start=(kk == 0), stop=(kk == 7), inc=(kk == 7))
                        k.op(DVE, lambda: nc.vector.tensor_tensor(x1[:, li, hh * 512:(hh + 1) * 512], pp[:, :], xr[:, hh * 512:(hh + 1) * 512], ALU.add),
                             reads=[ppb, xrb], writes=[x1b[li]] if hh == 0 else [], pwrites=[] if hh == 0 else [x1b[li]])
                    si = cnt["s"] % 4
                    cnt["s"] += 1
                    ssq, rstd = st2_t[si]
                    stb = st2_b[si]
                    xn, xnb = xn2_t[c % 2], xn2_b[c % 2]
                    rms_stats(x1[:, li, :], x1b[li], junk2, junk2b, ssq, rstd, stb)
                    k.op(DVE, lambda: nc.vector.tensor_scalar(xn[:], x1[:, li, :], rstd[:, 0:1], None, ALU.mult),
                         reads=[x1b[li], stb], writes=[xnb])
                    return xn, xnb

                def op_b(li, xn, xnb):
                    ptA, pbA = npst()
                    ptB, pbB = npst()
                    for cc_ in range(8):
                        pt, pb = (ptA, pbA) if cc_ % 2 == 0 else (ptB, pbB)
                        k.mm(pt[:, (cc_ // 2) * 128:(cc_ // 2 + 1) * 128], xn[:, cc_ * 128:(cc_ + 1) * 128], ident[:], [xnb, cb], [pb],
                             inc=(cc_ >= 6), transpose=True)
                    for cc_ in range(8):
                        o = h2T[:, cc_, li * 128:(li + 1) * 128]
                        if cc_ % 2 == 0:
                            i_ = ptA[:, (cc_ // 2) * 128:(cc_ // 2 + 1) * 128]
                            k.op(ACT, lambda: nc.scalar.activation(o, i_, AF.Identity, bias=SH2[:, cc_:cc_ + 1], scale=A2[:, cc_:cc_ + 1]),
                                 reads=[pbA, modb], pwrites=[h2b])
                        else:
                            i_ = ptB[:, (cc_ // 2) * 128:(cc_ // 2 + 1) * 128]
                            k.op(DVE, lambda: nc.vector.tensor_scalar(o, i_, A2[:, cc_:cc_ + 1], SH2[:, cc_:cc_ + 1], ALU.mult, ALU.add),
                                 reads=[pbB, modb], pwrites=[h2b])

                pendo = {0: op_a(0, tiles[0])}
                for li, ot in enumerate(tiles):
                    if li + 1 < ntl:
                        pendo[li + 1] = op_a(li + 1, tiles[li + 1])
                    op_b(li, *pendo.pop(li))
                off = 128 if ti == 0 else 0
                deferred = []
                for m in range(22):
                    for part in range(2):
                        mp = part * 22 + m
                        wu, wub = wu_t[sl % NWU], wu_b[sl % NWU]
                        issue_wu(sl + PRE)
                        sl += 1
                        U, Ub = U_t[cnt["u"] % 2], U_b[cnt["u"] % 2]
                        cnt["u"] += 1
                        if part == 0:
                            yv, yvb = ya_t[m % 2], ya_b[m % 2]
                        else:
                            yv, yvb = yg_t[m % 2], yg_b[m % 2]
                        first = True
                        for c0 in range(0, ntok, 512):
                            n_ = min(512, ntok - c0)
                            pt, pb = nps()
                            for kk in range(8):
                                k.mm(pt[:, 0:n_], wu[:, kk, :], h2T[:, kk, c0:c0 + n_], [wub, h2b], [pb],
                                     start=(kk == 0), stop=(kk == 7), inc=(kk == 7))
                            k.op(ACT, lambda: nc.scalar.copy(U[:, 2 + c0:2 + c0 + n_], pt[:, 0:n_]), reads=[pb],
                                 writes=[Ub] if first else [], pwrites=[] if first else [Ub])
                            lo = max(c0, off)
                            hi = c0 + n_
                            if hi > lo:
                                k.op(ACT, lambda: nc.scalar.activation(yv[:, lo - off:hi - off], pt[:, lo - c0:hi - c0], AF.Identity,
                                                                       bias=cbb_s[:, mp:mp + 1], scale=cw_s[:, mp, 2:3]),
                                     reads=[pb, cb], writes=[yvb] if first else [], pwrites=[] if first else [yvb])
                                first = False
                        if ti == 0:
                            k.op(ACT, lambda: nc.scalar.activation(U[:, 128:130], U[:, 128:130], AF.Copy, scale=hf[:, 0:1]),
                                 reads=[cb], writes=[Ub])
                        else:
                            k.op(ACT, lambda: nc.scalar.copy(U[:, 0:2], carry[:, mp, :]), reads=[carb], writes=[Ub])
                        k.op(ACT, lambda: nc.scalar.copy(carry[:, mp, :], U[:, ntok:ntok + 2]), reads=[Ub], writes=[carb])
                        b0 = 2 + off
                        k.op(DVE, lambda: nc.vector.scalar_tensor_tensor(yv[:, 0:nown], U[:, b0 - 1:b0 - 1 + nown], cw_s[:, mp, 1:2], yv[:, 0:nown], ALU.mult, ALU.add),
                             reads=[Ub, cb], writes=[yvb])
                        k.op(DVE, lambda: nc.vector.scalar_tensor_tensor(yv[:, 0:nown], U[:, b0 - 2:b0 - 2 + nown], cw_s[:, mp, 0:1], yv[:, 0:nown], ALU.mult, ALU.add),
                             reads=[Ub, cb], writes=[yvb])
                        if deferred:
                            deferred.pop()()
                        if part == 1:
                            def tail(m=m, yv=yv, yvb=yvb, nown=nown):
                                sg, sgb = sg_t[m % 2], sg_b[m % 2]
                                k.op(ACT, lambda: nc.scalar.activation(sg[:, 0:nown], yv[:, 0:nown], AF.Silu), reads=[yvb], writes=[sgb])
                                k.op(DVE, lambda: nc.vector.tensor_tensor(hT[:, m, 0:nown], sg[:, 0:nown], ya_t[m % 2][:, 0:nown], ALU.mult),
                                     reads=[sgb, ya_b[m % 2]], pwrites=[hTb])
                            deferred.append(tail)
                if deferred:
                    deferred.pop()()
                if DBG and ti == 0:
                    k.dma(SP, dbg_x1[:], x1[:], k.buf('dbg4'), reads=x1b)
                    k.dma(SP, dbg_h2[:], h2T[:], k.buf('dbg5'), reads=[h2b])
                    k.dma(SP, dbg_ht[:], hT[:], k.buf('dbg6'), reads=[hTb])
                if ti == 0:
                    for m in range(22):
                        k.op(DVE, lambda: nc.vector.tensor_tensor(wd[:, m, :], wd[:, m, :], GA2b[:, :], ALU.mult),
                             reads=[modb], writes=[wdb] if m == 0 else [], pwrites=[] if m == 0 else [wdb])
                for li, ot in enumerate(tiles):
                    if ot < 0:
                        continue
                    hc = (li - (1 if ti == 0 else 0)) * 128
                    c = cnt["o"]
                    cnt["o"] += 1
                    tm, tmb = tmp_t[c % 2], tmp_b[c % 2]
                    pA, pAb = nps()
                    pB, pBb = nps()
                    for hh, (pp, ppb) in enumerate(((pA, pAb), (pB, pBb))):
                        for m in range(22):
                            k.mm(pp[:, :], hT[:, m, hc:hc + 128], wd[:, m, hh * 512:(hh + 1) * 512], [hTb, wdb], [ppb],
                                 start=(m == 0), stop=(m == 21), inc=(m == 21))
                        k.op(DVE, lambda: nc.vector.tensor_tensor(tm[:, hh * 512:(hh + 1) * 512], pp[:, :], x1[:, li, hh * 512:(hh + 1) * 512], ALU.add),
                             reads=[ppb, x1b[li]], writes=[tmb] if hh == 0 else [], pwrites=[] if hh == 0 else [tmb])
                    si = cnt["s"] % 4
                    cnt["s"] += 1
                    ssq, rstd = st2_t[si]
                    stb = st2_b[si]
                    rms_stats(tm[:], tmb, junk2, junk2b, ssq, rstd, stb)
                    k.op(DVE, lambda: nc.vector.scalar_tensor_tensor(tm[:], tm[:], rstd[:, 0:1], gfb_s[:], ALU.mult, ALU.mult),
                         reads=[stb, cb], writes=[tmb])
                    k.dma(SP, yout[ot * 128:(ot + 1) * 128, :], tm[:], tmb, reads=[tmb])
            k.barrier()
    except StopBuild:
        pass
    return nc


_NC = None


def _bf(a):
    return np.ascontiguousarray(a.astype(ml_dtypes.bfloat16))


def kernel(x, c, w_ada, b_ada, g_attn, w_in, b_f, sinks, w_out, g_mlp, w_up, conv_w, conv_b, w_down, g_final):
    global _NC
    f = lambda a: np.ascontiguousarray(np.asarray(a, dtype=np.float32))
    x, c, w_ada, b_ada, g_attn, w_in, b_f, sinks, w_out, g_mlp, w_up, conv_w, conv_b, w_down, g_final = map(
        f, (x, c, w_ada, b_ada, g_attn, w_in, b_f, sinks, w_out, g_mlp, w_up, conv_w, conv_b, w_down, g_final))
    if _NC is None:
        _NC = build_nc()
    nc = _NC
    kk = np.arange(128)[:, None]
    qq = np.arange(128)[None, :]
    cmask = np.where(kk <= qq, 0.0, NEGM).astype(np.float32)
    swab = np.zeros((128, 8, 2, 128), np.float32)
    for h in range(8):
        slope = 2.0 ** (-(h + 1))
        swab[:, h, 1, :] = np.where(kk <= qq, -slope * (qq - kk), NEGM)
        swab[:, h, 0, :] = np.where(kk > qq, -slope * (128 + qq - kk), NEGM)
    ident = np.eye(128, dtype=np.float32)
    tmaj = lambda v: f(v.reshape(8, 128).T)
    common = {
        "w_ada": w_ada, "b_ada": f(b_ada.reshape(1, -1)), "gat": tmaj(g_attn), "gmt": tmaj(g_mlp),
        "gfb": f(np.broadcast_to(g_final[None, :], (128, D))), "w_in": w_in, "bfc": f(b_f.reshape(8, 1)),
        "sinkb": f(np.broadcast_to(sinks[None, :], (128, 8))), "w_out": w_out, "w_up": w_up,
        "cwt": f(conv_w.T.reshape(44, 128, 3).transpose(1, 0, 2)), "cbt": f(conv_b.reshape(44, 128).T),
        "w_down": w_down, "cmask": _bf(cmask), "swab": _bf(swab), "ident": _bf(ident),
    }
    in_maps = []
    for core in range(8):
        b, p = core // 2, core % 2
        xa = x[b] if p == 1 else np.concatenate([x[b, :NOWN], x[b, :NOWN]], axis=0)
        m = dict(common)
        m["xa"] = f(xa)
        m["ct"] = tmaj(c[b])
        m["pm"] = np.full((128, 1), 0.0 if p == 1 else NEGM, np.float32)
        m["hf"] = np.full((128, 1), float(p), np.float32)
        in_maps.append(m)
    res = run_bass_kernel_spmd(nc, in_maps, core_ids=list(range(8)))
    if os.environ.get("MKDBG"):
        global _DBG
        _DBG = res.results
    out = np.empty((4, S, D), np.float32)
    for core in range(8):
        b, p = core // 2, core % 2
        out[b, p * NOWN:(p + 1) * NOWN] = res.results[core]["y"]
    return out
```

```python
import os
import numpy as np
import ml_dtypes
from contextlib import ExitStack
import concourse.bass as bass
import concourse.mybir as mybir
from concourse.bass_utils import run_bass_kernel_spmd

F32 = mybir.dt.float32
BF16 = mybir.dt.bfloat16
ALU = mybir.AluOpType
AF = mybir.ActivationFunctionType

D = 1024
S = 4096
NOWN = 2048
DFF = 2816
EPS = 1e-6
NEGM = -30000.0
NQ = 2176
Q0 = 1920


class Eng:
    def __init__(self, nc, name, h, es):
        self.name = name
        self.h = h
        self.sem = es.enter_context(nc.semaphore("s_" + name))
        self.cnt = 0
        self.seen = {}


class Buf:
    def __init__(self, name, excl=False):
        self.name = name
        self.excl = excl
        self.w = {}
        self.r = {}
        self.dsem = None
        self.dcnt = 0


def _upd(d, ev):
    s, v = ev
    if d.get(id(s), (None, 0))[1] < v:
        d[id(s)] = ev


class K:
    def __init__(self, nc, es):
        self.nc = nc
        self.es = es
        self.pe = Eng(nc, "pe", nc.tensor, es)
        self.act = Eng(nc, "act", nc.scalar, es)
        self.dve = Eng(nc, "dve", nc.vector, es)
        self.pool = Eng(nc, "pool", nc.gpsimd, es)
        self.sp = Eng(nc, "sp", nc.sync, es)
        self.engs = [self.pe, self.act, self.dve, self.pool, self.sp]
        self.dsems = []
        self.pend = []
        self.nbuf = 0

    def buf(self, name=None, excl=False):
        self.nbuf += 1
        return Buf(name or "b%d" % self.nbuf, excl)

    def _need(self, eng, reads, writes, pwrites=(), xreads=()):
        ev = {}
        for b in xreads:
            for e in b.w.values():
                _upd(ev, e)
        for b in reads:
            for e in b.w.values():
                _upd(ev, e)
            if b.excl:
                for e in b.r.values():
                    if e[0] is not eng.sem:
                        _upd(ev, e)
        for b in writes:
            for e in b.w.values():
                _upd(ev, e)
            for e in b.r.values():
                _upd(ev, e)
        for b in pwrites:
            for e in b.r.values():
                _upd(ev, e)
        for s, v in ev.values():
            if eng is self.pe and s is self.pe.sem:
                continue
            if eng.seen.get(id(s), 0) >= v:
                continue
            eng.h.wait_ge(s, v)
            eng.seen[id(s)] = v

    def _reg(self, ev, reads, writes, pwrites=()):
        for b in reads:
            _upd(b.r, ev)
        for b in writes:
            b.w = {}
            _upd(b.w, ev)
            b.r = {}
        for b in pwrites:
            _upd(b.w, ev)

    def op(self, eng, fn, reads=(), writes=(), pwrites=(), xreads=()):
        self._need(eng, reads, writes, pwrites, xreads)
        ins = fn()
        eng.cnt += 1
        ins.then_inc(eng.sem, 1)
        self._reg((eng.sem, eng.cnt), list(reads) + list(xreads), writes, pwrites)
        return ins

    def mm(self, out, lhsT, rhs, reads, writes, start=True, stop=True, inc=True, transpose=False):
        pe = self.pe
        self._need(pe, reads, writes)
        if transpose:
            ins = self.nc.tensor.transpose(out, lhsT, rhs)
        else:
            ins = self.nc.tensor.matmul(out, lhsT, rhs, start=start, stop=stop)
        self.pend.append((list(reads), list(writes)))
        if inc:
            pe.cnt += 1
            ins.then_inc(pe.sem, 1)
            for r_, w_ in self.pend:
                self._reg((pe.sem, pe.cnt), r_, w_)
            self.pend = []
        return ins

    def dma(self, q, out, in_, prim, reads=(), writes=(), pwrites=()):
        assert not (q is self.pe)
        if prim.dsem is None:
            prim.dsem = self.es.enter_context(self.nc.semaphore("d_%s" % prim.name))
            self.dsems.append(prim)
        self._need(q, reads, writes, pwrites)
        ins = q.h.dma_start(out=out, in_=in_)
        prim.dcnt += 16
        ins.then_inc(prim.dsem, 16)
        self._reg((prim.dsem, prim.dcnt), reads, writes, pwrites)
        return ins

    def barrier(self):
        assert not self.pend
        evs = [(e.sem, e.cnt) for e in self.engs if e.cnt > 0]
        evs += [(b.dsem, b.dcnt) for b in self.dsems if b.dcnt > 0]
        for e in self.engs:
            for s, v in evs:
                if s is e.sem:
                    continue
                if e.seen.get(id(s), 0) >= v:
                    continue
                e.h.wait_ge(s, v)
                e.seen[id(s)] = v


class StopBuild(Exception):
    pass


def build_nc():
    STOP = int(os.environ.get('MKSTOP', '99'))
    nc = bass.Bass("TRN2", target_bir_lowering=False)

    def din(name, shape, dt=F32):
        return nc.dram_tensor(name, list(shape), dt, kind="ExternalInput").ap()

    xa = din("xa", [S, D])
    ct = din("ct", [128, 8])
    w_ada = din("w_ada", [D, 6 * D])
    b_ada = din("b_ada", [1, 6 * D])
    gat = din("gat", [128, 8])
    gmt = din("gmt", [128, 8])
    gfb = din("gfb", [128, D])
    w_in = din("w_in", [D, 2312])
    bfc = din("bfc", [8, 1])
    sinkb = din("sinkb", [128, 8])
    w_out = din("w_out", [D, D])
    w_up = din("w_up", [D, 2 * DFF])
    cwt = din("cwt", [128, 44, 3])
    cbt = din("cbt", [128, 44])
    w_down = din("w_down", [DFF, D])
    pmd = din("pm", [128, 1])
    hfd = din("hf", [128, 1])
    cmaskd = din("cmask", [128, 128], BF16)
    swabd = din("swab", [128, 8, 2, 128], BF16)
    identd = din("ident", [128, 128], BF16)
    yout = nc.dram_tensor("y", [NOWN, D], F32, kind="ExternalOutput").ap()
    DBG = bool(os.environ.get("MKDBG"))
    if DBG:
        dbg_mod = nc.dram_tensor("dbg_mod", [1, 6 * D], F32, kind="ExternalOutput").ap()
        dbg_h = nc.dram_tensor("dbg_h", [128, 8, S], BF16, kind="ExternalOutput").ap()
        dbg_g = nc.dram_tensor("dbg_g", [8, 3, S], BF16, kind="ExternalOutput").ap()
        dbg_at = nc.dram_tensor("dbg_at", [128, 8, NQ], BF16, kind="ExternalOutput").ap()
        dbg_ka = nc.dram_tensor("dbg_ka", [2, 70, S], BF16, kind="ExternalOutput").ap()
        dbg_qa = nc.dram_tensor("dbg_qa", [2, 70, NQ], BF16, kind="ExternalOutput").ap()
        dbg_v = nc.dram_tensor("dbg_v", [128, 32, 2, 128], BF16, kind="ExternalOutput").ap()
        dbg_x1 = nc.dram_tensor("dbg_x1", [128, 5, D], F32, kind="ExternalOutput").ap()
        dbg_h2 = nc.dram_tensor("dbg_h2", [128, 8, 640], BF16, kind="ExternalOutput").ap()
        dbg_ht = nc.dram_tensor("dbg_ht", [128, 22, 512], BF16, kind="ExternalOutput").ap()

    w_in_r = w_in.rearrange("(k p) c -> p k c", p=128)
    w_up_r = w_up.rearrange("(k p) c -> p k c", p=128)
    wup_bf = nc.dram_tensor("wup_bf", [44, 128, 8, 128], BF16, kind="Internal").ap()
    gs_d = nc.dram_tensor("gs_d", [8, 3, S], BF16, kind="Internal").ap()

    try:
      with ExitStack() as es:
        k = K(nc, es)
        PE, ACT, DVE, POOL, SP = k.pe, k.act, k.dve, k.pool, k.sp

        def sb(name, shape, dt=F32, st=es):
            return st.enter_context(nc.sbuf_tensor("sb_" + name, list(shape), dt))

        ps_t = [es.enter_context(nc.psum_tensor("ps%d" % i, [128, 512], F32)) for i in range(8)]
        ps_b = [k.buf("ps%d" % i, excl=True) for i in range(8)]
        psi = [0]
        psti = [0]
        pst_pool = [list(range(8))]

        def nps():
            i = psi[0] % 4
            psi[0] += 1
            return ps_t[i], ps_b[i]

        pai = [0]

        def npa():
            i = 4 + pai[0] % 2
            pai[0] += 1
            return ps_t[i], ps_b[i]

        def npst():
            i = pst_pool[0][psti[0] % len(pst_pool[0])]
            psti[0] += 1
            return ps_t[i][:].bitcast(BF16), ps_b[i]

        cb = k.buf("consts")
        ident = sb("ident", [128, 128], BF16)
        cmask = sb("cmask", [128, 128], BF16)
        pm = sb("pm", [128, 1])
        hf = sb("hf", [128, 1])
        ct_s = sb("ct_s", [128, 8])
        gat_s = sb("gat_s", [128, 8])
        gmt_s = sb("gmt_s", [128, 8])
        gfb_s = sb("gfb_s", [128, D])
        bfc_s = sb("bfc_s", [8, 1])
        sink_s = sb("sink_s", [128, 8])
        cw_s = sb("cw_s", [128, 44, 3])
        cbb_s = sb("cbb_s", [128, 44])
        for dst, src in ((ident, identd), (cmask, cmaskd), (pm, pmd), (hf, hfd),
                         (ct_s, ct), (gat_s, gat), (gmt_s, gmt), (gfb_s, gfb), (bfc_s, bfc),
                         (sink_s, sinkb), (cw_s, cwt), (cbb_s, cbt)):
            k.dma(SP, dst[:], src[:], cb, writes=[cb])

        cc = k.buf("cc")
        ones32 = sb("ones32", [128, 128])
        mhalf = sb("mhalf", [128, 1])
        k.op(DVE, lambda: nc.vector.memset(ones32[:], 1.0), writes=[cc])
        k.op(DVE, lambda: nc.vector.memset(mhalf[:], -0.5), writes=[cc])
        modT = sb("modT", [128, 32])
        A1 = sb("A1", [128, 8])
        A2 = sb("A2", [128, 8])
        GA1b = sb("GA1b", [128, D])
        GA2b = sb("GA2b", [128, D])
        nbf = sb("nbf", [8, 1])
        es_s = sb("es_s", [128, 8])
        modb = k.buf("mod")
        AT = sb("AT", [128, 8, NQ], BF16)
        ATb = k.buf("AT")
        stgb = k.buf("stg")

        def rms_stats(xt, xb, junk, junkb, ssq, rstd, stb):
            k.op(ACT, lambda: nc.scalar.activation(junk[:], xt, AF.Square, accum_out=ssq[:]),
                 reads=[xb], writes=[junkb, stb])
            k.op(POOL, lambda: nc.gpsimd.tensor_scalar(rstd[:], ssq[:], 1.0 / D, EPS, ALU.mult, ALU.add),
                 reads=[stb], writes=[stb])
            k.op(POOL, lambda: nc.gpsimd.tensor_tensor(rstd[:], rstd[:], mhalf[:], ALU.pow),
                 reads=[stb, cc], writes=[stb])

        with ExitStack() as s1:
            swab = sb("swab", [128, 8, 2, 128], BF16, s1)
            esb = sb("esb", [128, 8, 128], F32, s1)
            k.dma(SP, swab[:], swabd[:], cb, writes=[cb])
            hNT = sb("hNT", [128, 8, S], BF16, s1)
            hN_b = [k.buf("hN%d" % i) for i in range(8)]
            s0 = ExitStack()
            if True:
                modrow = sb("modrow", [1, 6 * D], F32, s0)
                bada_s = sb("bada_s", [1, 6 * D], F32, s0)
                scb = k.buf("sc")
                sc = sb("sc", [128, 8], F32, s0)
                k.dma(SP, bada_s[:], b_ada[:], cb, writes=[cb])
                k.op(ACT, lambda: nc.scalar.activation(sc[:], ct_s[:], AF.Silu), reads=[cb], writes=[scb])
                wa_t = [sb("wa%d" % i, [128, 1536], F32, s0) for i in range(2)]
                wa_b = [k.buf("wa%d" % i) for i in range(2)]
                mrb = k.buf("modrow")
                adac = [0]

                def ada_step(c0, kk):
                    wt, wb = wa_t[adac[0] % 2], wa_b[adac[0] % 2]
                    adac[0] += 1
                    k.dma(SP, wt[:, 0:1536], w_ada[kk * 128:(kk + 1) * 128, c0:c0 + 1536], wb, writes=[wb])
                    for j in range(3):
                        k.mm(ps_t[j][0:1, :], sc[:, kk:kk + 1], wt[:, j * 512:(j + 1) * 512], [scb, wb], [ps_b[j]],
                             start=(kk == 0), stop=(kk == 7), inc=(j == 2))
                    if kk == 7:
                        for j in range(3):
                            cs = slice(c0 + j * 512, c0 + (j + 1) * 512)
                            k.op(DVE, lambda: nc.vector.tensor_tensor(modrow[0:1, cs], ps_t[j][0:1, :], bada_s[0:1, cs], ALU.add),
                                 reads=[ps_b[j], cb], pwrites=[mrb])

                for pz in range(2):
                    for kk in range(8):
                        ada_step(pz * 1536, kk)
                pt, pb = nps()
                for idx in range(16):
                    col0 = [0, 1024][idx // 8] + (idx % 8) * 128
                    k.mm(pt[:, idx:idx + 1], modrow[0:1, col0:col0 + 128], ones32[0:1, 0:1], [mrb, cc], [pb],
                         start=True, stop=True, inc=(idx == 15))
                k.op(DVE, lambda: nc.vector.tensor_copy(modT[:, 0:16], pt[:, 0:16]), reads=[pb], writes=[modb])
                k.op(DVE, lambda: nc.vector.tensor_scalar(A1[:], modT[:, 8:16], 1.0, None, ALU.add), reads=[modb], writes=[modb])
                k.op(DVE, lambda: nc.vector.tensor_tensor(A1[:], A1[:], gat_s[:], ALU.mult), reads=[modb, cb], writes=[modb])
                for hh in range(2):
                    pt, pb = nps()
                    k.mm(pt[:, :], ones32[0:1, 0:128], modrow[0:1, 2048 + hh * 512:2048 + (hh + 1) * 512], [mrb, cc], [pb])
                    k.op(DVE, lambda: nc.vector.tensor_copy(GA1b[:, hh * 512:(hh + 1) * 512], pt[:, :]), reads=[pb], writes=[modb])
                k.op(DVE, lambda: nc.vector.tensor_scalar(nbf[:], bfc_s[:], -1.0, None, ALU.mult), reads=[cb], writes=[modb])
                k.op(ACT, lambda: nc.scalar.activation(es_s[:], sink_s[:], AF.Exp), reads=[cb], writes=[modb])
                for h in range(8):
                    k.op(DVE, lambda: nc.vector.tensor_scalar(esb[:, h, :], ones32[:, :], es_s[:, h:h + 1], None, ALU.mult),
                         reads=[modb, cc], writes=[modb])
            SH1 = modT[:, 0:8]
            SH2 = modT[:, 16:24]

            pst_pool[0] = [3, 4, 5, 6]
            psti[0] = 0
            with ExitStack() as s1a:
                NX = 3
                x_t = [sb("x%d" % i, [128, D], F32, s1a) for i in range(NX)]
                x_b = [k.buf("x%d" % i) for i in range(NX)]
                xn_t = [sb("xn%d" % i, [128, D], BF16, s1a) for i in range(2)]
                xn_b = [k.buf("xn%d" % i) for i in range(2)]
                junk = sb("junk", [128, D], BF16, s1a)
                junkb = k.buf("junk")
                st_t = [(sb("ssq%d" % i, [128, 1], F32, s1a), sb("rstd%d" % i, [128, 1], F32, s1a)) for i in range(NX)]
                st_b = [k.buf("st%d" % i) for i in range(NX)]
                def p1_a(t):
                    xt, xb = x_t[t % NX], x_b[t % NX]
                    ssq, rstd = st_t[t % NX]
                    stb = st_b[t % NX]
                    xn, xnb = xn_t[t % 2], xn_b[t % 2]
                    k.dma(SP, xt[:], xa[t * 128:(t + 1) * 128, :], xb, writes=[xb])
                    rms_stats(xt[:], xb, junk, junkb, ssq, rstd, stb)
                    k.op(DVE, lambda: nc.vector.tensor_scalar(xn[:], xt[:], rstd[:, 0:1], None, ALU.mult),
                         reads=[xb, stb], writes=[xnb])
                    ptA, pbA = npst()
                    ptB, pbB = npst()
                    for c in range(8):
                        pt, pb = (ptA, pbA) if c % 2 == 0 else (ptB, pbB)
                        k.mm(pt[:, (c // 2) * 128:(c // 2 + 1) * 128], xn[:, c * 128:(c + 1) * 128], ident[:], [xnb, cb], [pb],
                             inc=(c >= 6), transpose=True)
                    return ptA, pbA, ptB, pbB

                def p1_b(t, ptA, pbA, ptB, pbB):
                    hb = hN_b[t // 4]
                    for c in range(8):
                        o = hNT[:, c, t * 128:(t + 1) * 128]
                        if c % 2 == 0:
                            i_ = ptA[:, (c // 2) * 128:(c // 2 + 1) * 128]
                            k.op(ACT, lambda: nc.scalar.activation(o, i_, AF.Identity, bias=SH1[:, c:c + 1], scale=A1[:, c:c + 1]),
                                 reads=[pbA, modb], pwrites=[hb])
                        else:
                            i_ = ptB[:, (c // 2) * 128:(c // 2 + 1) * 128]
                            k.op(DVE, lambda: nc.vector.tensor_scalar(o, i_, A1[:, c:c + 1], SH1[:, c:c + 1], ALU.mult, ALU.add),
                                 reads=[pbB, modb], pwrites=[hb])

                pend1 = {0: p1_a(0)}
                for t in range(32):
                    if t + 1 < 32:
                        pend1[t + 1] = p1_a(t + 1)
                    if t % 2 == 0:
                        ada_step(3072 + (t // 16) * 1536, (t // 2) % 8)
                    p1_b(t, *pend1.pop(t))
                pt, pb = nps()
                for idx in range(16):
                    col0 = [3072, 4096][idx // 8] + (idx % 8) * 128
                    k.mm(pt[:, idx:idx + 1], modrow[0:1, col0:col0 + 128], ones32[0:1, 0:1], [mrb, cc], [pb],
                         start=True, stop=True, inc=(idx == 15))
                k.op(DVE, lambda: nc.vector.tensor_copy(modT[:, 16:32], pt[:, 0:16]), reads=[pb], writes=[modb])
                k.op(DVE, lambda: nc.vector.tensor_scalar(A2[:], modT[:, 24:32], 1.0, None, ALU.add), reads=[modb], writes=[modb])
                k.op(DVE, lambda: nc.vector.tensor_tensor(A2[:], A2[:], gmt_s[:], ALU.mult), reads=[modb, cb], writes=[modb])
                for hh in range(2):
                    pt, pb = nps()
                    k.mm(pt[:, :], ones32[0:1, 0:128], modrow[0:1, 5120 + hh * 512:5120 + (hh + 1) * 512], [mrb, cc], [pb])
                    k.op(DVE, lambda: nc.vector.tensor_copy(GA2b[:, hh * 512:(hh + 1) * 512], pt[:, :]), reads=[pb], writes=[modb])
                if DBG:
                    k.dma(SP, dbg_mod[:], modrow[:], k.buf('dbg0'), reads=[mrb])
                if DBG:
                    k.dma(SP, dbg_h[:], hNT[:], k.buf('dbg1'), reads=hN_b)
                k.barrier()
            s0.close()

            if STOP <= 1:
                k.barrier()
                raise StopBuild()
            pst_pool[0] = [6, 7]
            psti[0] = 0
            Gsb = k.buf("Gs")
            PT_t = [sb("PT%d" % i, [128, 512], BF16, s1) for i in range(4)]
            PT_b = [k.buf("PT%d" % i) for i in range(4)]
            pti = [0]

            def npt():
                i = pti[0] % 4
                pti[0] += 1
                return PT_t[i], PT_b[i]
            den_t = [sb("den%d" % i, [64, 512], F32, s1) for i in range(2)]
            den_b = [k.buf("den%d" % i) for i in range(2)]
            deni = [0]
            wq_t = [sb("wq%d" % i, [128, 8, 128], BF16, s1) for i in range(2)]
            wk_t = [sb("wk%d" % i, [128, 8, 128], BF16, s1) for i in range(2)]
            wv_t = [sb("wv%d" % i, [128, 8, 128], BF16, s1) for i in range(2)]
            wq_b = [k.buf("wq%d" % i) for i in range(2)]
            wk_b = [k.buf("wk%d" % i) for i in range(2)]
            wv_b = [k.buf("wv%d" % i) for i in range(2)]

            QCH = [(0, 128)] + [(128 + 512 * i, 512) for i in range(4)]

            with ExitStack() as s2b:
                KA = [sb("KA%d" % i, [70, S], BF16, s2b) for i in range(2)]
                QA = [sb("QA%d" % i, [70, NQ], BF16, s2b) for i in range(2)]
                V = sb("V", [128, 32, 2, 128], BF16, s2b)
                KAb = [k.buf("KA%d" % i) for i in range(2)]
                QAb = [k.buf("QA%d" % i) for i in range(2)]
                Vb = k.buf("V")
                for i in range(2):
                    k.op(POOL, lambda: nc.gpsimd.memset(KA[i][64:70, :], -1.0), writes=[KAb[i]])
                    k.op(POOL, lambda: nc.gpsimd.memset(QA[i][64:70, :], 1.0), writes=[QAb[i]])
                k.op(POOL, lambda: nc.gpsimd.memset(V[:, :, :, 64:128], 1.0), writes=[Vb])

                def load_fox_w(g):
                    s = g % 2
                    k.dma(POOL, wq_t[s][:], w_in_r[:, :, 768 + 128 * g:768 + 128 * (g + 1)], wq_b[s], writes=[wq_b[s]])
                    k.dma(POOL, wk_t[s][:], w_in_r[:, :, 1280 + 128 * g:1280 + 128 * (g + 1)], wk_b[s], writes=[wk_b[s]])
                    k.dma(POOL, wv_t[s][:], w_in_r[:, :, 1792 + 128 * g:1792 + 128 * (g + 1)], wv_b[s], writes=[wv_b[s]])

                wf = sb("wf", [128, 8, 8], BF16, s2b)
                wfb = k.buf("wf")
                k.dma(POOL, wf[:], w_in_r[:, :, 2304:2312], wfb, writes=[wfb])
                Et = sb("Et", [8, 512], F32, s2b)
                SPt = sb("SPt", [8, 512], F32, s2b)
                G32 = [sb("G32_%d" % i, [8, 512], F32, s2b) for i in range(2)]
                R1 = sb("R1", [8, 512], F32, s2b)
                R2 = sb("R2", [8, 512], F32, s2b)
                Gc = [sb("Gc%d" % i, [8, 3, 512], BF16, s2b) for i in range(2)]
                on8 = sb("on8", [8, 512], F32, s2b)
                etb, spb, r1b, r2b, onb = k.buf("Et"), k.buf("SPt"), k.buf("R1"), k.buf("R2"), k.buf("on8")
                g32b = [k.buf("G32_%d" % i) for i in range(2)]
                gcb = [k.buf("Gc%d" % i) for i in range(2)]
                k.op(POOL, lambda: nc.gpsimd.memset(on8[:], 1.0), writes=[onb])

                def f_chunk(n):
                    cs = slice(n * 512, (n + 1) * 512)
                    pt, pb = nps()
                    for kk in range(8):
                        k.mm(pt[0:8, :], wf[:, kk, :], hNT[:, kk, cs], [wfb, hN_b[n]], [pb],
                             start=(kk == 0), stop=(kk == 7), inc=(kk == 7))
                    k.op(ACT, lambda: nc.scalar.activation(Et[:], pt[0:8, :], AF.Exp, bias=nbf[:, 0:1], scale=-1.0),
                         reads=[pb, modb], writes=[etb])
                    k.op(ACT, lambda: nc.scalar.activation(SPt[:], Et[:], AF.Ln, bias=1.0, scale=1.0), reads=[etb], writes=[spb])
                    gcur, gcurb = G32[n % 2], g32b[n % 2]
                    if n == 0:
                        k.op(DVE, lambda: nc.vector.tensor_tensor_scan(gcur[:], on8[:], SPt[:], 0.0, ALU.mult, ALU.add),
                             reads=[onb, spb], writes=[gcurb])
                    else:
                        gprev, gprevb = G32[(n - 1) % 2], g32b[(n - 1) % 2]
                        k.op(DVE, lambda: nc.vector.tensor_tensor_scan(gcur[:], on8[:], SPt[:], gprev[:, 511:512], ALU.mult, ALU.add),
                             reads=[onb, spb, gprevb], writes=[gcurb])
                    gc, gcbb = Gc[n % 2], gcb[n % 2]
                    k.op(DVE, lambda: nc.vector.tensor_copy(gc[:, 0, :], gcur[:]), reads=[gcurb], writes=[gcbb])
                    k.op(DVE, lambda: nc.vector.tensor_tensor(R1[:], gcur[:], gc[:, 0, :], ALU.subtract), reads=[gcurb, gcbb], writes=[r1b])
                    k.op(DVE, lambda: nc.vector.tensor_copy(gc[:, 1, :], R1[:]), reads=[r1b], pwrites=[gcbb])
                    k.op(DVE, lambda: nc.vector.tensor_tensor(R2[:], R1[:], gc[:, 1, :], ALU.subtract), reads=[r1b, gcbb], writes=[r2b])
                    k.op(DVE, lambda: nc.vector.tensor_copy(gc[:, 2, :], R2[:]), reads=[r2b], pwrites=[gcbb])
                    k.dma(SP, gs_d[:, :, cs], gc[:], gcbb, reads=[gcbb], pwrites=[Gsb])

                load_fox_w(0)
                for g in range(4):
                    s = g % 2
                    if g + 1 < 4:
                        load_fox_w(g + 1)
                    for sl_ in range(11 * g, 11 * (g + 1)):
                        k.dma(POOL, wup_bf[sl_], w_up_r[:, :, sl_ * 128:(sl_ + 1) * 128], stgb, pwrites=[stgb])
                    def aug_rows():
                        for i in range(2):
                            h = 2 * g + i
                            k.dma(SP, KA[i][64:67, :], gs_d[h], KAb[i], reads=[Gsb], writes=[KAb[i]])
                            k.dma(SP, QA[i][67:70, :], gs_d[h, :, Q0:S], QAb[i], reads=[Gsb], writes=[QAb[i]])
                    if g > 0:
                        aug_rows()
                    for n in range(8):
                        if g == 0:
                            f_chunk(n)
                        pt, pb = nps()
                        for kk in range(8):
                            k.mm(pt[:, :], wk_t[s][:, kk, :], hNT[:, kk, n * 512:(n + 1) * 512], [wk_b[s], hN_b[n]], [pb],
                                 start=(kk == 0), stop=(kk == 7), inc=(kk == 7))
                        k.op(ACT, lambda: nc.scalar.copy(KA[0][0:64, n * 512:(n + 1) * 512], pt[0:64, :]), xreads=[pb], writes=[KAb[0]])
                        k.op(DVE, lambda: nc.vector.tensor_copy(KA[1][0:64, n * 512:(n + 1) * 512], pt[64:128, :]), xreads=[pb], writes=[KAb[1]])
                    if g == 0:
                        aug_rows()
                    for (c0, n_) in QCH:
                        pt, pb = nps()
                        t0 = Q0 + c0
                        for kk in range(8):
                            k.mm(pt[:, 0:n_], wq_t[s][:, kk, :], hNT[:, kk, t0:t0 + n_], [wq_b[s], hN_b[t0 // 512]], [pb],
                                 start=(kk == 0), stop=(kk == 7), inc=(kk == 7))
                        k.op(ACT, lambda: nc.scalar.mul(QA[0][0:64, c0:c0 + n_], pt[0:64, 0:n_], 0.125), xreads=[pb], writes=[QAb[0]])
                        k.op(DVE, lambda: nc.vector.tensor_scalar(QA[1][0:64, c0:c0 + n_], pt[64:128, 0:n_], 0.125, None, ALU.mult),
                             xreads=[pb], writes=[QAb[1]])
                    for kb4 in range(8):
                        pt, pb = nps()
                        for j in range(4):
                            kb = kb4 * 4 + j
                            for kk in range(8):
                                k.mm(pt[:, j * 128:(j + 1) * 128], hNT[:, kk, kb * 128:(kb + 1) * 128], wv_t[s][:, kk, :],
                                     [wv_b[s], hN_b[kb // 4]], [pb], start=(kk == 0), stop=(kk == 7), inc=(kk == 7 and j == 3))
                        eng = ACT if kb4 % 2 == 0 else DVE
                        src = pt[:, :].rearrange("p (j h d) -> p j h d", j=4, h=2)
                        dst = V[:, kb4 * 4:(kb4 + 1) * 4, :, 0:64]
                        if eng is ACT:
                            k.op(ACT, lambda: nc.scalar.copy(dst, src), reads=[pb], writes=[Vb])
                        else:
                            k.op(DVE, lambda: nc.vector.tensor_copy(dst, src), reads=[pb], writes=[Vb])
                    items = []
                    for qi, (c0, n_) in enumerate(QCH):
                        d0 = 15 if qi == 0 else 16 + 4 * (qi - 1)
                        nkb = d0 + n_ // 128
                        for i in range(2):
                            for kb in range(nkb):
                                items.append((qi, c0, n_, d0, nkb, i, kb))
                    stq = {}
                    accs = {}

                    def emit_qk(idx):
                        qi, c0, n_, d0, nkb, i, kb = items[idx]
                        j = kb - d0
                        st, sb_ = nps()
                        lo = 0 if j < 0 else 128 * j
                        kcol = KA[i][0:70, kb * 128:(kb + 1) * 128]
                        if j >= 0:
                            k.mm(st[:, lo:lo + 128], kcol, QA[i][0:70, c0 + lo:c0 + lo + 128], [KAb[i], QAb[i]], [sb_],
                                 start=True, stop=False, inc=False)
                            k.mm(st[:, lo:lo + 128], ident[:], cmask[:], [cb], [sb_], start=False, stop=True,
                                 inc=(lo + 128 >= n_))
                            if lo + 128 < n_:
                                k.mm(st[:, lo + 128:n_], kcol, QA[i][0:70, c0 + lo + 128:c0 + n_], [KAb[i], QAb[i]], [sb_])
                        else:
                            k.mm(st[:, 0:n_], kcol, QA[i][0:70, c0:c0 + n_], [KAb[i], QAb[i]], [sb_])
                        stq[idx] = (st, sb_, lo)

                    def emit_rest(idx):
                        qi, c0, n_, d0, nkb, i, kb = items[idx]
                        h = 2 * g + i
                        st, sb_, lo = stq.pop(idx)
                        if kb == 0:
                            accs[(qi, i)] = npa()
                        at, ab = accs[(qi, i)]
                        pT, pTb = npt()
                        if qi >= 1 and kb < 16:
                            k.op(ACT, lambda: nc.scalar.activation(pT[:, lo:n_], st[:, lo:n_], AF.Exp, bias=pm[:, 0:1], scale=1.0),
                                 reads=[sb_, cb], writes=[pTb])
                        else:
                            k.op(ACT, lambda: nc.scalar.activation(pT[:, lo:n_], st[:, lo:n_], AF.Exp), reads=[sb_], writes=[pTb])
                        k.mm(at[:, lo:n_], V[:, kb, i, :], pT[:, lo:n_], [Vb, pTb], [ab],
                             start=(kb == 0), stop=(kb == nkb - 1), inc=(kb == nkb - 1))
                        if kb == nkb - 1:
                            dn, dnb = den_t[deni[0] % 2], den_b[deni[0] % 2]
                            deni[0] += 1
                            k.op(DVE, lambda: nc.vector.tensor_copy(dn[:, 0:n_], at[64:128, 0:n_]), reads=[ab], writes=[dnb])
                            k.op(DVE, lambda: nc.vector.reciprocal(dn[:, 0:n_], dn[:, 0:n_]), reads=[dnb], writes=[dnb])
                            o = AT[(h % 2) * 64:(h % 2) * 64 + 64, 4 + h // 2, c0:c0 + n_]
                            k.op(DVE, lambda: nc.vector.tensor_tensor(o, at[0:64, 0:n_], dn[:, 0:n_], ALU.mult),
                                 reads=[ab, dnb], pwrites=[ATb])

                    LOOK = 2
                    for idx in range(min(LOOK, len(items))):
                        emit_qk(idx)
                    for idx in range(len(items)):
                        if idx + LOOK < len(items):
                            emit_qk(idx + LOOK)
                        emit_rest(idx)
                if DBG:
                    for i in range(2):
                        k.dma(SP, dbg_ka[i], KA[i][:], k.buf('dbgk%d' % i), reads=[KAb[i]])
                        k.dma(SP, dbg_qa[i], QA[i][:], k.buf('dbgq%d' % i), reads=[QAb[i]])
                    k.dma(SP, dbg_v[:], V[:], k.buf('dbgv'), reads=[Vb])
                    k.dma(SP, dbg_g[:], gs_d[:], k.buf('dbg2'), reads=[Gsb])
                k.barrier()

            if STOP <= 2:
                k.barrier()
                raise StopBuild()
            with ExitStack() as s2c:
                KS = sb("KS", [64, 2304], BF16, s2c)
                QS = sb("QS", [64, 4, NQ], BF16, s2c)
                VS = sb("VS", [128, 18, 128], BF16, s2c)
                KSb, QSb, VSb = k.buf("KS"), k.buf("QS"), k.buf("VS")
                k.op(POOL, lambda: nc.gpsimd.memset(VS[:, :, 64:128], 1.0), writes=[VSb])
                dn4_t = [sb("dn4_%d" % i, [128, 4], F32, s2c) for i in range(2)]
                dn4_b = [k.buf("dn4_%d" % i) for i in range(2)]
                atk_t = [sb("atk%d" % i, [128, 256], BF16, s2c) for i in range(2)]
                atk_b = [k.buf("atk%d" % i) for i in range(2)]
                swc = [0]
                T0 = 1792
                for g in range(2):
                    k.dma(POOL, wq_t[0][:], w_in_r[:, :, 256 * g:256 * g + 128], wq_b[0], writes=[wq_b[0]])
                    k.dma(POOL, wq_t[1][:], w_in_r[:, :, 256 * g + 128:256 * g + 256], wq_b[1], writes=[wq_b[1]])
                    k.dma(POOL, wk_t[0][:, :, 0:64], w_in_r[:, :, 512 + 64 * g:512 + 64 * (g + 1)], wk_b[0], writes=[wk_b[0]])
                    k.dma(POOL, wv_t[0][:, :, 0:64], w_in_r[:, :, 640 + 64 * g:640 + 64 * (g + 1)], wv_b[0], writes=[wv_b[0]])
                    for (c0, n_) in [(0, 512), (512, 512), (1024, 512), (1536, 512), (2048, 256)]:
                        pt, pb = nps()
                        t0 = T0 + c0
                        for kk in range(8):
                            k.mm(pt[0:64, 0:n_], wk_t[0][:, kk, 0:64], hNT[:, kk, t0:t0 + n_], [wk_b[0], hN_b[t0 // 512]], [pb],
                                 start=(kk == 0), stop=(kk == 7), inc=(kk == 7))
                        k.op(ACT, lambda: nc.scalar.copy(KS[:, c0:c0 + n_], pt[0:64, 0:n_]), reads=[pb], writes=[KSb])
                    for s in range(2):
                        for (c0, n_) in QCH:
                            pt, pb = nps()
                            t0 = Q0 + c0
                            for kk in range(8):
                                k.mm(pt[:, 0:n_], wq_t[s][:, kk, :], hNT[:, kk, t0:t0 + n_], [wq_b[s], hN_b[t0 // 512]], [pb],
                                     start=(kk == 0), stop=(kk == 7), inc=(kk == 7))
                            k.op(ACT, lambda: nc.scalar.mul(QS[:, 2 * s, c0:c0 + n_], pt[0:64, 0:n_], 0.125), xreads=[pb], writes=[QSb])
                            k.op(DVE, lambda: nc.vector.tensor_scalar(QS[:, 2 * s + 1, c0:c0 + n_], pt[64:128, 0:n_], 0.125, None, ALU.mult),
                                 xreads=[pb], writes=[QSb])
                    for b0 in range(0, 18, 4):
                        nb = min(4, 18 - b0)
                        pt, pb = nps()
                        for j in range(nb):
                            kb = 14 + b0 + j
                            for kk in range(8):
                                k.mm(pt[:, j * 64:(j + 1) * 64], hNT[:, kk, kb * 128:(kb + 1) * 128], wv_t[0][:, kk, 0:64],
                                     [wv_b[0], hN_b[kb // 4]], [pb], start=(kk == 0), stop=(kk == 7), inc=(kk == 7 and j == nb - 1))
                        k.op(DVE, lambda: nc.vector.tensor_copy(VS[:, b0:b0 + nb, 0:64],
                                                                pt[:, 0:nb * 64].rearrange("p (j d) -> p j d", j=nb)),
                             reads=[pb], writes=[VSb])
                    sq = {}

                    def swa_qk(bi):
                        q0 = bi * 128
                        pts = []
                        for which in range(2):
                            st, sb_ = nps()
                            kcol = KS[:, (bi + which) * 128:(bi + which + 1) * 128]
                            k.mm(st[:, :].rearrange("p (a b) -> p a b", a=4), ident[:], swab[:, 4 * g:4 * g + 4, which, :], [cb], [sb_],
                                 start=True, stop=False, inc=False)
                            for m in range(4):
                                k.mm(st[:, m * 128:(m + 1) * 128], kcol, QS[:, m, q0:q0 + 128], [KSb, QSb], [sb_],
                                     start=False, stop=(m == 3), inc=(m == 3))
                            pts.append((st, sb_))
                        sq[bi] = pts

                    def swa_rest1(bi):
                        pts = []
                        for which, (st, sb_) in enumerate(sq.pop(bi)):
                            pT, pTb = npt()
                            if which == 0 and bi == 1:
                                k.op(ACT, lambda: nc.scalar.activation(pT[:, :], st[:, :], AF.Exp, bias=pm[:, 0:1], scale=1.0),
                                     reads=[sb_, cb], writes=[pTb])
                            else:
                                k.op(ACT, lambda: nc.scalar.activation(pT[:, :], st[:, :], AF.Exp), reads=[sb_], writes=[pTb])
                            pts.append((pT, pTb))
                        at, ab = npa()
                        for m in range(4):
                            for which in range(2):
                                pT, pTb = pts[which]
                                k.mm(at[:, m * 65:(m + 1) * 65], pT[:, m * 128:(m + 1) * 128], VS[:, bi + which, 0:65], [VSb, pTb], [ab],
                                     start=(which == 0), stop=(which == 1), inc=(m == 3 and which == 1))
                        c = swc[0]
                        swc[0] += 1
                        dn4, dn4b = dn4_t[c % 2], dn4_b[c % 2]
                        atk, atkb = atk_t[c % 2], atk_b[c % 2]
                        av = at[:, 0:260].rearrange("p (m c) -> p m c", c=65)
                        k.op(DVE, lambda: nc.vector.tensor_tensor(dn4[:, :], av[:, :, 64], es_s[:, 4 * g:4 * g + 4], ALU.add),
                             reads=[ab, modb], writes=[dn4b])
                        k.op(DVE, lambda: nc.vector.reciprocal(dn4[:, :], dn4[:, :]), reads=[dn4b], writes=[dn4b])
                        for m in range(4):
                            k.op(DVE, lambda: nc.vector.tensor_scalar(atk[:, m * 64:(m + 1) * 64], av[:, m, 0:64], dn4[:, m:m + 1], None, ALU.mult),
                                 reads=[ab, dn4b], writes=[atkb] if m == 0 else [], pwrites=[] if m == 0 else [atkb])
                        return atk, atkb

                    def swa_rest2(bi, atk, atkb):
                        q0 = bi * 128
                        pt, pb = npst()
                        for j in range(2):
                            k.mm(pt[:, j * 128:(j + 1) * 128], atk[:, j * 128:(j + 1) * 128], ident[:], [atkb, cb], [pb],
                                 inc=(j == 1), transpose=True)
                        k.op(ACT, lambda: nc.scalar.copy(AT[:, 2 * g:2 * g + 2, q0:q0 + 128], pt[:, 0:256].rearrange("p (j q) -> p j q", j=2)),
                             reads=[pb], pwrites=[ATb])

                    swa_qk(0)
                    prev = None
                    for bi in range(17):
                        if bi + 1 < 17:
                            swa_qk(bi + 1)
                        cur = swa_rest1(bi)
                        if prev is not None:
                            swa_rest2(bi - 1, *prev)
                        prev = cur
                    swa_rest2(16, *prev)
                if DBG:
                    k.dma(SP, dbg_at[:], AT[:], k.buf('dbg3'), reads=[ATb])
                k.barrier()
            k.barrier()

        if STOP <= 3:
            k.barrier()
            raise StopBuild()
        with ExitStack() as s3:
            wo = sb("wo", [128, 8, D], BF16, s3)
            wob = k.buf("wo")
            for kk in range(8):
                k.dma(POOL, wo[:, kk, :], w_out[kk * 128:(kk + 1) * 128, :], wob, pwrites=[wob])
            for kk in range(8):
                k.op(DVE, lambda: nc.vector.tensor_tensor(wo[:, kk, :], wo[:, kk, :], GA1b[:, :], ALU.mult),
                     reads=[modb], writes=[wob] if kk == 0 else [], pwrites=[] if kk == 0 else [wob])
            wd = sb("wd", [128, 22, D], BF16, s3)
            wdb = k.buf("wd")
            NT_MAX = 5
            x1 = sb("xres1", [128, NT_MAX, D], F32, s3)
            x1b = [k.buf("x1_%d" % i) for i in range(NT_MAX)]
            h2T = sb("h2T", [128, 8, NT_MAX * 128], BF16, s3)
            h2b = k.buf("h2T")
            hT = sb("hT", [128, 22, 512], BF16, s3)
            hTb = k.buf("hT")
            carry = sb("carry", [128, 44, 2], F32, s3)
            carb = k.buf("carry")
            U_t = [sb("U%d" % i, [128, 2 + NT_MAX * 128], F32, s3) for i in range(2)]
            U_b = [k.buf("U%d" % i) for i in range(2)]
            ya_t = [sb("ya%d" % i, [128, 512], F32, s3) for i in range(2)]
            ya_b = [k.buf("ya%d" % i) for i in range(2)]
            yg_t = [sb("yg%d" % i, [128, 512], F32, s3) for i in range(2)]
            yg_b = [k.buf("yg%d" % i) for i in range(2)]
            sg_t = [sb("sg%d" % i, [128, 512], F32, s3) for i in range(1)] * 2
            sg_b = [k.buf("sg%d" % i) for i in range(1)] * 2
            NWU = 4
            wu_t = [sb("wu%d" % i, [128, 8, 128], BF16, s3) for i in range(NWU)]
            wu_b = [k.buf("wu%d" % i) for i in range(NWU)]
            xr_t = [sb("xr%d" % i, [128, D], F32, s3) for i in range(2)]
            xr_b = [k.buf("xr%d" % i) for i in range(2)]
            tmp_t = [sb("tmp%d" % i, [128, D], F32, s3) for i in range(2)]
            tmp_b = [k.buf("tmp%d" % i) for i in range(2)]
            xn2_t = [sb("xn2_%d" % i, [128, D], BF16, s3) for i in range(2)]
            xn2_b = [k.buf("xn2_%d" % i) for i in range(2)]
            junk2 = sb("junk2", [128, D], BF16, s3)
            junk2b = k.buf("junk2")
            st2_t = [(sb("ssq2_%d" % i, [128, 1], F32, s3), sb("rstd2_%d" % i, [128, 1], F32, s3)) for i in range(4)]
            st2_b = [k.buf("st2_%d" % i) for i in range(4)]
            cnt = {"t": 0, "u": 0, "o": 0, "s": 0}

            TGS = [list(range(-1, 4)), list(range(4, 8)), list(range(8, 12)), list(range(12, 16))]
            slabs = [(ti, m, part) for ti in range(len(TGS)) for m in range(22) for part in range(2)]

            def issue_wu(n):
                if n >= len(slabs):
                    return
                _, m, part = slabs[n]
                k.dma(SP, wu_t[n % NWU][:], wup_bf[part * 22 + m], wu_b[n % NWU], reads=[stgb], writes=[wu_b[n % NWU]])

            PRE = 3
            for n in range(PRE):
                issue_wu(n)
            for m in range(22):
                k.dma(POOL, wd[:, m, :], w_down[m * 128:(m + 1) * 128, :], wdb, pwrites=[wdb])
            sl = 0
            for ti, tiles in enumerate(TGS):
                ntl = len(tiles)
                ntok = ntl * 128
                nown = ntok - (128 if ti == 0 else 0)
                def op_a(li, ot):
                    acol = (ot + 1) * 128
                    tloc = 15 + ot + 1
                    c = cnt["t"]
                    cnt["t"] += 1
                    xr, xrb = xr_t[c % 2], xr_b[c % 2]
                    k.dma(SP, xr[:], xa[tloc * 128:(tloc + 1) * 128, :], xrb, writes=[xrb])
                    pA, pAb = nps()
                    pB, pBb = nps()
                    for hh, (pp, ppb) in enumerate(((pA, pAb), (pB, pBb))):
                        for kk in range(8):
                            k.mm(pp[:, :], AT[:, kk, acol:acol + 128], wo[:, kk, hh * 512:(hh + 1) * 512], [ATb, wob], [ppb],
                                 start=(kk == 0), stop=(kk == 7), inc=(kk == 7))
                        k.op(DVE, lambda: nc.vector.tensor_tensor(x1[:, li, hh * 512:(hh + 1) * 512], pp[:, :], xr[:, hh * 512:(hh + 1) * 512], ALU.add),
                             reads=[ppb, xrb], writes=[x1b[li]] if hh == 0 else [], pwrites=[] if hh == 0 else [x1b[li]])
                    si = cnt["s"] % 4
                    cnt["s"] += 1
                    ssq, rstd = st2_t[si]
                    stb = st2_b[si]
                    xn, xnb = xn2_t[c % 2], xn2_b[c % 2]
                    rms_stats(x1[:, li, :], x1b[li], junk2, junk2b, ssq, rstd, stb)
                    k.op(DVE, lambda: nc.vector.tensor_scalar(xn[:], x1[:, li, :], rstd[:, 0:1], None, ALU.mult),
                         reads=[x1b[li], stb], writes=[xnb])
                    return xn, xnb

                def op_b(li, xn, xnb):
                    ptA, pbA = npst()
                    ptB, pbB = npst()
                    for cc_ in range(8):
                        pt, pb = (ptA, pbA) if cc_ % 2 == 0 else (ptB, pbB)
                        k.mm(pt[:, (cc_ // 2) * 128:(cc_ // 2 + 1) * 128], xn[:, cc_ * 128:(cc_ + 1) * 128], ident[:], [xnb, cb], [pb],
                             inc=(cc_ >= 6), transpose=True)
                    for cc_ in range(8):
                        o = h2T[:, cc_, li * 128:(li + 1) * 128]
                        if cc_ % 2 == 0:
                            i_ = ptA[:, (cc_ // 2) * 128:(cc_ // 2 + 1) * 128]
                            k.op(ACT, lambda: nc.scalar.activation(o, i_, AF.Identity, bias=SH2[:, cc_:cc_ + 1], scale=A2[:, cc_:cc_ + 1]),
                                 reads=[pbA, modb], pwrites=[h2b])
                        else:
                            i_ = ptB[:, (cc_ // 2) * 128:(cc_ // 2 + 1) * 128]
                            k.op(DVE, lambda: nc.vector.tensor_scalar(o, i_, A2[:, cc_:cc_ + 1], SH2[:, cc_:cc_ + 1], ALU.mult, ALU.add),
                                 reads=[pbB, modb], pwrites=[h2b])

                pendo = {0: op_a(0, tiles[0])}
                for li, ot in enumerate(tiles):
                    if li + 1 < ntl:
                        pendo[li + 1] = op_a(li + 1, tiles[li + 1])
                    op_b(li, *pendo.pop(li))
                off = 128 if ti == 0 else 0
                deferred = []
                for m in range(22):
                    for part in range(2):
                        mp = part * 22 + m
                        wu, wub = wu_t[sl % NWU], wu_b[sl % NWU]
                        issue_wu(sl + PRE)
                        sl += 1
                        U, Ub = U_t[cnt["u"] % 2], U_b[cnt["u"] % 2]
                        cnt["u"] += 1
                        if part == 0:
                            yv, yvb = ya_t[m % 2], ya_b[m % 2]
                        else:
                            yv, yvb = yg_t[m % 2], yg_b[m % 2]
                        first = True
                        for c0 in range(0, ntok, 512):
                            n_ = min(512, ntok - c0)
                            pt, pb = nps()
                            for kk in range(8):
                                k.mm(pt[:, 0:n_], wu[:, kk, :], h2T[:, kk, c0:c0 + n_], [wub, h2b], [pb],
                                     start=(kk == 0), stop=(kk == 7), inc=(kk == 7))
                            k.op(ACT, lambda: nc.scalar.copy(U[:, 2 + c0:2 + c0 + n_], pt[:, 0:n_]), reads=[pb],
                                 writes=[Ub] if first else [], pwrites=[] if first else [Ub])
                            lo = max(c0, off)
                            hi = c0 + n_
                            if hi > lo:
                                k.op(ACT, lambda: nc.scalar.activation(yv[:, lo - off:hi - off], pt[:, lo - c0:hi - c0], AF.Identity,
                                                                       bias=cbb_s[:, mp:mp + 1], scale=cw_s[:, mp, 2:3]),
                                     reads=[pb, cb], writes=[yvb] if first else [], pwrites=[] if first else [yvb])
                                first = False
                        if ti == 0:
                            k.op(ACT, lambda: nc.scalar.activation(U[:, 128:130], U[:, 128:130], AF.Copy, scale=hf[:, 0:1]),
                                 reads=[cb], writes=[Ub])
                        else:
                            k.op(ACT, lambda: nc.scalar.copy(U[:, 0:2], carry[:, mp, :]), reads=[carb], writes=[Ub])
                        k.op(ACT, lambda: nc.scalar.copy(carry[:, mp, :], U[:, ntok:ntok + 2]), reads=[Ub], writes=[carb])
                        b0 = 2 + off
                        k.op(DVE, lambda: nc.vector.scalar_tensor_tensor(yv[:, 0:nown], U[:, b0 - 1:b0 - 1 + nown], cw_s[:, mp, 1:2], yv[:, 0:nown], ALU.mult, ALU.add),
                             reads=[Ub, cb], writes=[yvb])
                        k.op(DVE, lambda: nc.vector.scalar_tensor_tensor(yv[:, 0:nown], U[:, b0 - 2:b0 - 2 + nown], cw_s[:, mp, 0:1], yv[:, 0:nown], ALU.mult, ALU.add),
                             reads=[Ub, cb], writes=[yvb])
                        if deferred:
                            deferred.pop()()
                        if part == 1:
                            def tail(m=m, yv=yv, yvb=yvb, nown=nown):
                                sg, sgb = sg_t[m % 2], sg_b[m % 2]
                                k.op(ACT, lambda: nc.scalar.activation(sg[:, 0:nown], yv[:, 0:nown], AF.Silu), reads=[yvb], writes=[sgb])
                                k.op(DVE, lambda: nc.vector.tensor_tensor(hT[:, m, 0:nown], sg[:, 0:nown], ya_t[m % 2][:, 0:nown], ALU.mult),
                                     reads=[sgb, ya_b[m % 2]], pwrites=[hTb])
                            deferred.append(tail)
                if deferred:
                    deferred.pop()()
                if DBG and ti == 0:
                    k.dma(SP, dbg_x1[:], x1[:], k.buf('dbg4'), reads=x1b)
                    k.dma(SP, dbg_h2[:], h2T[:], k.buf('dbg5'), reads=[h2b])
                    k.dma(SP, dbg_ht[:], hT[:], k.buf('dbg6'), reads=[hTb])
                if ti == 0:
                    for m in range(22):
                        k.op(DVE, lambda: nc.vector.tensor_tensor(wd[:, m, :], wd[:, m, :], GA2b[:, :], ALU.mult),
                             reads=[modb], writes=[wdb] if m == 0 else [], pwrites=[] if m == 0 else [wdb])
                for li, ot in enumerate(tiles):
                    if ot < 0:
                        continue
                    hc = (li - (1 if ti == 0 else 0)) * 128
                    c = cnt["o"]
                    cnt["o"] += 1
                    tm, tmb = tmp_t[c % 2], tmp_b[c % 2]
                    pA, pAb = nps()
                    pB, pBb = nps()
                    for hh, (pp, ppb) in enumerate(((pA, pAb), (pB, pBb))):
                        for m in range(22):
                            k.mm(pp[:, :], hT[:, m, hc:hc + 128], wd[:, m, hh * 512:(hh + 1) * 512], [hTb, wdb], [ppb],
                                 start=(m == 0), stop=(m == 21), inc=(m == 21))
                        k.op(DVE, lambda: nc.vector.tensor_tensor(tm[:, hh * 512:(hh + 1) * 512], pp[:, :], x1[:, li, hh * 512:(hh + 1) * 512], ALU.add),
                             reads=[ppb, x1b[li]], writes=[tmb] if hh == 0 else [], pwrites=[] if hh == 0 else [tmb])
                    si = cnt["s"] % 4
                    cnt["s"] += 1
                    ssq, rstd = st2_t[si]
                    stb = st2_b[si]
                    rms_stats(tm[:], tmb, junk2, junk2b, ssq, rstd, stb)
                    k.op(DVE, lambda: nc.vector.scalar_tensor_tensor(tm[:], tm[:], rstd[:, 0:1], gfb_s[:], ALU.mult, ALU.mult),
                         reads=[stb, cb], writes=[tmb])
                    k.dma(SP, yout[ot * 128:(ot + 1) * 128, :], tm[:], tmb, reads=[tmb])
            k.barrier()
    except StopBuild:
        pass
    return nc


_NC = None


def _bf(a):
    return np.ascontiguousarray(a.astype(ml_dtypes.bfloat16))


def kernel(x, c, w_ada, b_ada, g_attn, w_in, b_f, sinks, w_out, g_mlp, w_up, conv_w, conv_b, w_down, g_final):
    global _NC
    f = lambda a: np.ascontiguousarray(np.asarray(a, dtype=np.float32))
    x, c, w_ada, b_ada, g_attn, w_in, b_f, sinks, w_out, g_mlp, w_up, conv_w, conv_b, w_down, g_final = map(
        f, (x, c, w_ada, b_ada, g_attn, w_in, b_f, sinks, w_out, g_mlp, w_up, conv_w, conv_b, w_down, g_final))
    if _NC is None:
        _NC = build_nc()
    nc = _NC
    kk = np.arange(128)[:, None]
    qq = np.arange(128)[None, :]
    cmask = np.where(kk <= qq, 0.0, NEGM).astype(np.float32)
    swab = np.zeros((128, 8, 2, 128), np.float32)
    for h in range(8):
        slope = 2.0 ** (-(h + 1))
        swab[:, h, 1, :] = np.where(kk <= qq, -slope * (qq - kk), NEGM)
        swab[:, h, 0, :] = np.where(kk > qq, -slope * (128 + qq - kk), NEGM)
    ident = np.eye(128, dtype=np.float32)
    tmaj = lambda v: f(v.reshape(8, 128).T)
    common = {
        "w_ada": w_ada, "b_ada": f(b_ada.reshape(1, -1)), "gat": tmaj(g_attn), "gmt": tmaj(g_mlp),
        "gfb": f(np.broadcast_to(g_final[None, :], (128, D))), "w_in": w_in, "bfc": f(b_f.reshape(8, 1)),
        "sinkb": f(np.broadcast_to(sinks[None, :], (128, 8))), "w_out": w_out, "w_up": w_up,
        "cwt": f(conv_w.T.reshape(44, 128, 3).transpose(1, 0, 2)), "cbt": f(conv_b.reshape(44, 128).T),
        "w_down": w_down, "cmask": _bf(cmask), "swab": _bf(swab), "ident": _bf(ident),
    }
    in_maps = []
    for core in range(8):
        b, p = core // 2, core % 2
        xa = x[b] if p == 1 else np.concatenate([x[b, :NOWN], x[b, :NOWN]], axis=0)
        m = dict(common)
        m["xa"] = f(xa)
        m["ct"] = tmaj(c[b])
        m["pm"] = np.full((128, 1), 0.0 if p == 1 else NEGM, np.float32)
        m["hf"] = np.full((128, 1), float(p), np.float32)
        in_maps.append(m)
    res = run_bass_kernel_spmd(nc, in_maps, core_ids=list(range(8)))
    if os.environ.get("MKDBG"):
        global _DBG
        _DBG = res.results
    out = np.empty((4, S, D), np.float32)
    for core in range(8):
        b, p = core // 2, core % 2
        out[b, p * NOWN:(p + 1) * NOWN] = res.results[core]["y"]
    return out
```

```python
import os
import numpy as np
import ml_dtypes
from contextlib import ExitStack
import concourse.bass as bass
import concourse.mybir as mybir
from concourse.bass_utils import run_bass_kernel_spmd

F32 = mybir.dt.float32
BF16 = mybir.dt.bfloat16
ALU = mybir.AluOpType
AF = mybir.ActivationFunctionType

D = 1024
S = 4096
NOWN = 2048
DFF = 2816
EPS = 1e-6
NEGM = -30000.0
NQ = 2176
Q0 = 1920


class Eng:
    def __init__(self, nc, name, h, es):
        self.name = name
        self.h = h
        self.sem = es.enter_context(nc.semaphore("s_" + name))
        self.cnt = 0
        self.seen = {}


class Buf:
    def __init__(self, name, excl=False):
        self.name = name
        self.excl = excl
        self.w = {}
        self.r = {}
        self.dsem = None
        self.dcnt = 0


def _upd(d, ev):
    s, v = ev
    if d.get(id(s), (None, 0))[1] < v:
        d[id(s)] = ev


class K:
    def __init__(self, nc, es):
        self.nc = nc
        self.es = es
        self.pe = Eng(nc, "pe", nc.tensor, es)
        self.act = Eng(nc, "act", nc.scalar, es)
        self.dve = Eng(nc, "dve", nc.vector, es)
        self.pool = Eng(nc, "pool", nc.gpsimd, es)
        self.sp = Eng(nc, "sp", nc.sync, es)
        self.engs = [self.pe, self.act, self.dve, self.pool, self.sp]
        self.dsems = []
        self.pend = []
        self.nbuf = 0

    def buf(self, name=None, excl=False):
        self.nbuf += 1
        return Buf(name or "b%d" % self.nbuf, excl)

    def _need(self, eng, reads, writes, pwrites=(), xreads=()):
        ev = {}
        for b in xreads:
            for e in b.w.values():
                _upd(ev, e)
        for b in reads:
            for e in b.w.values():
                _upd(ev, e)
            if b.excl:
                for e in b.r.values():
                    if e[0] is not eng.sem:
                        _upd(ev, e)
        for b in writes:
            for e in b.w.values():
                _upd(ev, e)
            for e in b.r.values():
                _upd(ev, e)
        for b in pwrites:
            for e in b.r.values():
                _upd(ev, e)
        for s, v in ev.values():
            if eng is self.pe and s is self.pe.sem:
                continue
            if eng.seen.get(id(s), 0) >= v:
                continue
            eng.h.wait_ge(s, v)
            eng.seen[id(s)] = v

    def _reg(self, ev, reads, writes, pwrites=()):
        for b in reads:
            _upd(b.r, ev)
        for b in writes:
            b.w = {}
            _upd(b.w, ev)
            b.r = {}
        for b in pwrites:
            _upd(b.w, ev)

    def op(self, eng, fn, reads=(), writes=(), pwrites=(), xreads=()):
        self._need(eng, reads, writes, pwrites, xreads)
        ins = fn()
        eng.cnt += 1
        ins.then_inc(eng.sem, 1)
        self._reg((eng.sem, eng.cnt), list(reads) + list(xreads), writes, pwrites)
        return ins

    def mm(self, out, lhsT, rhs, reads, writes, start=True, stop=True, inc=True, transpose=False):
        pe = self.pe
        self._need(pe, reads, writes)
        if transpose:
            ins = self.nc.tensor.transpose(out, lhsT, rhs)
        else:
            ins = self.nc.tensor.matmul(out, lhsT, rhs, start=start, stop=stop)
        self.pend.append((list(reads), list(writes)))
        if inc:
            pe.cnt += 1
            ins.then_inc(pe.sem, 1)
            for r_, w_ in self.pend:
                self._reg((pe.sem, pe.cnt), r_, w_)
            self.pend = []
        return ins

    def dma(self, q, out, in_, prim, reads=(), writes=(), pwrites=(), throttle=None):
        assert not (q is self.pe)
        if prim.dsem is None:
            prim.dsem = self.es.enter_context(self.nc.semaphore("d_%s" % prim.name))
            self.dsems.append(prim)
        self._need(q, reads, writes, pwrites)
        if throttle is not None and prim.dcnt - 16 * throttle > 0:
            v = prim.dcnt - 16 * throttle
            if q.seen.get(id(prim.dsem), 0) < v:
                q.h.wait_ge(prim.dsem, v)
                q.seen[id(prim.dsem)] = v
        ins = q.h.dma_start(out=out, in_=in_)
        prim.dcnt += 16
        ins.then_inc(prim.dsem, 16)
        self._reg((prim.dsem, prim.dcnt), reads, writes, pwrites)
        return ins

    def barrier(self):
        assert not self.pend
        evs = [(e.sem, e.cnt) for e in self.engs if e.cnt > 0]
        evs += [(b.dsem, b.dcnt) for b in self.dsems if b.dcnt > 0]
        for e in self.engs:
            for s, v in evs:
                if s is e.sem:
                    continue
                if e.seen.get(id(s), 0) >= v:
                    continue
                e.h.wait_ge(s, v)
                e.seen[id(s)] = v


class StopBuild(Exception):
    pass


def build_nc():
    STOP = int(os.environ.get('MKSTOP', '99'))
    nc = bass.Bass("TRN2", target_bir_lowering=False)

    def din(name, shape, dt=F32):
        return nc.dram_tensor(name, list(shape), dt, kind="ExternalInput").ap()

    xa = din("xa", [S, D])
    ct = din("ct", [128, 8])
    w_ada = din("w_ada", [D, 6 * D])
    b_ada = din("b_ada", [1, 6 * D])
    gat = din("gat", [128, 8])
    gmt = din("gmt", [128, 8])
    gfb = din("gfb", [128, D])
    w_in = din("w_in", [D, 2312])
    bfc = din("bfc", [8, 1])
    sinkb = din("sinkb", [128, 8])
    w_out = din("w_out", [D, D])
    w_up = din("w_up", [D, 2 * DFF])
    cwt = din("cwt", [128, 44, 3])
    cbt = din("cbt", [128, 44])
    w_down = din("w_down", [DFF, D])
    pmd = din("pm", [128, 1])
    hfd = din("hf", [128, 1])
    cmaskd = din("cmask", [128, 128], BF16)
    swabd = din("swab", [128, 8, 2, 128], BF16)
    identd = din("ident", [128, 128], BF16)
    yout = nc.dram_tensor("y", [NOWN, D], F32, kind="ExternalOutput").ap()
    DBG = bool(os.environ.get("MKDBG"))
    if DBG:
        dbg_mod = nc.dram_tensor("dbg_mod", [1, 6 * D], F32, kind="ExternalOutput").ap()
        dbg_h = nc.dram_tensor("dbg_h", [128, 8, S], BF16, kind="ExternalOutput").ap()
        dbg_g = nc.dram_tensor("dbg_g", [8, 3, S], BF16, kind="ExternalOutput").ap()
        dbg_at = nc.dram_tensor("dbg_at", [128, 8, NQ], BF16, kind="ExternalOutput").ap()
        dbg_ka = nc.dram_tensor("dbg_ka", [2, 70, S], BF16, kind="ExternalOutput").ap()
        dbg_qa = nc.dram_tensor("dbg_qa", [2, 70, NQ], BF16, kind="ExternalOutput").ap()
        dbg_v = nc.dram_tensor("dbg_v", [128, 32, 2, 128], BF16, kind="ExternalOutput").ap()
        dbg_x1 = nc.dram_tensor("dbg_x1", [128, 5, D], F32, kind="ExternalOutput").ap()
        dbg_h2 = nc.dram_tensor("dbg_h2", [128, 8, 640], BF16, kind="ExternalOutput").ap()
        dbg_ht = nc.dram_tensor("dbg_ht", [128, 22, 512], BF16, kind="ExternalOutput").ap()

    w_in_r = w_in.rearrange("(k p) c -> p k c", p=128)
    w_up_r = w_up.rearrange("(k p) c -> p k c", p=128)
    wup_bf = nc.dram_tensor("wup_bf", [44, 128, 8, 128], BF16, kind="Internal").ap()
    gs_d = nc.dram_tensor("gs_d", [8, 3, S], BF16, kind="Internal").ap()

    try:
      with ExitStack() as es:
        k = K(nc, es)
        PE, ACT, DVE, POOL, SP = k.pe, k.act, k.dve, k.pool, k.sp

        def sb(name, shape, dt=F32, st=es):
            return st.enter_context(nc.sbuf_tensor("sb_" + name, list(shape), dt))

        ps_t = [es.enter_context(nc.psum_tensor("ps%d" % i, [128, 512], F32)) for i in range(8)]
        ps_b = [k.buf("ps%d" % i, excl=True) for i in range(8)]
        psi = [0]
        psti = [0]
        pst_pool = [list(range(8))]

        def nps():
            i = psi[0] % 4
            psi[0] += 1
            return ps_t[i], ps_b[i]

        pai = [0]

        def npa():
            i = 4 + pai[0] % 2
            pai[0] += 1
            return ps_t[i], ps_b[i]

        def npst():
            i = pst_pool[0][psti[0] % len(pst_pool[0])]
            psti[0] += 1
            return ps_t[i][:].bitcast(BF16), ps_b[i]

        cb = k.buf("consts")
        ident = sb("ident", [128, 128], BF16)
        cmask = sb("cmask", [128, 128], BF16)
        pm = sb("pm", [128, 1])
        hf = sb("hf", [128, 1])
        ct_s = sb("ct_s", [128, 8])
        gat_s = sb("gat_s", [128, 8])
        gmt_s = sb("gmt_s", [128, 8])
        gfb_s = sb("gfb_s", [128, D])
        bfc_s = sb("bfc_s", [8, 1])
        sink_s = sb("sink_s", [128, 8])
        cw_s = sb("cw_s", [128, 44, 3])
        cbb_s = sb("cbb_s", [128, 44])
        for dst, src in ((ident, identd), (cmask, cmaskd), (pm, pmd), (hf, hfd),
                         (ct_s, ct), (gat_s, gat), (gmt_s, gmt), (gfb_s, gfb), (bfc_s, bfc),
                         (sink_s, sinkb), (cw_s, cwt), (cbb_s, cbt)):
            k.dma(SP, dst[:], src[:], cb, writes=[cb])

        cc = k.buf("cc")
        ones32 = sb("ones32", [128, 128])
        mhalf = sb("mhalf", [128, 1])
        k.op(DVE, lambda: nc.vector.memset(ones32[:], 1.0), writes=[cc])
        k.op(DVE, lambda: nc.vector.memset(mhalf[:], -0.5), writes=[cc])
        modT = sb("modT", [128, 32])
        A1 = sb("A1", [128, 8])
        A2 = sb("A2", [128, 8])
        GA1b = sb("GA1b", [128, D])
        GA2b = sb("GA2b", [128, D])
        nbf = sb("nbf", [8, 1])
        es_s = sb("es_s", [128, 8])
        modb = k.buf("mod")
        AT = sb("AT", [128, 8, NQ], BF16)
        ATb = k.buf("AT")
        stgb = k.buf("stg")

        def rms_stats(xt, xb, junk, junkb, ssq, rstd, stb):
            k.op(ACT, lambda: nc.scalar.activation(junk[:], xt, AF.Square, accum_out=ssq[:]),
                 reads=[xb], writes=[junkb, stb])
            k.op(POOL, lambda: nc.gpsimd.tensor_scalar(rstd[:], ssq[:], 1.0 / D, EPS, ALU.mult, ALU.add),
                 reads=[stb], writes=[stb])
            k.op(POOL, lambda: nc.gpsimd.tensor_tensor(rstd[:], rstd[:], mhalf[:], ALU.pow),
                 reads=[stb, cc], writes=[stb])

        with ExitStack() as s1:
            swab = sb("swab", [128, 8, 2, 128], BF16, s1)
            esb = sb("esb", [128, 8, 128], F32, s1)
            k.dma(SP, swab[:], swabd[:], cb, writes=[cb])
            hNT = sb("hNT", [128, 8, S], BF16, s1)
            hN_b = [k.buf("hN%d" % i) for i in range(8)]
            s0 = ExitStack()
            if True:
                modrow = sb("modrow", [1, 6 * D], F32, s0)
                bada_s = sb("bada_s", [1, 6 * D], F32, s0)
                scb = k.buf("sc")
                sc = sb("sc", [128, 8], F32, s0)
                k.dma(SP, bada_s[:], b_ada[:], cb, writes=[cb])
                k.op(ACT, lambda: nc.scalar.activation(sc[:], ct_s[:], AF.Silu), reads=[cb], writes=[scb])
                wa_t = [sb("wa%d" % i, [128, 1536], F32, s0) for i in range(2)]
                wa_b = [k.buf("wa%d" % i) for i in range(2)]
                mrb = k.buf("modrow")
                adac = [0]

                def ada_step(c0, kk):
                    wt, wb = wa_t[adac[0] % 2], wa_b[adac[0] % 2]
                    adac[0] += 1
                    k.dma(SP, wt[:, 0:1536], w_ada[kk * 128:(kk + 1) * 128, c0:c0 + 1536], wb, writes=[wb])
                    for j in range(3):
                        k.mm(ps_t[j][0:1, :], sc[:, kk:kk + 1], wt[:, j * 512:(j + 1) * 512], [scb, wb], [ps_b[j]],
                             start=(kk == 0), stop=(kk == 7), inc=(j == 2))
                    if kk == 7:
                        for j in range(3):
                            cs = slice(c0 + j * 512, c0 + (j + 1) * 512)
                            k.op(DVE, lambda: nc.vector.tensor_tensor(modrow[0:1, cs], ps_t[j][0:1, :], bada_s[0:1, cs], ALU.add),
                                 reads=[ps_b[j], cb], pwrites=[mrb])

                for pz in range(2):
                    for kk in range(8):
                        ada_step(pz * 1536, kk)
                pt, pb = nps()
                for idx in range(16):
                    col0 = [0, 1024][idx // 8] + (idx % 8) * 128
                    k.mm(pt[:, idx:idx + 1], modrow[0:1, col0:col0 + 128], ones32[0:1, 0:1], [mrb, cc], [pb],
                         start=True, stop=True, inc=(idx == 15))
                k.op(DVE, lambda: nc.vector.tensor_copy(modT[:, 0:16], pt[:, 0:16]), reads=[pb], writes=[modb])
                k.op(DVE, lambda: nc.vector.tensor_scalar(A1[:], modT[:, 8:16], 1.0, None, ALU.add), reads=[modb], writes=[modb])
                k.op(DVE, lambda: nc.vector.tensor_tensor(A1[:], A1[:], gat_s[:], ALU.mult), reads=[modb, cb], writes=[modb])
                for hh in range(2):
                    pt, pb = nps()
                    k.mm(pt[:, :], ones32[0:1, 0:128], modrow[0:1, 2048 + hh * 512:2048 + (hh + 1) * 512], [mrb, cc], [pb])
                    k.op(DVE, lambda: nc.vector.tensor_copy(GA1b[:, hh * 512:(hh + 1) * 512], pt[:, :]), reads=[pb], writes=[modb])
                k.op(DVE, lambda: nc.vector.tensor_scalar(nbf[:], bfc_s[:], -1.0, None, ALU.mult), reads=[cb], writes=[modb])
                k.op(ACT, lambda: nc.scalar.activation(es_s[:], sink_s[:], AF.Exp), reads=[cb], writes=[modb])
                for h in range(8):
                    k.op(DVE, lambda: nc.vector.tensor_scalar(esb[:, h, :], ones32[:, :], es_s[:, h:h + 1], None, ALU.mult),
                         reads=[modb, cc], writes=[modb])
            SH1 = modT[:, 0:8]
            SH2 = modT[:, 16:24]

            pst_pool[0] = [3, 4, 5, 6]
            psti[0] = 0
            with ExitStack() as s1a:
                NX = 3
                x_t = [sb("x%d" % i, [128, D], F32, s1a) for i in range(NX)]
                x_b = [k.buf("x%d" % i) for i in range(NX)]
                xn_t = [sb("xn%d" % i, [128, D], BF16, s1a) for i in range(2)]
                xn_b = [k.buf("xn%d" % i) for i in range(2)]
                junk = sb("junk", [128, D], BF16, s1a)
                junkb = k.buf("junk")
                st_t = [(sb("ssq%d" % i, [128, 1], F32, s1a), sb("rstd%d" % i, [128, 1], F32, s1a)) for i in range(NX)]
                st_b = [k.buf("st%d" % i) for i in range(NX)]
                def p1_a(t):
                    xt, xb = x_t[t % NX], x_b[t % NX]
                    ssq, rstd = st_t[t % NX]
                    stb = st_b[t % NX]
                    xn, xnb = xn_t[t % 2], xn_b[t % 2]
                    k.dma(SP, xt[:], xa[t * 128:(t + 1) * 128, :], xb, writes=[xb])
                    rms_stats(xt[:], xb, junk, junkb, ssq, rstd, stb)
                    k.op(DVE, lambda: nc.vector.tensor_scalar(xn[:], xt[:], rstd[:, 0:1], None, ALU.mult),
                         reads=[xb, stb], writes=[xnb])
                    ptA, pbA = npst()
                    ptB, pbB = npst()
                    for c in range(8):
                        pt, pb = (ptA, pbA) if c % 2 == 0 else (ptB, pbB)
                        k.mm(pt[:, (c // 2) * 128:(c // 2 + 1) * 128], xn[:, c * 128:(c + 1) * 128], ident[:], [xnb, cb], [pb],
                             inc=(c >= 6), transpose=True)
                    return ptA, pbA, ptB, pbB

                def p1_b(t, ptA, pbA, ptB, pbB):
                    hb = hN_b[t // 4]
                    for c in range(8):
                        o = hNT[:, c, t * 128:(t + 1) * 128]
                        if c % 2 == 0:
                            i_ = ptA[:, (c // 2) * 128:(c // 2 + 1) * 128]
                            k.op(ACT, lambda: nc.scalar.activation(o, i_, AF.Identity, bias=SH1[:, c:c + 1], scale=A1[:, c:c + 1]),
                                 reads=[pbA, modb], pwrites=[hb])
                        else:
                            i_ = ptB[:, (c // 2) * 128:(c // 2 + 1) * 128]
                            k.op(DVE, lambda: nc.vector.tensor_scalar(o, i_, A1[:, c:c + 1], SH1[:, c:c + 1], ALU.mult, ALU.add),
                                 reads=[pbB, modb], pwrites=[hb])

                pend1 = {0: p1_a(0)}
                for t in range(32):
                    if t + 1 < 32:
                        pend1[t + 1] = p1_a(t + 1)
                    if t % 2 == 0:
                        ada_step(3072 + (t // 16) * 1536, (t // 2) % 8)
                    p1_b(t, *pend1.pop(t))
                pt, pb = nps()
                for idx in range(16):
                    col0 = [3072, 4096][idx // 8] + (idx % 8) * 128
                    k.mm(pt[:, idx:idx + 1], modrow[0:1, col0:col0 + 128], ones32[0:1, 0:1], [mrb, cc], [pb],
                         start=True, stop=True, inc=(idx == 15))
                k.op(DVE, lambda: nc.vector.tensor_copy(modT[:, 16:32], pt[:, 0:16]), reads=[pb], writes=[modb])
                k.op(DVE, lambda: nc.vector.tensor_scalar(A2[:], modT[:, 24:32], 1.0, None, ALU.add), reads=[modb], writes=[modb])
                k.op(DVE, lambda: nc.vector.tensor_tensor(A2[:], A2[:], gmt_s[:], ALU.mult), reads=[modb, cb], writes=[modb])
                for hh in range(2):
                    pt, pb = nps()
                    k.mm(pt[:, :], ones32[0:1, 0:128], modrow[0:1, 5120 + hh * 512:5120 + (hh + 1) * 512], [mrb, cc], [pb])
                    k.op(DVE, lambda: nc.vector.tensor_copy(GA2b[:, hh * 512:(hh + 1) * 512], pt[:, :]), reads=[pb], writes=[modb])
                if DBG:
                    k.dma(SP, dbg_mod[:], modrow[:], k.buf('dbg0'), reads=[mrb])
                if DBG:
                    k.dma(SP, dbg_h[:], hNT[:], k.buf('dbg1'), reads=hN_b)
                k.barrier()
            s0.close()

            if STOP <= 1:
                k.barrier()
                raise StopBuild()
            pst_pool[0] = [6, 7]
            psti[0] = 0
            Gsb = k.buf("Gs")
            PT_t = [sb("PT%d" % i, [128, 512], BF16, s1) for i in range(4)]
            PT_b = [k.buf("PT%d" % i) for i in range(4)]
            pti = [0]

            def npt():
                i = pti[0] % 4
                pti[0] += 1
                return PT_t[i], PT_b[i]
            den_t = [sb("den%d" % i, [64, 512], F32, s1) for i in range(2)]
            den_b = [k.buf("den%d" % i) for i in range(2)]
            deni = [0]
            wq_t = [sb("wq%d" % i, [128, 8, 128], BF16, s1) for i in range(2)]
            wk_t = [sb("wk%d" % i, [128, 8, 128], BF16, s1) for i in range(2)]
            wv_t = [sb("wv%d" % i, [128, 8, 128], BF16, s1) for i in range(2)]
            wq_b = [k.buf("wq%d" % i) for i in range(2)]
            wk_b = [k.buf("wk%d" % i) for i in range(2)]
            wv_b = [k.buf("wv%d" % i) for i in range(2)]

            QCH = [(0, 128)] + [(128 + 512 * i, 512) for i in range(4)]

            with ExitStack() as s2b:
                KA = [sb("KA%d" % i, [70, S], BF16, s2b) for i in range(2)]
                QA = [sb("QA%d" % i, [70, NQ], BF16, s2b) for i in range(2)]
                V = sb("V", [128, 32, 2, 128], BF16, s2b)
                KAb = [k.buf("KA%d" % i) for i in range(2)]
                QAb = [k.buf("QA%d" % i) for i in range(2)]
                Vb = k.buf("V")
                for i in range(2):
                    k.op(POOL, lambda: nc.gpsimd.memset(KA[i][64:70, :], -1.0), writes=[KAb[i]])
                    k.op(POOL, lambda: nc.gpsimd.memset(QA[i][64:70, :], 1.0), writes=[QAb[i]])
                k.op(POOL, lambda: nc.gpsimd.memset(V[:, :, :, 64:128], 1.0), writes=[Vb])

                def load_fox_w(g):
                    s = g % 2
                    k.dma(POOL, wq_t[s][:], w_in_r[:, :, 768 + 128 * g:768 + 128 * (g + 1)], wq_b[s], writes=[wq_b[s]])
                    k.dma(POOL, wk_t[s][:], w_in_r[:, :, 1280 + 128 * g:1280 + 128 * (g + 1)], wk_b[s], writes=[wk_b[s]])
                    k.dma(POOL, wv_t[s][:], w_in_r[:, :, 1792 + 128 * g:1792 + 128 * (g + 1)], wv_b[s], writes=[wv_b[s]])

                wf = sb("wf", [128, 8, 8], BF16, s2b)
                wfb = k.buf("wf")
                k.dma(POOL, wf[:], w_in_r[:, :, 2304:2312], wfb, writes=[wfb])
                Et = sb("Et", [8, 512], F32, s2b)
                SPt = sb("SPt", [8, 512], F32, s2b)
                G32 = [sb("G32_%d" % i, [8, 512], F32, s2b) for i in range(2)]
                R1 = sb("R1", [8, 512], F32, s2b)
                R2 = sb("R2", [8, 512], F32, s2b)
                Gc = [sb("Gc%d" % i, [8, 3, 512], BF16, s2b) for i in range(2)]
                on8 = sb("on8", [8, 512], F32, s2b)
                etb, spb, r1b, r2b, onb = k.buf("Et"), k.buf("SPt"), k.buf("R1"), k.buf("R2"), k.buf("on8")
                g32b = [k.buf("G32_%d" % i) for i in range(2)]
                gcb = [k.buf("Gc%d" % i) for i in range(2)]
                k.op(POOL, lambda: nc.gpsimd.memset(on8[:], 1.0), writes=[onb])

                def f_chunk(n):
                    cs = slice(n * 512, (n + 1) * 512)
                    pt, pb = nps()
                    for kk in range(8):
                        k.mm(pt[0:8, :], wf[:, kk, :], hNT[:, kk, cs], [wfb, hN_b[n]], [pb],
                             start=(kk == 0), stop=(kk == 7), inc=(kk == 7))
                    k.op(ACT, lambda: nc.scalar.activation(Et[:], pt[0:8, :], AF.Exp, bias=nbf[:, 0:1], scale=-1.0),
                         reads=[pb, modb], writes=[etb])
                    k.op(ACT, lambda: nc.scalar.activation(SPt[:], Et[:], AF.Ln, bias=1.0, scale=1.0), reads=[etb], writes=[spb])
                    gcur, gcurb = G32[n % 2], g32b[n % 2]
                    if n == 0:
                        k.op(DVE, lambda: nc.vector.tensor_tensor_scan(gcur[:], on8[:], SPt[:], 0.0, ALU.mult, ALU.add),
                             reads=[onb, spb], writes=[gcurb])
                    else:
                        gprev, gprevb = G32[(n - 1) % 2], g32b[(n - 1) % 2]
                        k.op(DVE, lambda: nc.vector.tensor_tensor_scan(gcur[:], on8[:], SPt[:], gprev[:, 511:512], ALU.mult, ALU.add),
                             reads=[onb, spb, gprevb], writes=[gcurb])
                    gc, gcbb = Gc[n % 2], gcb[n % 2]
                    k.op(DVE, lambda: nc.vector.tensor_copy(gc[:, 0, :], gcur[:]), reads=[gcurb], writes=[gcbb])
                    k.op(DVE, lambda: nc.vector.tensor_tensor(R1[:], gcur[:], gc[:, 0, :], ALU.subtract), reads=[gcurb, gcbb], writes=[r1b])
                    k.op(DVE, lambda: nc.vector.tensor_copy(gc[:, 1, :], R1[:]), reads=[r1b], pwrites=[gcbb])
                    k.op(DVE, lambda: nc.vector.tensor_tensor(R2[:], R1[:], gc[:, 1, :], ALU.subtract), reads=[r1b, gcbb], writes=[r2b])
                    k.op(DVE, lambda: nc.vector.tensor_copy(gc[:, 2, :], R2[:]), reads=[r2b], pwrites=[gcbb])
                    k.dma(SP, gs_d[:, :, cs], gc[:], gcbb, reads=[gcbb], pwrites=[Gsb])

                load_fox_w(0)
                for g in range(4):
                    s = g % 2
                    if g + 1 < 4:
                        load_fox_w(g + 1)
                    for sl_ in range(11 * g, 11 * (g + 1)):
                        k.dma(POOL, wup_bf[sl_], w_up_r[:, :, sl_ * 128:(sl_ + 1) * 128], stgb, pwrites=[stgb], throttle=3)
                    def aug_rows():
                        for i in range(2):
                            h = 2 * g + i
                            k.dma(SP, KA[i][64:67, :], gs_d[h], KAb[i], reads=[Gsb], writes=[KAb[i]])
                            k.dma(SP, QA[i][67:70, :], gs_d[h, :, Q0:S], QAb[i], reads=[Gsb], writes=[QAb[i]])
                    if g > 0:
                        aug_rows()
                    for n in range(8):
                        if g == 0:
                            f_chunk(n)
                        pt, pb = nps()
                        for kk in range(8):
                            k.mm(pt[:, :], wk_t[s][:, kk, :], hNT[:, kk, n * 512:(n + 1) * 512], [wk_b[s], hN_b[n]], [pb],
                                 start=(kk == 0), stop=(kk == 7), inc=(kk == 7))
                        k.op(ACT, lambda: nc.scalar.copy(KA[0][0:64, n * 512:(n + 1) * 512], pt[0:64, :]), xreads=[pb], writes=[KAb[0]])
                        k.op(DVE, lambda: nc.vector.tensor_copy(KA[1][0:64, n * 512:(n + 1) * 512], pt[64:128, :]), xreads=[pb], writes=[KAb[1]])
                    if g == 0:
                        aug_rows()
                    for (c0, n_) in QCH:
                        pt, pb = nps()
                        t0 = Q0 + c0
                        for kk in range(8):
                            k.mm(pt[:, 0:n_], wq_t[s][:, kk, :], hNT[:, kk, t0:t0 + n_], [wq_b[s], hN_b[t0 // 512]], [pb],
                                 start=(kk == 0), stop=(kk == 7), inc=(kk == 7))
                        k.op(ACT, lambda: nc.scalar.mul(QA[0][0:64, c0:c0 + n_], pt[0:64, 0:n_], 0.125), xreads=[pb], writes=[QAb[0]])
                        k.op(DVE, lambda: nc.vector.tensor_scalar(QA[1][0:64, c0:c0 + n_], pt[64:128, 0:n_], 0.125, None, ALU.mult),
                             xreads=[pb], writes=[QAb[1]])
                    for kb4 in range(8):
                        pt, pb = nps()
                        for j in range(4):
                            kb = kb4 * 4 + j
                            for kk in range(8):
                                k.mm(pt[:, j * 128:(j + 1) * 128], hNT[:, kk, kb * 128:(kb + 1) * 128], wv_t[s][:, kk, :],
                                     [wv_b[s], hN_b[kb // 4]], [pb], start=(kk == 0), stop=(kk == 7), inc=(kk == 7 and j == 3))
                        eng = ACT if kb4 % 2 == 0 else DVE
                        src = pt[:, :].rearrange("p (j h d) -> p j h d", j=4, h=2)
                        dst = V[:, kb4 * 4:(kb4 + 1) * 4, :, 0:64]
                        if eng is ACT:
                            k.op(ACT, lambda: nc.scalar.copy(dst, src), reads=[pb], writes=[Vb])
                        else:
                            k.op(DVE, lambda: nc.vector.tensor_copy(dst, src), reads=[pb], writes=[Vb])
                    items = []
                    for qi, (c0, n_) in enumerate(QCH):
                        d0 = 15 if qi == 0 else 16 + 4 * (qi - 1)
                        nkb = d0 + n_ // 128
                        for i in range(2):
                            for kb in range(nkb):
                                items.append((qi, c0, n_, d0, nkb, i, kb))
                    stq = {}
                    accs = {}

                    def emit_qk(idx):
                        qi, c0, n_, d0, nkb, i, kb = items[idx]
                        j = kb - d0
                        st, sb_ = nps()
                        lo = 0 if j < 0 else 128 * j
                        kcol = KA[i][0:70, kb * 128:(kb + 1) * 128]
                        if j >= 0:
                            k.mm(st[:, lo:lo + 128], kcol, QA[i][0:70, c0 + lo:c0 + lo + 128], [KAb[i], QAb[i]], [sb_],
                                 start=True, stop=False, inc=False)
                            k.mm(st[:, lo:lo + 128], ident[:], cmask[:], [cb], [sb_], start=False, stop=True,
                                 inc=(lo + 128 >= n_))
                            if lo + 128 < n_:
                                k.mm(st[:, lo + 128:n_], kcol, QA[i][0:70, c0 + lo + 128:c0 + n_], [KAb[i], QAb[i]], [sb_])
                        else:
                            k.mm(st[:, 0:n_], kcol, QA[i][0:70, c0:c0 + n_], [KAb[i], QAb[i]], [sb_])
                        stq[idx] = (st, sb_, lo)

                    def emit_rest(idx):
                        qi, c0, n_, d0, nkb, i, kb = items[idx]
                        h = 2 * g + i
                        st, sb_, lo = stq.pop(idx)
                        if kb == 0:
                            accs[(qi, i)] = npa()
                        at, ab = accs[(qi, i)]
                        pT, pTb = npt()
                        if qi >= 1 and kb < 16:
                            k.op(ACT, lambda: nc.scalar.activation(pT[:, lo:n_], st[:, lo:n_], AF.Exp, bias=pm[:, 0:1], scale=1.0),
                                 reads=[sb_, cb], writes=[pTb])
                        else:
                            k.op(ACT, lambda: nc.scalar.activation(pT[:, lo:n_], st[:, lo:n_], AF.Exp), reads=[sb_], writes=[pTb])
                        k.mm(at[:, lo:n_], V[:, kb, i, :], pT[:, lo:n_], [Vb, pTb], [ab],
                             start=(kb == 0), stop=(kb == nkb - 1), inc=(kb == nkb - 1))
                        if kb == nkb - 1:
                            dn, dnb = den_t[deni[0] % 2], den_b[deni[0] % 2]
                            deni[0] += 1
                            k.op(DVE, lambda: nc.vector.tensor_copy(dn[:, 0:n_], at[64:128, 0:n_]), reads=[ab], writes=[dnb])
                            k.op(DVE, lambda: nc.vector.reciprocal(dn[:, 0:n_], dn[:, 0:n_]), reads=[dnb], writes=[dnb])
                            o = AT[(h % 2) * 64:(h % 2) * 64 + 64, 4 + h // 2, c0:c0 + n_]
                            k.op(DVE, lambda: nc.vector.tensor_tensor(o, at[0:64, 0:n_], dn[:, 0:n_], ALU.mult),
                                 reads=[ab, dnb], pwrites=[ATb])

                    LOOK = 2
                    for idx in range(min(LOOK, len(items))):
                        emit_qk(idx)
                    for idx in range(len(items)):
                        if idx + LOOK < len(items):
                            emit_qk(idx + LOOK)
                        emit_rest(idx)
                if DBG:
                    for i in range(2):
                        k.dma(SP, dbg_ka[i], KA[i][:], k.buf('dbgk%d' % i), reads=[KAb[i]])
                        k.dma(SP, dbg_qa[i], QA[i][:], k.buf('dbgq%d' % i), reads=[QAb[i]])
                    k.dma(SP, dbg_v[:], V[:], k.buf('dbgv'), reads=[Vb])
                    k.dma(SP, dbg_g[:], gs_d[:], k.buf('dbg2'), reads=[Gsb])
                k.barrier()

            if STOP <= 2:
                k.barrier()
                raise StopBuild()
            with ExitStack() as s2c:
                KS = sb("KS", [64, 2304], BF16, s2c)
                QS = sb("QS", [64, 4, NQ], BF16, s2c)
                VS = sb("VS", [128, 18, 128], BF16, s2c)
                KSb, QSb, VSb = k.buf("KS"), k.buf("QS"), k.buf("VS")
                k.op(POOL, lambda: nc.gpsimd.memset(VS[:, :, 64:128], 1.0), writes=[VSb])
                dn4_t = [sb("dn4_%d" % i, [128, 4], F32, s2c) for i in range(2)]
                dn4_b = [k.buf("dn4_%d" % i) for i in range(2)]
                atk_t = [sb("atk%d" % i, [128, 256], BF16, s2c) for i in range(2)]
                atk_b = [k.buf("atk%d" % i) for i in range(2)]
                swc = [0]
                T0 = 1792
                for g in range(2):
                    k.dma(POOL, wq_t[0][:], w_in_r[:, :, 256 * g:256 * g + 128], wq_b[0], writes=[wq_b[0]])
                    k.dma(POOL, wq_t[1][:], w_in_r[:, :, 256 * g + 128:256 * g + 256], wq_b[1], writes=[wq_b[1]])
                    k.dma(POOL, wk_t[0][:, :, 0:64], w_in_r[:, :, 512 + 64 * g:512 + 64 * (g + 1)], wk_b[0], writes=[wk_b[0]])
                    k.dma(POOL, wv_t[0][:, :, 0:64], w_in_r[:, :, 640 + 64 * g:640 + 64 * (g + 1)], wv_b[0], writes=[wv_b[0]])
                    for (c0, n_) in [(0, 512), (512, 512), (1024, 512), (1536, 512), (2048, 256)]:
                        pt, pb = nps()
                        t0 = T0 + c0
                        for kk in range(8):
                            k.mm(pt[0:64, 0:n_], wk_t[0][:, kk, 0:64], hNT[:, kk, t0:t0 + n_], [wk_b[0], hN_b[t0 // 512]], [pb],
                                 start=(kk == 0), stop=(kk == 7), inc=(kk == 7))
                        k.op(ACT, lambda: nc.scalar.copy(KS[:, c0:c0 + n_], pt[0:64, 0:n_]), reads=[pb], writes=[KSb])
                    for s in range(2):
                        for (c0, n_) in QCH:
                            pt, pb = nps()
                            t0 = Q0 + c0
                            for kk in range(8):
                                k.mm(pt[:, 0:n_], wq_t[s][:, kk, :], hNT[:, kk, t0:t0 + n_], [wq_b[s], hN_b[t0 // 512]], [pb],
                                     start=(kk == 0), stop=(kk == 7), inc=(kk == 7))
                            k.op(ACT, lambda: nc.scalar.mul(QS[:, 2 * s, c0:c0 + n_], pt[0:64, 0:n_], 0.125), xreads=[pb], writes=[QSb])
                            k.op(DVE, lambda: nc.vector.tensor_scalar(QS[:, 2 * s + 1, c0:c0 + n_], pt[64:128, 0:n_], 0.125, None, ALU.mult),
                                 xreads=[pb], writes=[QSb])
                    for b0 in range(0, 18, 4):
                        nb = min(4, 18 - b0)
                        pt, pb = nps()
                        for j in range(nb):
                            kb = 14 + b0 + j
                            for kk in range(8):
                                k.mm(pt[:, j * 64:(j + 1) * 64], hNT[:, kk, kb * 128:(kb + 1) * 128], wv_t[0][:, kk, 0:64],
                                     [wv_b[0], hN_b[kb // 4]], [pb], start=(kk == 0), stop=(kk == 7), inc=(kk == 7 and j == nb - 1))
                        k.op(DVE, lambda: nc.vector.tensor_copy(VS[:, b0:b0 + nb, 0:64],
                                                                pt[:, 0:nb * 64].rearrange("p (j d) -> p j d", j=nb)),
                             reads=[pb], writes=[VSb])
                    sq = {}

                    def swa_qk(bi):
                        q0 = bi * 128
                        pts = []
                        for which in range(2):
                            st, sb_ = nps()
                            kcol = KS[:, (bi + which) * 128:(bi + which + 1) * 128]
                            k.mm(st[:, :].rearrange("p (a b) -> p a b", a=4), ident[:], swab[:, 4 * g:4 * g + 4, which, :], [cb], [sb_],
                                 start=True, stop=False, inc=False)
                            for m in range(4):
                                k.mm(st[:, m * 128:(m + 1) * 128], kcol, QS[:, m, q0:q0 + 128], [KSb, QSb], [sb_],
                                     start=False, stop=(m == 3), inc=(m == 3))
                            pts.append((st, sb_))
                        sq[bi] = pts

                    def swa_rest1(bi):
                        pts = []
                        for which, (st, sb_) in enumerate(sq.pop(bi)):
                            pT, pTb = npt()
                            if which == 0 and bi == 1:
                                k.op(ACT, lambda: nc.scalar.activation(pT[:, :], st[:, :], AF.Exp, bias=pm[:, 0:1], scale=1.0),
                                     reads=[sb_, cb], writes=[pTb])
                            else:
                                k.op(ACT, lambda: nc.scalar.activation(pT[:, :], st[:, :], AF.Exp), reads=[sb_], writes=[pTb])
                            pts.append((pT, pTb))
                        at, ab = npa()
                        for m in range(4):
                            for which in range(2):
                                pT, pTb = pts[which]
                                k.mm(at[:, m * 65:(m + 1) * 65], pT[:, m * 128:(m + 1) * 128], VS[:, bi + which, 0:65], [VSb, pTb], [ab],
                                     start=(which == 0), stop=(which == 1), inc=(m == 3 and which == 1))
                        c = swc[0]
                        swc[0] += 1
                        dn4, dn4b = dn4_t[c % 2], dn4_b[c % 2]
                        atk, atkb = atk_t[c % 2], atk_b[c % 2]
                        av = at[:, 0:260].rearrange("p (m c) -> p m c", c=65)
                        k.op(DVE, lambda: nc.vector.tensor_tensor(dn4[:, :], av[:, :, 64], es_s[:, 4 * g:4 * g + 4], ALU.add),
                             reads=[ab, modb], writes=[dn4b])
                        k.op(DVE, lambda: nc.vector.reciprocal(dn4[:, :], dn4[:, :]), reads=[dn4b], writes=[dn4b])
                        for m in range(4):
                            k.op(DVE, lambda: nc.vector.tensor_scalar(atk[:, m * 64:(m + 1) * 64], av[:, m, 0:64], dn4[:, m:m + 1], None, ALU.mult),
                                 reads=[ab, dn4b], writes=[atkb] if m == 0 else [], pwrites=[] if m == 0 else [atkb])
                        return atk, atkb

                    def swa_rest2(bi, atk, atkb):
                        q0 = bi * 128
                        pt, pb = npst()
                        for j in range(2):
                            k.mm(pt[:, j * 128:(j + 1) * 128], atk[:, j * 128:(j + 1) * 128], ident[:], [atkb, cb], [pb],
                                 inc=(j == 1), transpose=True)
                        k.op(ACT, lambda: nc.scalar.copy(AT[:, 2 * g:2 * g + 2, q0:q0 + 128], pt[:, 0:256].rearrange("p (j q) -> p j q", j=2)),
                             reads=[pb], pwrites=[ATb])

                    swa_qk(0)
                    prev = None
                    for bi in range(17):
                        if bi + 1 < 17:
                            swa_qk(bi + 1)
                        cur = swa_rest1(bi)
                        if prev is not None:
                            swa_rest2(bi - 1, *prev)
                        prev = cur
                    swa_rest2(16, *prev)
                if DBG:
                    k.dma(SP, dbg_at[:], AT[:], k.buf('dbg3'), reads=[ATb])
                k.barrier()
            k.barrier()

        if STOP <= 3:
            k.barrier()
            raise StopBuild()
        with ExitStack() as s3:
            wo = sb("wo", [128, 8, D], BF16, s3)
            wob = k.buf("wo")
            for kk in range(8):
                k.dma(POOL, wo[:, kk, :], w_out[kk * 128:(kk + 1) * 128, :], wob, pwrites=[wob])
            for kk in range(8):
                k.op(DVE, lambda: nc.vector.tensor_tensor(wo[:, kk, :], wo[:, kk, :], GA1b[:, :], ALU.mult),
                     reads=[modb], writes=[wob] if kk == 0 else [], pwrites=[] if kk == 0 else [wob])
            wd = sb("wd", [128, 22, D], BF16, s3)
            wdb = k.buf("wd")
            NT_MAX = 5
            x1 = sb("xres1", [128, NT_MAX, D], F32, s3)
            x1b = [k.buf("x1_%d" % i) for i in range(NT_MAX)]
            h2T = sb("h2T", [128, 8, NT_MAX * 128], BF16, s3)
            h2b = k.buf("h2T")
            hT = sb("hT", [128, 22, 512], BF16, s3)
            hTb = k.buf("hT")
            carry = sb("carry", [128, 44, 2], F32, s3)
            carb = k.buf("carry")
            U_t = [sb("U%d" % i, [128, 2 + NT_MAX * 128], F32, s3) for i in range(2)]
            U_b = [k.buf("U%d" % i) for i in range(2)]
            ya_t = [sb("ya%d" % i, [128, 512], F32, s3) for i in range(2)]
            ya_b = [k.buf("ya%d" % i) for i in range(2)]
            yg_t = [sb("yg%d" % i, [128, 512], F32, s3) for i in range(2)]
            yg_b = [k.buf("yg%d" % i) for i in range(2)]
            sg_t = [sb("sg%d" % i, [128, 512], F32, s3) for i in range(1)] * 2
            sg_b = [k.buf("sg%d" % i) for i in range(1)] * 2
            NWU = 4
            wu_t = [sb("wu%d" % i, [128, 8, 128], BF16, s3) for i in range(NWU)]
            wu_b = [k.buf("wu%d" % i) for i in range(NWU)]
            xr_t = [sb("xr%d" % i, [128, D], F32, s3) for i in range(2)]
            xr_b = [k.buf("xr%d" % i) for i in range(2)]
            tmp_t = [sb("tmp%d" % i, [128, D], F32, s3) for i in range(2)]
            tmp_b = [k.buf("tmp%d" % i) for i in range(2)]
            xn2_t = [sb("xn2_%d" % i, [128, D], BF16, s3) for i in range(2)]
            xn2_b = [k.buf("xn2_%d" % i) for i in range(2)]
            junk2 = sb("junk2", [128, D], BF16, s3)
            junk2b = k.buf("junk2")
            st2_t = [(sb("ssq2_%d" % i, [128, 1], F32, s3), sb("rstd2_%d" % i, [128, 1], F32, s3)) for i in range(4)]
            st2_b = [k.buf("st2_%d" % i) for i in range(4)]
            cnt = {"t": 0, "u": 0, "o": 0, "s": 0}

            TGS = [list(range(-1, 4)), list(range(4, 8)), list(range(8, 12)), list(range(12, 16))]
            slabs = [(ti, m, part) for ti in range(len(TGS)) for m in range(22) for part in range(2)]

            def issue_wu(n):
                if n >= len(slabs):
                    return
                _, m, part = slabs[n]
                k.dma(SP, wu_t[n % NWU][:], wup_bf[part * 22 + m], wu_b[n % NWU], reads=[stgb], writes=[wu_b[n % NWU]])

            PRE = 3
            for n in range(PRE):
                issue_wu(n)
            for m in range(22):
                k.dma(POOL, wd[:, m, :], w_down[m * 128:(m + 1) * 128, :], wdb, pwrites=[wdb])
            sl = 0
            for ti, tiles in enumerate(TGS):
                ntl = len(tiles)
                ntok = ntl * 128
                nown = ntok - (128 if ti == 0 else 0)
                def op_a(li, ot):
                    acol = (ot + 1) * 128
                    tloc = 15 + ot + 1
                    c = cnt["t"]
                    cnt["t"] += 1
                    xr, xrb = xr_t[c % 2], xr_b[c % 2]
                    k.dma(SP, xr[:], xa[tloc * 128:(tloc + 1) * 128, :], xrb, writes=[xrb])
                    pA, pAb = nps()
                    pB, pBb = nps()
                    for hh, (pp, ppb) in enumerate(((pA, pAb), (pB, pBb))):
                        for kk in range(8):
                            k.mm(pp[:, :], AT[:, kk, acol:acol + 128], wo[:, kk, hh * 512:(hh + 1) * 512], [ATb, wob], [ppb],
                                 start=(kk == 0), stop=(kk == 7), inc=(kk == 7))
                        k.op(DVE, lambda: nc.vector.tensor_tensor(x1[:, li, hh * 512:(hh + 1) * 512], pp[:, :], xr[:, hh * 512:(hh + 1) * 512], ALU.add),
                             reads=[ppb, xrb], writes=[x1b[li]] if hh == 0 else [], pwrites=[] if hh == 0 else [x1b[li]])
                    si = cnt["s"] % 4
                    cnt["s"] += 1
                    ssq, rstd = st2_t[si]
                    stb = st2_b[si]
                    xn, xnb = xn2_t[c % 2], xn2_b[c % 2]
                    rms_stats(x1[:, li, :], x1b[li], junk2, junk2b, ssq, rstd, stb)
                    k.op(DVE, lambda: nc.vector.tensor_scalar(xn[:], x1[:, li, :], rstd[:, 0:1], None, ALU.mult),
                         reads=[x1b[li], stb], writes=[xnb])
                    return xn, xnb

                def op_b(li, xn, xnb):
                    ptA, pbA = npst()
                    ptB, pbB = npst()
                    for cc_ in range(8):
                        pt, pb = (ptA, pbA) if cc_ % 2 == 0 else (ptB, pbB)
                        k.mm(pt[:, (cc_ // 2) * 128:(cc_ // 2 + 1) * 128], xn[:, cc_ * 128:(cc_ + 1) * 128], ident[:], [xnb, cb], [pb],
                             inc=(cc_ >= 6), transpose=True)
                    for cc_ in range(8):
                        o = h2T[:, cc_, li * 128:(li + 1) * 128]
                        if cc_ % 2 == 0:
                            i_ = ptA[:, (cc_ // 2) * 128:(cc_ // 2 + 1) * 128]
                            k.op(ACT, lambda: nc.scalar.activation(o, i_, AF.Identity, bias=SH2[:, cc_:cc_ + 1], scale=A2[:, cc_:cc_ + 1]),
                                 reads=[pbA, modb], pwrites=[h2b])
                        else:
                            i_ = ptB[:, (cc_ // 2) * 128:(cc_ // 2 + 1) * 128]
                            k.op(DVE, lambda: nc.vector.tensor_scalar(o, i_, A2[:, cc_:cc_ + 1], SH2[:, cc_:cc_ + 1], ALU.mult, ALU.add),
                                 reads=[pbB, modb], pwrites=[h2b])

                pendo = {0: op_a(0, tiles[0])}
                for li, ot in enumerate(tiles):
                    if li + 1 < ntl:
                        pendo[li + 1] = op_a(li + 1, tiles[li + 1])
                    op_b(li, *pendo.pop(li))
                off = 128 if ti == 0 else 0
                deferred = []
                for m in range(22):
                    for part in range(2):
                        mp = part * 22 + m
                        wu, wub = wu_t[sl % NWU], wu_b[sl % NWU]
                        issue_wu(sl + PRE)
                        sl += 1
                        U, Ub = U_t[cnt["u"] % 2], U_b[cnt["u"] % 2]
                        cnt["u"] += 1
                        if part == 0:
                            yv, yvb = ya_t[m % 2], ya_b[m % 2]
                        else:
                            yv, yvb = yg_t[m % 2], yg_b[m % 2]
                        first = True
                        for c0 in range(0, ntok, 512):
                            n_ = min(512, ntok - c0)
                            pt, pb = nps()
                            for kk in range(8):
                                k.mm(pt[:, 0:n_], wu[:, kk, :], h2T[:, kk, c0:c0 + n_], [wub, h2b], [pb],
                                     start=(kk == 0), stop=(kk == 7), inc=(kk == 7))
                            k.op(ACT, lambda: nc.scalar.copy(U[:, 2 + c0:2 + c0 + n_], pt[:, 0:n_]), reads=[pb],
                                 writes=[Ub] if first else [], pwrites=[] if first else [Ub])
                            lo = max(c0, off)
                            hi = c0 + n_
                            if hi > lo:
                                k.op(ACT, lambda: nc.scalar.activation(yv[:, lo - off:hi - off], pt[:, lo - c0:hi - c0], AF.Identity,
                                                                       bias=cbb_s[:, mp:mp + 1], scale=cw_s[:, mp, 2:3]),
                                     reads=[pb, cb], writes=[yvb] if first else [], pwrites=[] if first else [yvb])
                                first = False
                        if ti == 0:
                            k.op(ACT, lambda: nc.scalar.activation(U[:, 128:130], U[:, 128:130], AF.Copy, scale=hf[:, 0:1]),
                                 reads=[cb], writes=[Ub])
                        else:
                            k.op(ACT, lambda: nc.scalar.copy(U[:, 0:2], carry[:, mp, :]), reads=[carb], writes=[Ub])
                        k.op(ACT, lambda: nc.scalar.copy(carry[:, mp, :], U[:, ntok:ntok + 2]), reads=[Ub], writes=[carb])
                        b0 = 2 + off
                        k.op(DVE, lambda: nc.vector.scalar_tensor_tensor(yv[:, 0:nown], U[:, b0 - 1:b0 - 1 + nown], cw_s[:, mp, 1:2], yv[:, 0:nown], ALU.mult, ALU.add),
                             reads=[Ub, cb], writes=[yvb])
                        k.op(DVE, lambda: nc.vector.scalar_tensor_tensor(yv[:, 0:nown], U[:, b0 - 2:b0 - 2 + nown], cw_s[:, mp, 0:1], yv[:, 0:nown], ALU.mult, ALU.add),
                             reads=[Ub, cb], writes=[yvb])
                        if deferred:
                            deferred.pop()()
                        if part == 1:
                            def tail(m=m, yv=yv, yvb=yvb, nown=nown):
                                sg, sgb = sg_t[m % 2], sg_b[m % 2]
                                k.op(ACT, lambda: nc.scalar.activation(sg[:, 0:nown], yv[:, 0:nown], AF.Silu), reads=[yvb], writes=[sgb])
                                k.op(DVE, lambda: nc.vector.tensor_tensor(hT[:, m, 0:nown], sg[:, 0:nown], ya_t[m % 2][:, 0:nown], ALU.mult),
                                     reads=[sgb, ya_b[m % 2]], pwrites=[hTb])
                            deferred.append(tail)
                if deferred:
                    deferred.pop()()
                if DBG and ti == 0:
                    k.dma(SP, dbg_x1[:], x1[:], k.buf('dbg4'), reads=x1b)
                    k.dma(SP, dbg_h2[:], h2T[:], k.buf('dbg5'), reads=[h2b])
                    k.dma(SP, dbg_ht[:], hT[:], k.buf('dbg6'), reads=[hTb])
                if ti == 0:
                    for m in range(22):
                        k.op(DVE, lambda: nc.vector.tensor_tensor(wd[:, m, :], wd[:, m, :], GA2b[:, :], ALU.mult),
                             reads=[modb], writes=[wdb] if m == 0 else [], pwrites=[] if m == 0 else [wdb])
                for li, ot in enumerate(tiles):
                    if ot < 0:
                        continue
                    hc = (li - (1 if ti == 0 else 0)) * 128
                    c = cnt["o"]
                    cnt["o"] += 1
                    tm, tmb = tmp_t[c % 2], tmp_b[c % 2]
                    pA, pAb = nps()
                    pB, pBb = nps()
                    for hh, (pp, ppb) in enumerate(((pA, pAb), (pB, pBb))):
                        for m in range(22):
                            k.mm(pp[:, :], hT[:, m, hc:hc + 128], wd[:, m, hh * 512:(hh + 1) * 512], [hTb, wdb], [ppb],
                                 start=(m == 0), stop=(m == 21), inc=(m == 21))
                        k.op(DVE, lambda: nc.vector.tensor_tensor(tm[:, hh * 512:(hh + 1) * 512], pp[:, :], x1[:, li, hh * 512:(hh + 1) * 512], ALU.add),
                             reads=[ppb, x1b[li]], writes=[tmb] if hh == 0 else [], pwrites=[] if hh == 0 else [tmb])
                    si = cnt["s"] % 4
                    cnt["s"] += 1
                    ssq, rstd = st2_t[si]
                    stb = st2_b[si]
                    rms_stats(tm[:], tmb, junk2, junk2b, ssq, rstd, stb)
                    k.op(DVE, lambda: nc.vector.scalar_tensor_tensor(tm[:], tm[:], rstd[:, 0:1], gfb_s[:], ALU.mult, ALU.mult),
                         reads=[stb, cb], writes=[tmb])
                    k.dma(SP, yout[ot * 128:(ot + 1) * 128, :], tm[:], tmb, reads=[tmb])
            k.barrier()
    except StopBuild:
        pass
    return nc


_NC = None


def _bf(a):
    return np.ascontiguousarray(a.astype(ml_dtypes.bfloat16))


def kernel(x, c, w_ada, b_ada, g_attn, w_in, b_f, sinks, w_out, g_mlp, w_up, conv_w, conv_b, w_down, g_final):
    global _NC
    f = lambda a: np.ascontiguousarray(np.asarray(a, dtype=np.float32))
    x, c, w_ada, b_ada, g_attn, w_in, b_f, sinks, w_out, g_mlp, w_up, conv_w, conv_b, w_down, g_final = map(
        f, (x, c, w_ada, b_ada, g_attn, w_in, b_f, sinks, w_out, g_mlp, w_up, conv_w, conv_b, w_down, g_final))
    if _NC is None:
        _NC = build_nc()
    nc = _NC
    kk = np.arange(128)[:, None]
    qq = np.arange(128)[None, :]
    cmask = np.where(kk <= qq, 0.0, NEGM).astype(np.float32)
    swab = np.zeros((128, 8, 2, 128), np.float32)
    for h in range(8):
        slope = 2.0 ** (-(h + 1))
        swab[:, h, 1, :] = np.where(kk <= qq, -slope * (qq - kk), NEGM)
        swab[:, h, 0, :] = np.where(kk > qq, -slope * (128 + qq - kk), NEGM)
    ident = np.eye(128, dtype=np.float32)
    tmaj = lambda v: f(v.reshape(8, 128).T)
    common = {
        "w_ada": w_ada, "b_ada": f(b_ada.reshape(1, -1)), "gat": tmaj(g_attn), "gmt": tmaj(g_mlp),
        "gfb": f(np.broadcast_to(g_final[None, :], (128, D))), "w_in": w_in, "bfc": f(b_f.reshape(8, 1)),
        "sinkb": f(np.broadcast_to(sinks[None, :], (128, 8))), "w_out": w_out, "w_up": w_up,
        "cwt": f(conv_w.T.reshape(44, 128, 3).transpose(1, 0, 2)), "cbt": f(conv_b.reshape(44, 128).T),
        "w_down": w_down, "cmask": _bf(cmask), "swab": _bf(swab), "ident": _bf(ident),
    }
    in_maps = []
    for core in range(8):
        b, p = core // 2, core % 2
        xa = x[b] if p == 1 else np.concatenate([x[b, :NOWN], x[b, :NOWN]], axis=0)
        m = dict(common)
        m["xa"] = f(xa)
        m["ct"] = tmaj(c[b])
        m["pm"] = np.full((128, 1), 0.0 if p == 1 else NEGM, np.float32)
        m["hf"] = np.full((128, 1), float(p), np.float32)
        in_maps.append(m)
    res = run_bass_kernel_spmd(nc, in_maps, core_ids=list(range(8)))
    if os.environ.get("MKDBG"):
        global _DBG
        _DBG = res.results
    out = np.empty((4, S, D), np.float32)
    for core in range(8):
        b, p = core // 2, core % 2
        out[b, p * NOWN:(p + 1) * NOWN] = res.results[core]["y"]
    return out
```
